# Optimizing a Trainium2 kernel written in Bass

```python
import jax, jax.numpy as jnp
from jax import lax
import numpy as np

D_MODEL = 1024
BATCH = 16
SEQ = 2048
DEPTH = 2

D_MIX = 2 * D_MODEL
D_ML = D_MIX // 2
N_ML_HEADS = 4
ML_HEAD_DIM = D_ML // N_ML_HEADS
QK_CONV = 4
ML_CHUNK = 128
D_CV = D_MIX - D_ML
CV_KERNEL = 31
N_IN = 2 * D_ML + D_ML + D_ML + 2 * N_ML_HEADS + 2 * D_CV
D_FF = ((8 * D_MODEL // 3 + 127) // 128) * 128
FFN_KERNEL = 3
N_MOD = 6
EPS = 1e-6

kernel_name = "hymba_style_mlstm_conformer_convffn_block"


def rms_norm(x, g):
    xf = x.astype(jnp.float32)
    y = xf * lax.rsqrt(jnp.mean(xf * xf, axis=-1, keepdims=True) + EPS)
    return (y * g.astype(jnp.float32)).astype(x.dtype)


def layer_norm(x, g, b):
    xf = x.astype(jnp.float32)
    mu = jnp.mean(xf, axis=-1, keepdims=True)
    var = jnp.mean(jnp.square(xf - mu), axis=-1, keepdims=True)
    y = (xf - mu) * lax.rsqrt(var + EPS)
    return (y * g.astype(jnp.float32) + b.astype(jnp.float32)).astype(x.dtype)


def causal_dwconv(x, w, b):
    K, C = w.shape
    y = lax.conv_general_dilated(
        x, w[:, None, :].astype(x.dtype), window_strides=(1,), padding=[(K - 1, 0)],
        dimension_numbers=("NWC", "WIO", "NWC"), feature_group_count=C)
    return y + b


def mlstm_chunkwise(q, k, v, i_pre, f_pre):
    Bsz, S, H, Dh = q.shape
    L = ML_CHUNK
    nc = S // L

    def chunks(t):
        t = t.astype(jnp.float32).reshape((Bsz, nc, L, H) + t.shape[3:])
        return jnp.moveaxis(t, (1, 3), (0, 2))

    qc = chunks(q)
    kc = chunks(k) * (Dh ** -0.5)
    vc = chunks(v)
    ic = chunks(i_pre)
    lfc = jax.nn.log_sigmoid(chunks(f_pre))
    causal = jnp.tril(jnp.ones((L, L), dtype=bool))

    def step(carry, inp):
        C, n, m = carry
        q_, k_, v_, i_, lf_ = inp
        b = jnp.cumsum(lf_, axis=-1)
        dmat = jnp.where(causal, b[..., :, None] - b[..., None, :] + i_[..., None, :], -jnp.inf)
        m_t = jnp.maximum(b + m[..., None], jnp.max(dmat, axis=-1))
        w_inter = jnp.exp(b + m[..., None] - m_t)
        scores = jnp.einsum("bhtd,bhsd->bhts", q_, k_) * jnp.exp(dmat - m_t[..., None])
        num = (w_inter[..., None] * jnp.einsum("bhtd,bhde->bhte", q_, C)
               + jnp.einsum("bhts,bhse->bhte", scores, v_))
        den = w_inter * jnp.einsum("bhtd,bhd->bht", q_, n) + jnp.sum(scores, axis=-1)
        h = num / jnp.maximum(jnp.abs(den), jnp.exp(-m_t))[..., None]
        b_last = b[..., -1]
        g = b_last[..., None] - b + i_
        m_new = jnp.maximum(b_last + m, jnp.max(g, axis=-1))
        decay = jnp.exp(b_last + m - m_new)
        kw = k_ * jnp.exp(g - m_new[..., None])[..., None]
        C = decay[..., None, None] * C + jnp.einsum("bhsd,bhse->bhde", kw, v_)
        n = decay[..., None] * n + jnp.sum(kw, axis=2)
        return (C, n, m_new), h

    init = (jnp.zeros((Bsz, H, Dh, Dh), jnp.float32),
            jnp.zeros((Bsz, H, Dh), jnp.float32),
            jnp.zeros((Bsz, H), jnp.float32))
    _, h = lax.scan(step, init, (qc, kc, vc, ic, lfc))
    return jnp.moveaxis(h, (0, 2), (1, 3)).reshape(Bsz, S, H, Dh)


def setup_inputs(seed: int = 0) -> dict:
    key = jax.random.key(seed)
    ks = jax.random.split(key, 24)

    def nrm(k, shape, scale):
        return jax.random.normal(k, shape, jnp.float32) * scale

    def gain(k, shape):
        return 1.0 + nrm(k, shape, 0.05)

    Ly = DEPTH
    fbias = jnp.linspace(3.0, 6.0, N_ML_HEADS, dtype=jnp.float32)[None, :] + nrm(ks[9], (Ly, N_ML_HEADS), 0.1)
    return {
        "x": nrm(ks[0], (BATCH, SEQ, D_MODEL), 1.0),
        "c": nrm(ks[1], (BATCH, D_MODEL), 1.0),
        "ada_w": nrm(ks[2], (Ly, D_MODEL, N_MOD * D_MODEL), 0.5 * D_MODEL ** -0.5),
        "ada_b": nrm(ks[3], (Ly, N_MOD * D_MODEL), 0.02),
        "mix_pre_g": gain(ks[4], (Ly, D_MODEL)),
        "mix_post_g": gain(ks[5], (Ly, D_MODEL)),
        "w_in": nrm(ks[6], (Ly, D_MODEL, N_IN), D_MODEL ** -0.5),
        "qk_conv_w": nrm(ks[7], (Ly, QK_CONV, 2 * D_ML), QK_CONV ** -0.5),
        "qk_conv_b": nrm(ks[8], (Ly, 2 * D_ML), 0.02),
        "igate_b": nrm(ks[10], (Ly, N_ML_HEADS), 0.1),
        "fgate_b": fbias,
        "ml_norm_g": gain(ks[11], (Ly, D_ML)),
        "cv_dw_w": nrm(ks[12], (Ly, CV_KERNEL, D_CV), CV_KERNEL ** -0.5),
        "cv_dw_b": nrm(ks[13], (Ly, D_CV), 0.02),
        "cv_ln_g": gain(ks[14], (Ly, D_CV)),
        "cv_ln_b": nrm(ks[15], (Ly, D_CV), 0.02),
        "w_out": nrm(ks[16], (Ly, D_MIX, D_MODEL), D_MIX ** -0.5),
        "ffn_pre_g": gain(ks[17], (Ly, D_MODEL)),
        "ffn_post_g": gain(ks[18], (Ly, D_MODEL)),
        "ffn_up": nrm(ks[19], (Ly, D_MODEL, 2 * D_FF), D_MODEL ** -0.5),
        "ffn_conv_w": nrm(ks[20], (Ly, FFN_KERNEL, 2 * D_FF), FFN_KERNEL ** -0.5),
        "ffn_conv_b": nrm(ks[21], (Ly, 2 * D_FF), 0.02),
        "ffn_down": nrm(ks[22], (Ly, D_FF, D_MODEL), D_FF ** -0.5),
    }


def reference(x, c, ada_w, ada_b, mix_pre_g, mix_post_g, w_in, qk_conv_w, qk_conv_b,
              igate_b, fgate_b, ml_norm_g, cv_dw_w, cv_dw_b, cv_ln_g, cv_ln_b, w_out,
              ffn_pre_g, ffn_post_g, ffn_up, ffn_conv_w, ffn_conv_b, ffn_down):
    Bsz, S, _ = x.shape
    H, Dh = N_ML_HEADS, ML_HEAD_DIM
    cond = jax.nn.silu(c)
    for l in range(DEPTH):
        mod = cond @ ada_w[l] + ada_b[l]
        sh1, sc1, g1, sh2, sc2, g2 = jnp.split(mod[:, None, :], N_MOD, axis=-1)

        u = rms_norm(x, mix_pre_g[l]) * (1.0 + sc1) + sh1
        proj = u @ w_in[l]
        qk_raw, v, o_pre, gates, glu = jnp.split(
            proj, [2 * D_ML, 3 * D_ML, 4 * D_ML, 4 * D_ML + 2 * H], axis=-1)

        qk = jax.nn.silu(causal_dwconv(qk_raw, qk_conv_w[l], qk_conv_b[l]))
        q, k = jnp.split(qk, 2, axis=-1)
        heads = lambda t: t.reshape(Bsz, S, H, Dh)
        h = mlstm_chunkwise(heads(q), heads(k), heads(v),
                            gates[..., :H] + igate_b[l], gates[..., H:] + fgate_b[l])
        h = rms_norm(h, ml_norm_g[l].reshape(H, Dh)).reshape(Bsz, S, D_ML)
        h = (h * jax.nn.sigmoid(o_pre.astype(jnp.float32))).astype(x.dtype)

        a, gt = jnp.split(glu, 2, axis=-1)
        y = causal_dwconv(a * jax.nn.sigmoid(gt), cv_dw_w[l], cv_dw_b[l])
        y = jax.nn.silu(layer_norm(y, cv_ln_g[l], cv_ln_b[l]))

        mix = jnp.concatenate([h, y], axis=-1) @ w_out[l]
        x = x + g1 * rms_norm(mix, mix_post_g[l])

        u = rms_norm(x, ffn_pre_g[l]) * (1.0 + sc2) + sh2
        up = causal_dwconv(u @ ffn_up[l], ffn_conv_w[l], ffn_conv_b[l])
        a, gt = jnp.split(up, 2, axis=-1)
        f = (jax.nn.silu(gt) * a) @ ffn_down[l]
        x = x + g2 * rms_norm(f, ffn_post_g[l])
    return x
```

```python
import numpy as np
from contextlib import ExitStack
import concourse.bass as bass
import concourse.mybir as mybir
from concourse.bass_utils import run_bass_kernel_spmd

F32 = mybir.dt.float32
BF16 = mybir.dt.bfloat16
AF = mybir.ActivationFunctionType
ALU = mybir.AluOpType

D = 1024
KC = 8
T = 512
TT = 4
H = 4
DH = 256
SEQ = 2048
N_IN = 6152
DFF = 2816
NFC = 22
EPS = 1e-6

_off = {}
_n = 0
for _name, _w in (("pre1", 8), ("qkw", 64), ("qkb", 16), ("cvw", 248), ("cvb", 8), ("lng", 8),
                  ("lnb", 8), ("pre2", 8), ("fw", 132), ("fb", 44), ("mlg", 8), ("adab", 32)):
    _off[_name] = _n
    _n += _w
NPV = _n
BC_POST1, BC_POST2, BC_ADAG1, BC_ADAG2, BC_GB = 0, 1024, 2048, 3072, 4096
NBC = 4096 + 32


class Tk:
    __slots__ = ("name", "w", "r", "dsem", "dcount", "excl")

    def __init__(self, name, excl=False):
        self.name = name
        self.excl = excl
        self.w = None
        self.r = {}
        self.dsem = None
        self.dcount = 0


class Eng:
    def __init__(self, fw, name, handle):
        self.fw = fw
        self.name = name
        self.h = handle
        self.sem = None
        self.count = 0
        self.waited = {}
        self.nsem = 0

    def newsem(self):
        self.sem = self.fw.alloc_sem(f"{self.name}_e{self.nsem}")
        self.nsem += 1
        self.count = 0


class FW:
    EPOCH = 12000
    STRICT = True

    def __init__(self, nc, stack):
        self.nc = nc
        self.stack = stack
        self.nsems = 0
        self.engs = {}
        for name, h in (("pe", nc.tensor), ("act", nc.scalar), ("dve", nc.vector),
                        ("pool", nc.gpsimd), ("sp", nc.sync)):
            e = Eng(self, name, h)
            e.newsem()
            self.engs[name] = e
        self.ninst = 0
        self.per = {k: 0 for k in self.engs}

    def alloc_sem(self, name):
        self.nsems += 1
        return self.stack.enter_context(self.nc.semaphore(name))

    def _wait(self, eng, tok, kind):
        sem, val, en = tok
        if en == eng.name and kind != "raw" and (eng.name == "pe" or not self.STRICT):
            return
        if eng.waited.get(sem, 0) >= val:
            return
        eng.h.wait_ge(sem, val)
        eng.waited[sem] = val
        self.ninst += 1

    def _deps(self, eng, reads, writes):
        for t in reads:
            if t.w is not None:
                self._wait(eng, t.w, "raw")
            if t.excl:
                for sem, (val, en) in t.r.items():
                    if en != eng.name:
                        self._wait(eng, (sem, val, en), "raw")
        for t in writes:
            if t.w is not None:
                self._wait(eng, t.w, "waw")
            for sem, (val, en) in t.r.items():
                self._wait(eng, (sem, val, en), "war")

    def op(self, engname, fn, reads=(), writes=()):
        eng = self.engs[engname]
        self._deps(eng, reads, writes)
        inst = fn(eng.h)
        if eng.count >= self.EPOCH:
            eng.newsem()
        inst.then_inc(eng.sem, 1)
        eng.count += 1
        self.ninst += 1
        self.per[engname] += 1
        tok = (eng.sem, eng.count, eng.name)
        for t in reads:
            t.r[eng.sem] = (eng.count, eng.name)
        for t in writes:
            t.w = tok
            t.r = {}
        return inst

    def dma(self, qname, out, in_, reads=(), writes=(), **kw):
        eng = self.engs[qname]
        for t in reads:
            if t.w is not None:
                self._wait(eng, t.w, "raw")
        for t in writes:
            if t.w is not None:
                self._wait(eng, t.w, "raw")
            for sem, (val, en) in t.r.items():
                self._wait(eng, (sem, val, en), "raw")
        owner = (list(writes) + list(reads))[0]
        if owner.dsem is None:
            owner.dsem = self.alloc_sem("d_" + owner.name)
        inst = eng.h.dma_start(out=out, in_=in_, **kw)
        inst.then_inc(owner.dsem, 16)
        owner.dcount += 16
        self.ninst += 1
        tok = (owner.dsem, owner.dcount, "dma")
        for t in reads:
            t.r[owner.dsem] = (owner.dcount, "dma")
        for t in writes:
            t.w = tok
            t.r = {}
        return tok

    def finish(self, toks):
        eng = self.engs["sp"]
        for tok in toks:
            self._wait(eng, tok, "raw")


class _Stop(Exception):
    pass


def build(nseq=2, nblk=4, nlayers=2, dbg=(), stop=99):
    S = nblk * T
    nc = bass.Bass("TRN2", target_bir_lowering=False)
    dr = lambda name, shape, kind="ExternalInput": nc.dram_tensor(name, shape, F32, kind=kind).ap()
    x_d = dr("x", [nseq, S, D])
    c_d = dr("cT", [128, nseq * KC])
    ada_d = dr("ada_w", [nlayers, D, 6 * D])
    win_d = dr("w_in", [nlayers, D, N_IN])
    wout_d = dr("w_out", [nlayers, 2 * D, D])
    fup_d = dr("ffn_up", [nlayers, D, 2 * DFF])
    fdn_d = dr("ffn_down", [nlayers, DFF, D])
    pv_d = dr("pvec", [nlayers, 128, NPV])
    bc_d = dr("bcv", [nlayers, NBC])
    y_d = dr("y", [nseq, S, D], kind="ExternalOutput")
    dbg_d = {name: dr("dbg_" + name, shape, kind="ExternalOutput") for name, shape in dbg}

    st = ExitStack()
    with st:
        sb = lambda name, shape, dt=F32: st.enter_context(nc.sbuf_tensor(name, shape, dt))
        NSLOT = 4
        wsl = sb("wsl", [128, NSLOT, KC, 512], BF16)
        xb = sb("xb", [128, TT, D])
        xn = sb("xn", [128, 2, D], BF16)
        u = sb("u", [128, KC, T], BF16)
        R1 = sb("R1", [128, 24, T], BF16)
        vp = sb("vp", [128, TT, H, DH + 2], BF16)
        go = sb("go", [128, TT, D], BF16)
        NCIN = 2
        CINW = 544
        cin = sb("cin", [128, NCIN, 4, CINW], BF16)
        NDG = 2
        DGT = 16
        dg = sb("dg", [128, NDG, DGT, 128], BF16)
        tmpf = sb("tmpf", [128, 4, T])
        stat = sb("stat", [128, 2, T])
        my = sb("my", [128, KC, T], BF16)
        hfin = sb("hfin", [128, 2, D], BF16)
        Tm = sb("Tm", [128, nlayers, H, 2 * DH])
        Cb = sb("Cb", [128, nlayers, H, 2 * DH], BF16)
        nm = sb("nm", [128, nlayers, H, 2])
        nbw = sb("nbw", [128, nlayers, H, 2, 2], BF16)
        elp = sb("elp", [128, nlayers, H])
        hq = sb("hq", [128, nlayers, 16, 3], BF16)
        hg = sb("hg", [128, nlayers, 8, 30], BF16)
        hf = sb("hf", [128, nlayers, 2 * NFC, 2], BF16)
        gp = sb("gp", [128, nlayers, 2, D])
        pv = sb("pv", [128, nlayers, NPV])
        gb = sb("gb", [128, nlayers, 32])
        modv = sb("modv", [128, nlayers, 4, KC])
        junk = sb("junk", [128, D], BF16)
        ident = sb("ident", [128, 128], BF16)
        identf = sb("identf", [128, 128])
        mask = sb("mask", [128, 128])
        onesf = sb("onesf", [128, 128])
        onesln = sb("onesln", [128, 128])
        cT = sb("cTs", [128, nseq * KC])
        condb = sb("condb", [128, KC, 2], BF16)
        condbc = sb("condbc", [128, KC, 128], BF16)
        sm = sb("sm", [128, 512])
        P = [st.enter_context(nc.psum_tensor(f"P{i}", [128, 512], F32)) for i in range(8)]
        PB = [p[:].bitcast(BF16) for p in P]

        fw = FW(nc, st)
        st.enter_context(nc.Block())

        tk = lambda n: Tk(n)
        t_wsl = [tk(f"wsl{i}") for i in range(NSLOT)]
        t_xb = [tk(f"xb{i}") for i in range(TT)]
        t_xn = [tk(f"xn{i}") for i in range(2)]
        t_u = [tk(f"u{i}") for i in range(KC)]
        t_R1 = [tk(f"R1_{i}") for i in range(24)]
        t_vp = [tk(f"vp{i}") for i in range(TT)]
        t_go = [tk(f"go{i}") for i in range(TT)]
        t_cin = [tk(f"cin{i}") for i in range(NCIN)]
        t_dg = [tk(f"dg{i}") for i in range(NDG)]
        t_tmpf = [tk(f"tmpf{i}") for i in range(4)]
        t_stat = tk("stat")
        t_my = [tk(f"my{i}") for i in range(KC)]
        t_hfin = [tk(f"hfin{i}") for i in range(2)]
        t_Tm = [[tk(f"Tm{l}_{h}") for h in range(H)] for l in range(nlayers)]
        t_Cb = [[tk(f"Cb{l}_{h}") for h in range(H)] for l in range(nlayers)]
        t_nm = [tk(f"nm{l}") for l in range(nlayers)]
        t_nb = [tk(f"nb{l}") for l in range(nlayers)]
        t_elp = [tk(f"elp{l}") for l in range(nlayers)]
        t_hq = [tk(f"hq{l}") for l in range(nlayers)]
        t_hg = [tk(f"hg{l}") for l in range(nlayers)]
        t_hf = [tk(f"hf{l}") for l in range(nlayers)]
        t_gp = [[tk(f"gp{l}_{j}") for j in range(2)] for l in range(nlayers)]
        t_pv = tk("pv")
        t_gb = tk("gb")
        t_modv = tk("modv")
        t_junk = tk("junk")
        t_const = tk("const")
        t_cT = tk("cT")
        t_cond = tk("cond")
        t_sm = tk("sm")
        t_P = [Tk(f"P{i}", excl=True) for i in range(8)]
        out_toks = []

        E = fw.op

        def mm(out, lhsT, rhs, start, stop, reads, writes):
            return E("pe", lambda e: e.matmul(out, lhsT=lhsT, rhs=rhs, start=start, stop=stop),
                     reads, writes)

        def dump(name, src_ap, reads):
            if name in dbg_d:
                fw.dma("sp", dbg_d[name], src_ap, reads=reads)

        E("dve", lambda e: e.memset(identf[:], 0.0), writes=[t_const])
        E("pool", lambda e: e.affine_select(out=identf[:], in_=identf[:], pattern=[[-1, 128]],
                                            compare_op=ALU.not_equal, fill=1.0, base=0,
                                            channel_multiplier=1), reads=[t_const], writes=[t_const])
        E("dve", lambda e: e.tensor_copy(out=ident[:], in_=identf[:]), reads=[t_const], writes=[t_const])
        E("dve", lambda e: e.memset(onesf[:], 1.0), writes=[t_const])
        E("dve", lambda e: e.memset(onesln[:], 1.0 / D), writes=[t_const])
        E("dve", lambda e: e.memset(mask[:], 1.0), writes=[t_const])
        E("pool", lambda e: e.affine_select(out=mask[:], in_=mask[:], pattern=[[1, 128]],
                                            compare_op=ALU.is_ge, fill=0.0, base=0,
                                            channel_multiplier=-1), reads=[t_const], writes=[t_const])
        E("dve", lambda e: e.memset(junk[:], 0.0), writes=[t_junk])
        for l in range(nlayers):
            fw.dma("sp", pv[:, l, :], pv_d[l], writes=[t_pv])
            fw.dma("sp", gb[:, l, :], bc_d[l, BC_GB:BC_GB + 32].partition_broadcast(128), writes=[t_gb])
        fw.dma("sp", cT[:], c_d[:, :], writes=[t_cT])

        slot_ctr = [0]

        def load_slab(src2d, nkc, ncols):
            i = slot_ctr[0] % NSLOT
            slot_ctr[0] += 1
            fw.dma("pool", wsl[:, i, 0:nkc, 0:ncols],
                   src2d.rearrange("(kc p) n -> p kc n", p=128), writes=[t_wsl[i]])
            return i

        class Stream:
            def __init__(self):
                self.items = []
                self.issued = 0
                self.taken = 0
                self.released = 0
                self.slots = []

            def add(self, src2d, nkc, ncols):
                self.items.append((src2d, nkc, ncols))

            def pump(self):
                while self.issued < len(self.items) and self.issued < self.released + NSLOT:
                    self.slots.append(load_slab(*self.items[self.issued]))
                    self.issued += 1

            def take(self):
                self.pump()
                assert self.issued > self.taken
                i = self.slots[self.taken]
                self.taken += 1
                return i

            def release(self, n=1):
                self.released += n
                self.pump()

        stream = Stream()

        def setup_seq(s):
            E("act", lambda e: e.activation(out=condb[:, :, 0], in_=cT[:, s * KC:(s + 1) * KC], func=AF.Silu),
              reads=[t_cT], writes=[t_cond])
            E("act", lambda e: e.activation(out=condb[:, :, 1], in_=cT[:, s * KC:(s + 1) * KC], func=AF.Silu),
              reads=[t_cT], writes=[t_cond])
            E("dve", lambda e: e.tensor_copy(out=condbc[:], in_=condb[:, :, 0:1].broadcast_to([128, KC, 128])),
              reads=[t_cond], writes=[t_cond])
            for l in range(nlayers):
                for vi, c0 in enumerate((0, 1024, 3072, 4096)):
                    for half in range(2):
                        si = stream.take()
                        for j in range(4):
                            cc = half * 4 + j
                            col = (vi * KC + cc) * 2
                            for kc in range(KC):
                                mm(P[4][:, col:col + 2], wsl[:, si, kc, j * 128:(j + 1) * 128],
                                   condb[:, kc, :], kc == 0, kc == KC - 1,
                                   [t_wsl[si], t_cond], [t_P[4]])
                        stream.release()
                pview = P[4][:, 0:64].rearrange("p (v c two) -> p v c two", v=4, two=2)[:, :, :, 0]
                E("dve", lambda e: e.tensor_tensor(
                    out=modv[:, l, :, :], in0=pview,
                    in1=pv[:, l, _off["adab"]:_off["adab"] + 32].rearrange("p (v c) -> p v c", v=4),
                    op=ALU.add), reads=[t_P[4], t_pv], writes=[t_modv])
                for vi, pre in ((1, "pre1"), (3, "pre2")):
                    E("dve", lambda e: e.scalar_tensor_tensor(
                        out=modv[:, l, vi, :], in0=modv[:, l, vi, :], scalar=1.0,
                        in1=pv[:, l, _off[pre]:_off[pre] + 8], op0=ALU.add, op1=ALU.mult),
                      reads=[t_modv, t_pv], writes=[t_modv])
                for gi, (c0, bco, bcp) in enumerate(((2048, BC_ADAG1, BC_POST1), (5120, BC_ADAG2, BC_POST2))):
                    for half in range(2):
                        si = stream.take()
                        pb = P[5 + half]
                        for kc in range(KC):
                            mm(pb[:, :], condbc[:, kc, :], wsl[:, si, kc, :], kc == 0, kc == KC - 1,
                               [t_wsl[si], t_cond], [t_P[5 + half]])
                        stream.release()
                        fw.dma("sp", tmpf[:, half * 2, :],
                               bc_d[l, bco + half * 512: bco + (half + 1) * 512].partition_broadcast(128),
                               writes=[t_tmpf[half * 2]])
                        fw.dma("sp", tmpf[:, half * 2 + 1, :],
                               bc_d[l, bcp + half * 512: bcp + (half + 1) * 512].partition_broadcast(128),
                               writes=[t_tmpf[half * 2 + 1]])
                        E("dve", lambda e: e.tensor_tensor(out=gp[:, l, gi, half * 512:(half + 1) * 512],
                                                           in0=pb[:, :], in1=tmpf[:, half * 2, :], op=ALU.add),
                          reads=[t_P[5 + half], t_tmpf[half * 2]], writes=[t_gp[l][gi]])
                        E("dve", lambda e: e.tensor_tensor(out=gp[:, l, gi, half * 512:(half + 1) * 512],
                                                           in0=gp[:, l, gi, half * 512:(half + 1) * 512],
                                                           in1=tmpf[:, half * 2 + 1, :], op=ALU.mult),
                          reads=[t_gp[l][gi], t_tmpf[half * 2 + 1]], writes=[t_gp[l][gi]])
            for l in range(nlayers):
                E("dve", lambda e: e.memset(Tm[:, l], 0.0), writes=t_Tm[l])
                E("dve", lambda e: e.memset(Cb[:, l], 0.0), writes=t_Cb[l])
                E("dve", lambda e: e.memset(nm[:, l], 0.0), writes=[t_nm[l]])
                E("dve", lambda e: e.memset(nbw[:, l], 0.0), writes=[t_nb[l]])
                E("dve", lambda e: e.memset(elp[:, l], 1.0), writes=[t_elp[l]])
                E("dve", lambda e: e.memset(hq[:, l], 0.0), writes=[t_hq[l]])
                E("dve", lambda e: e.memset(hg[:, l], 0.0), writes=[t_hg[l]])
                E("dve", lambda e: e.memset(hf[:, l], 0.0), writes=[t_hf[l]])

        def queue_setup_weights(s):
            for l in range(nlayers):
                for c0 in (0, 1024, 3072, 4096, 2048, 5120):
                    for half in range(2):
                        stream.add(ada_d[l, :, c0 + half * 512: c0 + (half + 1) * 512], KC, 512)

        def queue_layer_weights(l):
            stream.add(win_d[l, :, 4096:4104], KC, 8)
            for j in range(4):
                stream.add(win_d[l, :, j * 512:(j + 1) * 512], KC, 512)
            for j in range(2):
                stream.add(win_d[l, :, 4104 + j * 512: 4104 + (j + 1) * 512], KC, 512)
                stream.add(win_d[l, :, 5128 + j * 512: 5128 + (j + 1) * 512], KC, 512)
            for j in range(4):
                stream.add(win_d[l, :, 2048 + j * 512: 2048 + (j + 1) * 512], KC, 512)
            for nh in range(2):
                for kg in range(2):
                    stream.add(wout_d[l, kg * 1024:(kg + 1) * 1024, nh * 512:(nh + 1) * 512], KC, 512)
            for j in range(6):
                w_ = 512 if j < 5 else 256
                stream.add(fup_d[l, :, j * 512: j * 512 + w_], KC, w_)
                stream.add(fup_d[l, :, DFF + j * 512: DFF + j * 512 + w_], KC, w_)
            for nh in range(2):
                for kg in range(3):
                    nk = 8 if kg < 2 else 6
                    stream.add(fdn_d[l, kg * 1024: kg * 1024 + nk * 128, nh * 512:(nh + 1) * 512], nk, 512)

        cin_ctr = [0]
        dg_ctr = [0]
        tmp_ctr = [0]

        def next_tmp():
            i = tmp_ctr[0] % 4
            tmp_ctr[0] += 1
            return i

        def prenorm(l, which):
            vs, vg = (0, 1) if which == 0 else (2, 3)
            for tt in range(TT):
                E("act", lambda e: e.activation(out=junk[:], in_=xb[:, tt, :], func=AF.Square,
                                                accum_out=sm[:, tt:tt + 1]),
                  reads=[t_xb[tt]], writes=[t_junk, t_sm])
            chk(1.2)
            E("act", lambda e: e.activation(out=sm[:, 4:8], in_=sm[:, 0:4], func=AF.Sqrt, scale=1.0 / D, bias=EPS),
              reads=[t_sm], writes=[t_sm])
            E("dve", lambda e: e.reciprocal(out=sm[:, 8:12], in_=sm[:, 4:8]), reads=[t_sm], writes=[t_sm])
            chk(1.4)
            for tt in range(TT):
                b = tt % 2
                E("dve", lambda e: e.tensor_scalar(out=xn[:, b, :], in0=xb[:, tt, :], scalar1=sm[:, 8 + tt:9 + tt],
                                                   scalar2=None, op0=ALU.mult),
                  reads=[t_xb[tt], t_sm], writes=[t_xn[b]])
                chk(1.6)
                pi = (tt % 2) * 2
                for kc in range(KC):
                    pq = pi + kc // 4
                    mm(P[pq][:, (kc % 4) * 128:(kc % 4 + 1) * 128], xn[:, b, kc * 128:(kc + 1) * 128], ident[:],
                       True, True, [t_xn[b], t_const], [t_P[pq]])
                chk(1.8)
                for kc in range(KC):
                    pq = pi + kc // 4
                    src = P[pq][:, (kc % 4) * 128:(kc % 4 + 1) * 128]
                    import os as _os
                    _sel = {"dve": True, "act": False}.get(_os.environ.get("EVAC", ""), kc % 2)
                    E("dve" if _sel else "act", (lambda e: e.tensor_scalar(
                        out=u[:, kc, tt * 128:(tt + 1) * 128], in0=src,
                        scalar1=modv[:, l, vg, kc:kc + 1], scalar2=modv[:, l, vs, kc:kc + 1],
                        op0=ALU.mult, op1=ALU.add)) if _sel else (lambda e: e.activation(
                            out=u[:, kc, tt * 128:(tt + 1) * 128], in_=src,
                            func=AF.Identity, scale=modv[:, l, vg, kc:kc + 1], bias=modv[:, l, vs, kc:kc + 1])),
                      reads=[t_P[pq], t_modv], writes=[t_u[kc]])

        def build_diag(l, woff, ntap, chunk, tap0, ntp):
            i = dg_ctr[0] % NDG
            dg_ctr[0] += 1
            c0 = woff + chunk * ntap + tap0
            E("pool", lambda e: e.tensor_tensor(
                out=dg[:, i, 0:ntp, :], in0=identf[:].unsqueeze(1).broadcast_to([128, ntp, 128]),
                in1=pv[:, l, c0:c0 + ntp].unsqueeze(2).broadcast_to([128, ntp, 128]), op=ALU.mult),
              reads=[t_const, t_pv], writes=[t_dg[i]])
            return i

        def conv_pe(l, pout, t_pout, ci, j, woff, ntap, chunk):
            done = 0
            while done < ntap:
                ntp = min(DGT, ntap - done)
                di = build_diag(l, woff, ntap, chunk, done, ntp)
                for k in range(ntp):
                    kk = done + k
                    mm(pout[:, :], dg[:, di, k, :], cin[:, ci, j, kk:kk + T], kk == 0, kk == ntap - 1,
                       [t_dg[di], t_cin[ci]], [t_pout])
                done += ntp

        def post_norm_residual(l, gi, tt, pa, pb, ta, tb):
            E("act", lambda e: e.activation(out=junk[:, 0:512], in_=pa[:, :], func=AF.Square,
                                            accum_out=sm[:, 16:17]), reads=[ta], writes=[t_junk, t_sm])
            E("act", lambda e: e.activation(out=junk[:, 512:1024], in_=pb[:, :], func=AF.Square,
                                            accum_out=sm[:, 17:18]), reads=[tb], writes=[t_junk, t_sm])
            E("dve", lambda e: e.tensor_tensor(out=sm[:, 18:19], in0=sm[:, 16:17], in1=sm[:, 17:18], op=ALU.add),
              reads=[t_sm], writes=[t_sm])
            E("act", lambda e: e.activation(out=sm[:, 19:20], in_=sm[:, 18:19], func=AF.Sqrt, scale=1.0 / D, bias=EPS),
              reads=[t_sm], writes=[t_sm])
            E("dve", lambda e: e.reciprocal(out=sm[:, 20:21], in_=sm[:, 19:20]), reads=[t_sm], writes=[t_sm])
            for half, (pp, tp) in enumerate(((pa, ta), (pb, tb))):
                ti = next_tmp()
                E("dve", lambda e: e.scalar_tensor_tensor(
                    out=tmpf[:, ti, :], in0=pp[:, :], scalar=sm[:, 20:21],
                    in1=gp[:, l, gi, half * 512:(half + 1) * 512], op0=ALU.mult, op1=ALU.mult),
                  reads=[tp, t_sm, t_gp[l][gi]], writes=[t_tmpf[ti]])
                E("dve", lambda e: e.tensor_tensor(out=xb[:, tt, half * 512:(half + 1) * 512],
                                                    in0=xb[:, tt, half * 512:(half + 1) * 512],
                                                    in1=tmpf[:, ti, :], op=ALU.add),
                  reads=[t_xb[tt], t_tmpf[ti]], writes=[t_xb[tt]])

        def chk(ph):
            if ph >= stop:
                raise _Stop()

        def block_layer(l):
            chk(1)
            prenorm(l, 0)
            chk(2)
            si = stream.take()
            for tt in range(TT):
                for kc in range(KC):
                    mm(P[4][:, tt * 8:(tt + 1) * 8], u[:, kc, tt * 128:(tt + 1) * 128], wsl[:, si, kc, 0:8],
                       kc == 0, kc == KC - 1, [t_u[kc], t_wsl[si]], [t_P[4]])
            stream.release()
            G = sm[:, 32:64]
            E("dve", lambda e: e.tensor_tensor(out=G, in0=P[4][:, 0:32], in1=gb[:, l, :], op=ALU.add),
              reads=[t_P[4], t_gb], writes=[t_sm])
            Gv = G.rearrange("p (t g) -> p t g", g=8)
            LF = sm[:, 64:80].rearrange("p (t h) -> p t h", h=4)
            E("act", lambda e: e.activation(out=LF, in_=Gv[:, :, 4:8], func=AF.Exp, scale=-1.0),
              reads=[t_sm], writes=[t_sm])
            E("act", lambda e: e.activation(out=LF, in_=LF, func=AF.Ln, bias=1.0, scale=1.0),
              reads=[t_sm], writes=[t_sm])
            mm(P[5][:, 0:16], mask[:], sm[:, 64:80], True, True, [t_const, t_sm], [t_P[5]])
            mm(P[5][:, 16:32], onesf[:], sm[:, 64:80], True, True, [t_const, t_sm], [t_P[5]])
            A = sm[:, 80:96]
            Ee = sm[:, 96:112]
            EL = sm[:, 112:128]
            E("dve", lambda e: e.tensor_tensor(out=A.rearrange("p (t h) -> p t h", h=4), in0=Gv[:, :, 0:4],
                                               in1=P[5][:, 0:16].rearrange("p (t h) -> p t h", h=4), op=ALU.add),
              reads=[t_sm, t_P[5]], writes=[t_sm])
            E("act", lambda e: e.activation(out=A, in_=A, func=AF.Exp, bias=sm[:, 128:129], scale=1.0),
              reads=[t_sm], writes=[t_sm])
            E("act", lambda e: e.activation(out=sm[:, 96:128], in_=P[5][:, 0:32], func=AF.Exp, scale=-1.0),
              reads=[t_P[5]], writes=[t_sm])
            E("dve", lambda e: e.tensor_copy(out=vp[:, :, :, DH:DH + 2],
                                             in_=A.rearrange("p (t h o) -> p t h o", h=4, o=1).broadcast_to([128, TT, H, 2])),
              reads=[t_sm], writes=t_vp)

            chk(3)
            pend = None
            for sj in range(4):
                si = stream.take()
                ci = cin_ctr[0] % NCIN
                cin_ctr[0] += 1
                E("dve", lambda e: e.tensor_copy(out=cin[:, ci, :, 0:3], in_=hq[:, l, sj * 4:(sj + 1) * 4, :]),
                  reads=[t_hq[l]], writes=[t_cin[ci]])
                for j in range(4):
                    c = sj * 4 + j
                    pa = c % 2
                    for kc in range(KC):
                        mm(P[pa][:, :], wsl[:, si, kc, j * 128:(j + 1) * 128], u[:, kc, :], kc == 0, kc == KC - 1,
                           [t_wsl[si], t_u[kc]], [t_P[pa]])
                    E("act", lambda e: e.activation(out=cin[:, ci, j, 3:3 + T], in_=P[pa][:, :], func=AF.Copy),
                      reads=[t_P[pa]], writes=[t_cin[ci]])
                stream.release()
                E("dve", lambda e: e.tensor_copy(out=hq[:, l, sj * 4:(sj + 1) * 4, :], in_=cin[:, ci, :, T:T + 3]),
                  reads=[t_cin[ci]], writes=[t_hq[l]])
                for j in range(4):
                    c = sj * 4 + j
                    pc = 2 + c % 2
                    conv_pe(l, P[pc], t_P[pc], ci, j, _off["qkw"], 4, c)
                    E("act", lambda e: e.activation(out=R1[:, c, :], in_=P[pc][:, :], func=AF.Silu,
                                                    bias=pv[:, l, _off["qkb"] + c:_off["qkb"] + c + 1], scale=1.0),
                      reads=[t_P[pc], t_pv], writes=[t_R1[c]])

            chk(4)
            for sj in range(2):
                sa = stream.take()
                sg = stream.take()
                for j in range(4):
                    c = sj * 4 + j
                    ci = cin_ctr[0] % NCIN
                    cin_ctr[0] += 1
                    E("dve", lambda e: e.tensor_copy(out=cin[:, ci, 0, 0:30], in_=hg[:, l, c, :]),
                      reads=[t_hg[l]], writes=[t_cin[ci]])
                    for kc in range(KC):
                        mm(P[0][:, :], wsl[:, sa, kc, j * 128:(j + 1) * 128], u[:, kc, :], kc == 0, kc == KC - 1,
                           [t_wsl[sa], t_u[kc]], [t_P[0]])
                    for kc in range(KC):
                        mm(P[1][:, :], wsl[:, sg, kc, j * 128:(j + 1) * 128], u[:, kc, :], kc == 0, kc == KC - 1,
                           [t_wsl[sg], t_u[kc]], [t_P[1]])
                    ti = next_tmp()
                    E("act", lambda e: e.activation(out=tmpf[:, ti, :], in_=P[1][:, :], func=AF.Sigmoid),
                      reads=[t_P[1]], writes=[t_tmpf[ti]])
                    E("dve", lambda e: e.tensor_tensor(out=cin[:, ci, 0, 30:30 + T], in0=P[0][:, :],
                                                       in1=tmpf[:, ti, :], op=ALU.mult),
                      reads=[t_P[0], t_tmpf[ti]], writes=[t_cin[ci]])
                    E("dve", lambda e: e.tensor_copy(out=hg[:, l, c, :], in_=cin[:, ci, 0, T:T + 30]),
                      reads=[t_cin[ci]], writes=[t_hg[l]])
                    pc = 2 + c % 2
                    conv_pe(l, P[pc], t_P[pc], ci, 0, _off["cvw"], 31, c)
                    t1 = next_tmp()
                    E("act", lambda e: e.activation(out=tmpf[:, t1, :], in_=P[pc][:, :], func=AF.Identity,
                                                    bias=pv[:, l, _off["cvb"] + c:_off["cvb"] + c + 1], scale=1.0),
                      reads=[t_P[pc], t_pv], writes=[t_tmpf[t1]])
                    t2 = next_tmp()
                    E("act", lambda e: e.activation(out=tmpf[:, t2, :], in_=tmpf[:, t1, :], func=AF.Square),
                      reads=[t_tmpf[t1]], writes=[t_tmpf[t2]])
                    E("dve", lambda e: e.tensor_copy(out=my[:, c, :], in_=tmpf[:, t1, :]),
                      reads=[t_tmpf[t1]], writes=[t_my[c]])
                    mm(P[6][:, :], onesln[:], tmpf[:, t1, :], c == 0, c == KC - 1, [t_const, t_tmpf[t1]], [t_P[6]])
                    mm(P[7][:, :], onesln[:], tmpf[:, t2, :], c == 0, c == KC - 1, [t_const, t_tmpf[t2]], [t_P[7]])
                stream.release(2)
            pass
            E("act", lambda e: e.activation(out=stat[:, 0, :], in_=P[6][:, :], func=AF.Copy),
              reads=[t_P[6]], writes=[t_stat])
            ti = next_tmp()
            E("act", lambda e: e.activation(out=tmpf[:, ti, :], in_=P[6][:, :], func=AF.Square),
              reads=[t_P[6]], writes=[t_tmpf[ti]])
            E("dve", lambda e: e.tensor_tensor(out=tmpf[:, ti, :], in0=P[7][:, :], in1=tmpf[:, ti, :], op=ALU.subtract),
              reads=[t_P[7], t_tmpf[ti]], writes=[t_tmpf[ti]])
            E("act", lambda e: e.activation(out=tmpf[:, ti, :], in_=tmpf[:, ti, :], func=AF.Sqrt, bias=EPS, scale=1.0),
              reads=[t_tmpf[ti]], writes=[t_tmpf[ti]])
            E("dve", lambda e: e.reciprocal(out=stat[:, 1, :], in_=tmpf[:, ti, :]),
              reads=[t_tmpf[ti]], writes=[t_stat])
            for c in range(KC):
                ti = next_tmp()
                E("dve", lambda e: e.tensor_tensor(out=tmpf[:, ti, :], in0=my[:, c, :], in1=stat[:, 0, :], op=ALU.subtract),
                  reads=[t_my[c], t_stat], writes=[t_tmpf[ti]])
                E("dve", lambda e: e.tensor_tensor(out=tmpf[:, ti, :], in0=tmpf[:, ti, :], in1=stat[:, 1, :], op=ALU.mult),
                  reads=[t_tmpf[ti], t_stat], writes=[t_tmpf[ti]])
                E("act", lambda e: e.activation(out=my[:, c, :], in_=tmpf[:, ti, :], func=AF.Silu,
                                                scale=pv[:, l, _off["lng"] + c:_off["lng"] + c + 1],
                                                bias=pv[:, l, _off["lnb"] + c:_off["lnb"] + c + 1]),
                  reads=[t_tmpf[ti], t_pv], writes=[t_my[c]])

            chk(5)
            for sj in range(4):
                si = stream.take()
                for tt in range(TT):
                    pa = (sj * TT + tt) % 2
                    for kc in range(KC):
                        mm(P[pa][:, :], u[:, kc, tt * 128:(tt + 1) * 128], wsl[:, si, kc, :], kc == 0, kc == KC - 1,
                           [t_u[kc], t_wsl[si]], [t_P[pa]])
                    if sj < 2:
                        for hh in range(2):
                            h = sj * 2 + hh
                            E("act" if hh else "dve", (lambda e: e.activation(
                                out=vp[:, tt, h, 0:DH], in_=P[pa][:, hh * DH:(hh + 1) * DH], func=AF.Identity,
                                scale=sm[:, 80 + tt * 4 + h:81 + tt * 4 + h])) if hh else (lambda e: e.tensor_scalar(
                                    out=vp[:, tt, h, 0:DH], in0=P[pa][:, hh * DH:(hh + 1) * DH],
                                    scalar1=sm[:, 80 + tt * 4 + h:81 + tt * 4 + h], scalar2=None, op0=ALU.mult)),
                              reads=[t_P[pa], t_sm], writes=[t_vp[tt]])
                    else:
                        E("act", lambda e: e.activation(out=go[:, tt, (sj - 2) * 512:(sj - 1) * 512], in_=P[pa][:, :],
                                                        func=AF.Sigmoid), reads=[t_P[pa]], writes=[t_go[tt]])
                stream.release()

            chk(6)
            for tt in range(TT):
                ts = slice(tt * 128, (tt + 1) * 128)
                for c in range(KC):
                    mm(P[c // 4][:, (c % 4) * 128:(c % 4 + 1) * 128], R1[:, 8 + c, ts], ident[:], True, True,
                       [t_R1[8 + c], t_const], [t_P[c // 4]])
                for a_ in range(2):
                    E("act" if a_ else "dve", (lambda e: e.activation(out=R1[:, 16 + 2 * tt + a_, :], in_=P[a_][:, :], func=AF.Copy))
                      if a_ else (lambda e: e.tensor_copy(out=R1[:, 16 + 2 * tt + a_, :], in_=P[a_][:, :])),
                      reads=[t_P[a_]], writes=[t_R1[16 + 2 * tt + a_]])
                ktf = lambda h, dc: R1[:, 16 + 2 * tt + h // 2, (h % 2) * 256 + dc * 128:(h % 2) * 256 + (dc + 1) * 128]
                for h in range(H):
                    idx = tt * 4 + h
                    for dc in range(2):
                        mm(P[2][:, h * 128:(h + 1) * 128], R1[:, 8 + 2 * h + dc, ts], R1[:, 2 * h + dc, ts],
                           dc == 0, dc == 1, [t_R1[8 + 2 * h + dc], t_R1[2 * h + dc]], [t_P[2]])
                    ti = next_tmp()
                    St = tmpf[:, ti, :].bitcast(BF16)[:, 0:128]
                    E("dve", lambda e: e.tensor_tensor(out=St, in0=P[2][:, h * 128:(h + 1) * 128], in1=mask[:], op=ALU.mult),
                      reads=[t_P[2], t_const], writes=[t_tmpf[ti]])
                    pn = P[4 + h // 2]
                    tpn = t_P[4 + h // 2]
                    no = (h % 2) * DH
                    for dc in range(2):
                        mm(pn[:, no:no + DH], R1[:, 2 * h + dc, ts], Cb[:, l, h, dc * DH:(dc + 1) * DH], dc == 0, False,
                           [t_R1[2 * h + dc], t_Cb[l][h]], [tpn])
                    mm(pn[:, no:no + DH], St, vp[:, tt, h, 0:DH], False, True, [t_tmpf[ti], t_vp[tt]], [tpn])
                    for dc in range(2):
                        mm(P[3][:, h * 2:h * 2 + 2], R1[:, 2 * h + dc, ts], nbp(l, h, dc), dc == 0, False, [t_R1[2 * h + dc], t_nb[l]], [t_P[3]])
                    mm(P[3][:, h * 2:h * 2 + 2], St, vp[:, tt, h, DH:DH + 2], False, True, [t_tmpf[ti], t_vp[tt]], [t_P[3]])
                    pu = P[6 + h % 2]
                    tpu = t_P[6 + h % 2]
                    for dc in range(2):
                        mm(pu[:, dc * DH:(dc + 1) * DH], ktf(h, dc), vp[:, tt, h, 0:DH], True, True,
                           [t_R1[16 + 2 * tt + h // 2], t_vp[tt]], [tpu])
                    for dc in range(2):
                        mm(P[3][:, 16 + h * 4 + dc * 2:18 + h * 4 + dc * 2], ktf(h, dc), vp[:, tt, h, DH:DH + 2], True, True,
                           [t_R1[16 + 2 * tt + h // 2], t_vp[tt]], [t_P[3]])
                    elprev = elp[:, l, h:h + 1] if tt == 0 else sm[:, 112 + idx - 4:113 + idx - 4]
                    t_elprev = t_elp[l] if tt == 0 else t_sm
                    E("dve", lambda e: e.scalar_tensor_tensor(out=Tm[:, l, h, :], in0=Tm[:, l, h, :], scalar=elprev,
                                                              in1=pu[:, :], op0=ALU.mult, op1=ALU.add),
                      reads=[t_Tm[l][h], t_elprev, tpu], writes=[t_Tm[l][h]])
                    E("act", lambda e: e.activation(out=Cb[:, l, h, :], in_=Tm[:, l, h, :], func=AF.Identity,
                                                    scale=sm[:, 112 + idx:113 + idx]),
                      reads=[t_Tm[l][h], t_sm], writes=[t_Cb[l][h]])
                    nsl = P[3][:, 16 + h * 4:20 + h * 4].rearrange("p (d two) -> p d two", two=2)[:, :, 0]
                    E("dve", lambda e: e.scalar_tensor_tensor(out=nm[:, l, h, :], in0=nm[:, l, h, :], scalar=elprev,
                                                              in1=nsl, op0=ALU.mult, op1=ALU.add),
                      reads=[t_nm[l], t_elprev, t_P[3]], writes=[t_nm[l]])
                    E("dve", lambda e: e.tensor_scalar(out=nb2(l, h), in0=nm[:, l, h, :].unsqueeze(2).broadcast_to([128, 2, 2]),
                                                       scalar1=sm[:, 112 + idx:113 + idx], scalar2=None, op0=ALU.mult),
                      reads=[t_nm[l], t_sm], writes=[t_nb[l]])
                e4 = sm[:, 96 + tt * 4:100 + tt * 4]
                den = P[3][:, 0:8].rearrange("p (h two) -> p h two", two=2)[:, :, 0]
                W0 = sm[:, 136:140]
                W1 = sm[:, 140:144]
                W2 = sm[:, 144:148]
                E("dve", lambda e: e.tensor_tensor(out=W0, in0=den, in1=e4, op=ALU.mult), reads=[t_P[3], t_sm], writes=[t_sm])
                E("act", lambda e: e.activation(out=W0, in_=W0, func=AF.Abs), reads=[t_sm], writes=[t_sm])
                E("dve", lambda e: e.tensor_scalar_max(out=W0, in0=W0, scalar1=1.0), reads=[t_sm], writes=[t_sm])
                E("dve", lambda e: e.reciprocal(out=W0, in_=W0), reads=[t_sm], writes=[t_sm])
                E("dve", lambda e: e.tensor_tensor(out=W0, in0=W0, in1=e4, op=ALU.mult), reads=[t_sm], writes=[t_sm])
                for h in range(H):
                    pn = P[4 + h // 2]
                    no = (h % 2) * DH
                    E("act", lambda e: e.activation(out=junk[:, 0:DH], in_=pn[:, no:no + DH], func=AF.Square,
                                                    accum_out=sm[:, 148 + h:149 + h]),
                      reads=[t_P[4 + h // 2]], writes=[t_junk, t_sm])
                SS = sm[:, 148:152]
                E("dve", lambda e: e.tensor_tensor(out=W1, in0=W0, in1=W0, op=ALU.mult), reads=[t_sm], writes=[t_sm])
                E("dve", lambda e: e.tensor_tensor(out=W1, in0=W1, in1=SS, op=ALU.mult), reads=[t_sm], writes=[t_sm])
                E("act", lambda e: e.activation(out=W1, in_=W1, func=AF.Sqrt, scale=1.0 / DH, bias=EPS), reads=[t_sm], writes=[t_sm])
                E("dve", lambda e: e.reciprocal(out=W1, in_=W1), reads=[t_sm], writes=[t_sm])
                E("dve", lambda e: e.tensor_tensor(out=W2, in0=W1, in1=W0, op=ALU.mult), reads=[t_sm], writes=[t_sm])
                hb = tt % 2
                for h in range(H):
                    pn = P[4 + h // 2]
                    no = (h % 2) * DH
                    E("dve", lambda e: e.scalar_tensor_tensor(out=hfin[:, hb, h * DH:(h + 1) * DH], in0=pn[:, no:no + DH],
                                                              scalar=sm[:, 144 + h:145 + h], in1=go[:, tt, h * DH:(h + 1) * DH],
                                                              op0=ALU.mult, op1=ALU.mult),
                      reads=[t_P[4 + h // 2], t_sm, t_go[tt]], writes=[t_hfin[hb]])
                for c in range(KC):
                    mm(P[c // 4][:, (c % 4) * 128:(c % 4 + 1) * 128], hfin[:, hb, c * 128:(c + 1) * 128], ident[:], True, True,
                       [t_hfin[hb], t_const], [t_P[c // 4]])
                for c in range(KC):
                    src = P[c // 4][:, (c % 4) * 128:(c % 4 + 1) * 128]
                    E("dve" if c % 2 else "act", (lambda e: e.tensor_scalar(
                        out=u[:, c, ts], in0=src,
                        scalar1=pv[:, l, _off["mlg"] + c:_off["mlg"] + c + 1], scalar2=None, op0=ALU.mult)) if c % 2 else (
                        lambda e: e.activation(out=u[:, c, ts], in_=src, func=AF.Identity,
                                               scale=pv[:, l, _off["mlg"] + c:_off["mlg"] + c + 1])),
                      reads=[t_P[c // 4], t_pv], writes=[t_u[c]])
            E("dve", lambda e: e.tensor_copy(out=elp[:, l, :], in_=sm[:, 124:128]), reads=[t_sm], writes=[t_elp[l]])

            chk(7)
            for nh in range(2):
                for kg in range(2):
                    si = stream.take()
                    for tt in range(TT):
                        pi = nh * 4 + tt
                        for kc in range(KC):
                            src = (u[:, kc, tt * 128:(tt + 1) * 128], t_u[kc]) if kg == 0 else \
                                (my[:, kc, tt * 128:(tt + 1) * 128], t_my[kc])
                            mm(P[pi][:, :], src[0], wsl[:, si, kc, :], kg == 0 and kc == 0, kg == 1 and kc == KC - 1,
                               [src[1], t_wsl[si]], [t_P[pi]])
                    stream.release()
            for tt in range(TT):
                post_norm_residual(l, 0, tt, P[tt], P[4 + tt], t_P[tt], t_P[4 + tt])

            chk(8)
            prenorm(l, 1)
            pend = []
            for sj in range(6):
                sa = stream.take()
                sg = stream.take()
                nch = 4 if sj < 5 else 2
                ca = cin_ctr[0] % NCIN
                cin_ctr[0] += 1
                cg = cin_ctr[0] % NCIN
                cin_ctr[0] += 1
                c0 = sj * 4
                E("dve", lambda e: e.tensor_copy(out=cin[:, ca, 0:nch, 0:2], in_=hf[:, l, c0:c0 + nch, :]),
                  reads=[t_hf[l]], writes=[t_cin[ca]])
                E("dve", lambda e: e.tensor_copy(out=cin[:, cg, 0:nch, 0:2], in_=hf[:, l, NFC + c0:NFC + c0 + nch, :]),
                  reads=[t_hf[l]], writes=[t_cin[cg]])
                for j in range(nch):
                    for (sw, cb, pp) in ((sa, ca, 0), (sg, cg, 1)):
                        for kc in range(KC):
                            mm(P[pp][:, :], wsl[:, sw, kc, j * 128:(j + 1) * 128], u[:, kc, :], kc == 0, kc == KC - 1,
                               [t_wsl[sw], t_u[kc]], [t_P[pp]])
                        E("act" if pp else "dve", (lambda e: e.activation(out=cin[:, cb, j, 2:2 + T], in_=P[pp][:, :], func=AF.Copy))
                          if pp else (lambda e: e.tensor_copy(out=cin[:, cb, j, 2:2 + T], in_=P[pp][:, :])),
                          reads=[t_P[pp]], writes=[t_cin[cb]])
                stream.release(2)
                E("dve", lambda e: e.tensor_copy(out=hf[:, l, c0:c0 + nch, :], in_=cin[:, ca, 0:nch, T:T + 2]),
                  reads=[t_cin[ca]], writes=[t_hf[l]])
                E("dve", lambda e: e.tensor_copy(out=hf[:, l, NFC + c0:NFC + c0 + nch, :], in_=cin[:, cg, 0:nch, T:T + 2]),
                  reads=[t_cin[cg]], writes=[t_hf[l]])
                for j in range(nch):
                    c = c0 + j
                    conv_pe(l, P[2], t_P[2], ca, j, _off["fw"], 3, c)
                    conv_pe(l, P[3], t_P[3], cg, j, _off["fw"], 3, NFC + c)
                    ti = next_tmp()
                    E("act", lambda e: e.activation(out=tmpf[:, ti, :], in_=P[3][:, :], func=AF.Silu,
                                                    bias=pv[:, l, _off["fb"] + NFC + c:_off["fb"] + NFC + c + 1], scale=1.0),
                      reads=[t_P[3], t_pv], writes=[t_tmpf[ti]])
                    E("dve", lambda e: e.scalar_tensor_tensor(out=R1[:, c, :], in0=P[2][:, :],
                                                              scalar=pv[:, l, _off["fb"] + c:_off["fb"] + c + 1],
                                                              in1=tmpf[:, ti, :], op0=ALU.add, op1=ALU.mult),
                      reads=[t_P[2], t_pv, t_tmpf[ti]], writes=[t_R1[c]])
            for nh in range(2):
                for kg in range(3):
                    nk = 8 if kg < 2 else 6
                    si = stream.take()
                    for tt in range(TT):
                        pi = nh * 4 + tt
                        for kc in range(nk):
                            cidx = kg * 8 + kc
                            mm(P[pi][:, :], R1[:, cidx, tt * 128:(tt + 1) * 128], wsl[:, si, kc, :],
                               cidx == 0, cidx == NFC - 1, [t_R1[cidx], t_wsl[si]], [t_P[pi]])
                    stream.release()
            for tt in range(TT):
                post_norm_residual(l, 1, tt, P[tt], P[4 + tt], t_P[tt], t_P[4 + tt])

        def nbp(l, h, dc):
            return nbw[:, l, h, dc, :]

        def nb2(l, h):
            return nbw[:, l, h, :, :]

        E("dve", lambda e: e.memset(sm[:, 128:129], float(-np.log(16.0))), writes=[t_sm])

        for s in range(nseq):
            queue_setup_weights(s)
            for b in range(nblk):
                for l in range(nlayers):
                    queue_layer_weights(l)
        for s in range(nseq):
            setup_seq(s)
            for b in range(nblk):
                for tt in range(TT):
                    fw.dma("sp", xb[:, tt, :], x_d[s, b * T + tt * 128: b * T + (tt + 1) * 128, :], writes=[t_xb[tt]])
                try:
                    for l in range(nlayers):
                        block_layer(l)
                except _Stop:
                    pass
                for tt in range(TT):
                    out_toks.append(fw.dma("sp", y_d[s, b * T + tt * 128: b * T + (tt + 1) * 128, :], xb[:, tt, :],
                                           reads=[t_xb[tt]]))
        fw.finish(out_toks)
        print("instructions:", fw.ninst, fw.per, "sems:", fw.nsems, "slabs:", len(stream.items))
    return nc


def _fm(v):
    return np.ascontiguousarray(v.reshape(-1, 128).T)


def _fmw(w):
    K, C = w.shape
    return np.ascontiguousarray(w.T.reshape(C // 128, 128, K).transpose(1, 0, 2).reshape(128, -1))


def pack_params(inp, layers):
    pvs, bcs = [], []
    for l in layers:
        ab = inp["ada_b"][l]
        cols = [
            _fm(inp["mix_pre_g"][l]), _fmw(inp["qk_conv_w"][l]), _fm(inp["qk_conv_b"][l]),
            _fmw(inp["cv_dw_w"][l]), _fm(inp["cv_dw_b"][l]), _fm(inp["cv_ln_g"][l]), _fm(inp["cv_ln_b"][l]),
            _fm(inp["ffn_pre_g"][l]), _fmw(inp["ffn_conv_w"][l]), _fm(inp["ffn_conv_b"][l]),
            _fm(inp["ml_norm_g"][l]),
            _fm(ab[0:1024]), _fm(ab[1024:2048]), _fm(ab[3072:4096]), _fm(ab[4096:5120]),
        ]
        pvs.append(np.concatenate(cols, axis=1))
        gbias = np.concatenate([inp["igate_b"][l], inp["fgate_b"][l]])
        bcs.append(np.concatenate([inp["mix_post_g"][l], inp["ffn_post_g"][l], ab[2048:3072], ab[5120:6144],
                                   np.tile(gbias, 4)]))
    pvec = np.ascontiguousarray(np.stack(pvs)).astype(np.float32)
    bcv = np.ascontiguousarray(np.stack(bcs)).astype(np.float32)
    assert pvec.shape[2] == NPV and bcv.shape[1] == NBC
    return pvec, bcv


def make_in_maps(inp, ncores, nseq, nblk, layers):
    pvec, bcv = pack_params(inp, layers)
    S = nblk * T
    L = list(layers)
    shared = {
        "ada_w": np.ascontiguousarray(inp["ada_w"][L]), "w_in": np.ascontiguousarray(inp["w_in"][L]),
        "w_out": np.ascontiguousarray(inp["w_out"][L]), "ffn_up": np.ascontiguousarray(inp["ffn_up"][L]),
        "ffn_down": np.ascontiguousarray(inp["ffn_down"][L]), "pvec": pvec, "bcv": bcv,
    }
    maps = []
    for c in range(ncores):
        xs = np.ascontiguousarray(inp["x"][c * nseq:(c + 1) * nseq, :S, :])
        cc = inp["c"][c * nseq:(c + 1) * nseq]
        cT = np.ascontiguousarray(cc.reshape(nseq, KC, 128).transpose(2, 0, 1).reshape(128, nseq * KC))
        m = dict(shared)
        m["x"] = xs
        m["cT"] = cT
        maps.append(m)
    return maps


_NC_CACHE = {}


def kernel(**inputs):
    inp = {k: np.asarray(v, dtype=np.float32) for k, v in inputs.items()}
    ncores, nseq, nblk = 8, 2, 4
    key = (nseq, nblk, 2)
    if key not in _NC_CACHE:
        _NC_CACHE[key] = build(nseq, nblk, 2)
    nc = _NC_CACHE[key]
    maps = make_in_maps(inp, ncores, nseq, nblk, (0, 1))
    res = run_bass_kernel_spmd(nc, maps, core_ids=list(range(ncores)))
    out = np.concatenate([np.asarray(r["y"]).reshape(nseq, SEQ, D) for r in res.results], axis=0)
    return out.astype(np.float32)
```

```python
import numpy as np
from contextlib import ExitStack
import concourse.bass as bass
import concourse.mybir as mybir
from concourse.bass_utils import run_bass_kernel_spmd

F32 = mybir.dt.float32
BF16 = mybir.dt.bfloat16
AF = mybir.ActivationFunctionType
ALU = mybir.AluOpType

D = 1024
KC = 8
T = 512
TT = 4
H = 4
DH = 256
SEQ = 2048
N_IN = 6152
DFF = 2816
NFC = 22
EPS = 1e-6

_off = {}
_n = 0
for _name, _w in (("pre1", 8), ("qkw", 64), ("qkb", 16), ("cvw", 248), ("cvb", 8), ("lng", 8),
                  ("lnb", 8), ("pre2", 8), ("fw", 132), ("fb", 44), ("mlg", 8), ("adab", 32)):
    _off[_name] = _n
    _n += _w
NPV = _n
BC_POST1, BC_POST2, BC_ADAG1, BC_ADAG2, BC_GB = 0, 1024, 2048, 3072, 4096
NBC = 4096 + 32


class Tk:
    __slots__ = ("name", "w", "r", "dsem", "dcount", "excl")

    def __init__(self, name, excl=False):
        self.name = name
        self.excl = excl
        self.w = None
        self.r = {}
        self.dsem = None
        self.dcount = 0


class Eng:
    def __init__(self, fw, name, handle):
        self.fw = fw
        self.name = name
        self.h = handle
        self.sem = None
        self.count = 0
        self.waited = {}
        self.nsem = 0

    def newsem(self):
        self.sem = self.fw.alloc_sem(f"{self.name}_e{self.nsem}")
        self.nsem += 1
        self.count = 0


class FW:
    EPOCH = 12000
    STRICT = True

    def __init__(self, nc, stack):
        self.nc = nc
        self.stack = stack
        self.nsems = 0
        self.engs = {}
        for name, h in (("pe", nc.tensor), ("act", nc.scalar), ("dve", nc.vector),
                        ("pool", nc.gpsimd), ("sp", nc.sync)):
            e = Eng(self, name, h)
            e.newsem()
            self.engs[name] = e
        self.ninst = 0
        self.per = {k: 0 for k in self.engs}

    def alloc_sem(self, name):
        self.nsems += 1
        return self.stack.enter_context(self.nc.semaphore(name))

    def _wait(self, eng, tok, kind):
        sem, val, en = tok
        if en == eng.name and kind != "raw" and (eng.name == "pe" or not self.STRICT):
            return
        if eng.waited.get(sem, 0) >= val:
            return
        eng.h.wait_ge(sem, val)
        eng.waited[sem] = val
        self.ninst += 1

    def _deps(self, eng, reads, writes):
        for t in reads:
            if t.w is not None:
                self._wait(eng, t.w, "raw")
            if t.excl:
                for sem, (val, en) in t.r.items():
                    if en != eng.name:
                        self._wait(eng, (sem, val, en), "raw")
        for t in writes:
            if t.w is not None:
                self._wait(eng, t.w, "waw")
            for sem, (val, en) in t.r.items():
                self._wait(eng, (sem, val, en), "war")

    def op(self, engname, fn, reads=(), writes=()):
        eng = self.engs[engname]
        self._deps(eng, reads, writes)
        inst = fn(eng.h)
        if eng.count >= self.EPOCH:
            eng.newsem()
        inst.then_inc(eng.sem, 1)
        eng.count += 1
        self.ninst += 1
        self.per[engname] += 1
        tok = (eng.sem, eng.count, eng.name)
        for t in reads:
            t.r[eng.sem] = (eng.count, eng.name)
        for t in writes:
            t.w = tok
            t.r = {}
        return inst

    def dma(self, qname, out, in_, reads=(), writes=(), **kw):
        eng = self.engs[qname]
        for t in reads:
            if t.w is not None:
                self._wait(eng, t.w, "raw")
        for t in writes:
            if t.w is not None:
                self._wait(eng, t.w, "raw")
            for sem, (val, en) in t.r.items():
                self._wait(eng, (sem, val, en), "raw")
        owner = (list(writes) + list(reads))[0]
        if owner.dsem is None:
            owner.dsem = self.alloc_sem("d_" + owner.name)
        inst = eng.h.dma_start(out=out, in_=in_, **kw)
        inst.then_inc(owner.dsem, 16)
        owner.dcount += 16
        self.ninst += 1
        tok = (owner.dsem, owner.dcount, "dma")
        for t in reads:
            t.r[owner.dsem] = (owner.dcount, "dma")
        for t in writes:
            t.w = tok
            t.r = {}
        return tok

    def finish(self, toks):
        eng = self.engs["sp"]
        for tok in toks:
            self._wait(eng, tok, "raw")


class _Stop(Exception):
    pass


def build(nseq=2, nblk=4, nlayers=2, dbg=(), stop=99):
    S = nblk * T
    nc = bass.Bass("TRN2", target_bir_lowering=False)
    dr = lambda name, shape, kind="ExternalInput": nc.dram_tensor(name, shape, F32, kind=kind).ap()
    x_d = dr("x", [nseq, S, D])
    c_d = dr("cT", [128, nseq * KC])
    ada_d = dr("ada_w", [nlayers, D, 6 * D])
    win_d = dr("w_in", [nlayers, D, N_IN])
    wout_d = dr("w_out", [nlayers, 2 * D, D])
    fup_d = dr("ffn_up", [nlayers, D, 2 * DFF])
    fdn_d = dr("ffn_down", [nlayers, DFF, D])
    pv_d = dr("pvec", [nlayers, 128, NPV])
    bc_d = dr("bcv", [nlayers, NBC])
    y_d = dr("y", [nseq, S, D], kind="ExternalOutput")
    dbg_d = {name: dr("dbg_" + name, shape, kind="ExternalOutput") for name, shape in dbg}

    st = ExitStack()
    with st:
        sb = lambda name, shape, dt=F32: st.enter_context(nc.sbuf_tensor(name, shape, dt))
        NSLOT = 4
        wsl = sb("wsl", [128, NSLOT, KC, 512], BF16)
        xb = sb("xb", [128, TT, D])
        xn = sb("xn", [128, 2, D], BF16)
        u = sb("u", [128, KC, T], BF16)
        R1 = sb("R1", [128, 24, T], BF16)
        vp = sb("vp", [128, TT, H, DH + 2], BF16)
        go = sb("go", [128, TT, D], BF16)
        NCIN = 2
        CINW = 544
        cin = sb("cin", [128, NCIN, 4, CINW], BF16)
        NDG = 3
        DGT = 16
        dg = sb("dg", [128, NDG, DGT, 128], BF16)
        tmpf = sb("tmpf", [128, 4, T])
        stat = sb("stat", [128, 2, T])
        my = sb("my", [128, KC, T], BF16)
        hfin = sb("hfin", [128, 2, D], BF16)
        Tm = sb("Tm", [128, nlayers, H, 2 * DH])
        Cb = sb("Cb", [128, nlayers, H, 2 * DH], BF16)
        nm = sb("nm", [128, nlayers, H, 2])
        nbw = sb("nbw", [128, nlayers, H, 2, 2], BF16)
        elp = sb("elp", [128, nlayers, H])
        hq = sb("hq", [128, nlayers, 16, 3], BF16)
        hg = sb("hg", [128, nlayers, 8, 30], BF16)
        hf = sb("hf", [128, nlayers, 2 * NFC, 2], BF16)
        gp = sb("gp", [128, nlayers, 2, D])
        pv = sb("pv", [128, nlayers, NPV])
        gb = sb("gb", [128, nlayers, 32])
        modv = sb("modv", [128, nlayers, 4, KC])
        junk = sb("junk", [128, D], BF16)
        ident = sb("ident", [128, 128], BF16)
        identf = sb("identf", [128, 128])
        mask = sb("mask", [128, 128])
        onesf = sb("onesf", [128, 128])
        onesln = sb("onesln", [128, 128], BF16)
        cT = sb("cTs", [128, nseq * KC])
        condb = sb("condb", [128, KC, 2], BF16)
        condbc = sb("condbc", [128, KC, 128], BF16)
        sm = sb("sm", [128, 512])
        sm2 = sb("sm2", [128, 8])
        P = [st.enter_context(nc.psum_tensor(f"P{i}", [128, 512], F32)) for i in range(8)]
        PB = [p[:].bitcast(BF16) for p in P]

        fw = FW(nc, st)
        st.enter_context(nc.Block())

        tk = lambda n: Tk(n)
        t_wsl = [tk(f"wsl{i}") for i in range(NSLOT)]
        t_xb = [tk(f"xb{i}") for i in range(TT)]
        t_xn = [tk(f"xn{i}") for i in range(2)]
        t_u = [tk(f"u{i}") for i in range(KC)]
        t_R1 = [tk(f"R1_{i}") for i in range(24)]
        t_vp = [tk(f"vp{i}") for i in range(TT)]
        t_go = [tk(f"go{i}") for i in range(TT)]
        t_cin = [tk(f"cin{i}") for i in range(NCIN)]
        t_dg = [tk(f"dg{i}") for i in range(NDG)]
        t_tmpf = [tk(f"tmpf{i}") for i in range(4)]
        t_stat = tk("stat")
        t_my = [tk(f"my{i}") for i in range(KC)]
        t_hfin = [tk(f"hfin{i}") for i in range(2)]
        t_Tm = [[tk(f"Tm{l}_{h}") for h in range(H)] for l in range(nlayers)]
        t_Cb = [[tk(f"Cb{l}_{h}") for h in range(H)] for l in range(nlayers)]
        t_nm = [tk(f"nm{l}") for l in range(nlayers)]
        t_nb = [tk(f"nb{l}") for l in range(nlayers)]
        t_elp = [tk(f"elp{l}") for l in range(nlayers)]
        t_hq = [tk(f"hq{l}") for l in range(nlayers)]
        t_hg = [tk(f"hg{l}") for l in range(nlayers)]
        t_hf = [tk(f"hf{l}") for l in range(nlayers)]
        t_gp = [[tk(f"gp{l}_{j}") for j in range(2)] for l in range(nlayers)]
        t_pv = tk("pv")
        t_gb = tk("gb")
        t_modv = tk("modv")
        t_junk = tk("junk")
        t_const = tk("const")
        t_cT = tk("cT")
        t_cond = tk("cond")
        t_sm = tk("sm")
        t_sm2 = tk("sm2")
        t_P = [Tk(f"P{i}", excl=True) for i in range(8)]
        out_toks = []

        E = fw.op

        def mm(out, lhsT, rhs, start, stop, reads, writes):
            return E("pe", lambda e: e.matmul(out, lhsT=lhsT, rhs=rhs, start=start, stop=stop),
                     reads, writes)

        def dump(name, src_ap, reads):
            if name in dbg_d:
                fw.dma("sp", dbg_d[name], src_ap, reads=reads)

        E("dve", lambda e: e.memset(identf[:], 0.0), writes=[t_const])
        E("pool", lambda e: e.affine_select(out=identf[:], in_=identf[:], pattern=[[-1, 128]],
                                            compare_op=ALU.not_equal, fill=1.0, base=0,
                                            channel_multiplier=1), reads=[t_const], writes=[t_const])
        E("dve", lambda e: e.tensor_copy(out=ident[:], in_=identf[:]), reads=[t_const], writes=[t_const])
        E("dve", lambda e: e.memset(onesf[:], 1.0), writes=[t_const])
        E("dve", lambda e: e.memset(onesln[:], 1.0 / D), writes=[t_const])
        E("dve", lambda e: e.memset(mask[:], 1.0), writes=[t_const])
        E("pool", lambda e: e.affine_select(out=mask[:], in_=mask[:], pattern=[[1, 128]],
                                            compare_op=ALU.is_ge, fill=0.0, base=0,
                                            channel_multiplier=-1), reads=[t_const], writes=[t_const])
        E("dve", lambda e: e.memset(junk[:], 0.0), writes=[t_junk])
        for l in range(nlayers):
            fw.dma("sp", pv[:, l, :], pv_d[l], writes=[t_pv])
            fw.dma("sp", gb[:, l, :], bc_d[l, BC_GB:BC_GB + 32].partition_broadcast(128), writes=[t_gb])
        fw.dma("sp", cT[:], c_d[:, :], writes=[t_cT])

        slot_ctr = [0]

        def load_slab(src2d, nkc, ncols):
            i = slot_ctr[0] % NSLOT
            slot_ctr[0] += 1
            fw.dma("pool", wsl[:, i, 0:nkc, 0:ncols],
                   src2d.rearrange("(kc p) n -> p kc n", p=128), writes=[t_wsl[i]])
            return i

        class Stream:
            def __init__(self):
                self.items = []
                self.issued = 0
                self.taken = 0
                self.released = 0
                self.slots = []

            def add(self, src2d, nkc, ncols):
                self.items.append((src2d, nkc, ncols))

            def pump(self):
                while self.issued < len(self.items) and self.issued < self.released + NSLOT:
                    self.slots.append(load_slab(*self.items[self.issued]))
                    self.issued += 1

            def take(self):
                self.pump()
                assert self.issued > self.taken
                i = self.slots[self.taken]
                self.taken += 1
                return i

            def release(self, n=1):
                self.released += n
                self.pump()

        stream = Stream()

        def setup_seq(s):
            E("act", lambda e: e.activation(out=condb[:, :, 0], in_=cT[:, s * KC:(s + 1) * KC], func=AF.Silu),
              reads=[t_cT], writes=[t_cond])
            E("act", lambda e: e.activation(out=condb[:, :, 1], in_=cT[:, s * KC:(s + 1) * KC], func=AF.Silu),
              reads=[t_cT], writes=[t_cond])
            E("dve", lambda e: e.tensor_copy(out=condbc[:], in_=condb[:, :, 0:1].broadcast_to([128, KC, 128])),
              reads=[t_cond], writes=[t_cond])
            for l in range(nlayers):
                for vi, c0 in enumerate((0, 1024, 3072, 4096)):
                    for half in range(2):
                        si = stream.take()
                        for j in range(4):
                            cc = half * 4 + j
                            col = (vi * KC + cc) * 2
                            for kc in range(KC):
                                mm(P[4][:, col:col + 2], wsl[:, si, kc, j * 128:(j + 1) * 128],
                                   condb[:, kc, :], kc == 0, kc == KC - 1,
                                   [t_wsl[si], t_cond], [t_P[4]])
                        stream.release()
                pview = P[4][:, 0:64].rearrange("p (v c two) -> p v c two", v=4, two=2)[:, :, :, 0]
                E("dve", lambda e: e.tensor_tensor(
                    out=modv[:, l, :, :], in0=pview,
                    in1=pv[:, l, _off["adab"]:_off["adab"] + 32].rearrange("p (v c) -> p v c", v=4),
                    op=ALU.add), reads=[t_P[4], t_pv], writes=[t_modv])
                for vi, pre in ((1, "pre1"), (3, "pre2")):
                    E("dve", lambda e: e.scalar_tensor_tensor(
                        out=modv[:, l, vi, :], in0=modv[:, l, vi, :], scalar=1.0,
                        in1=pv[:, l, _off[pre]:_off[pre] + 8], op0=ALU.add, op1=ALU.mult),
                      reads=[t_modv, t_pv], writes=[t_modv])
                for gi, (c0, bco, bcp) in enumerate(((2048, BC_ADAG1, BC_POST1), (5120, BC_ADAG2, BC_POST2))):
                    for half in range(2):
                        si = stream.take()
                        pb = P[5 + half]
                        for kc in range(KC):
                            mm(pb[:, :], condbc[:, kc, :], wsl[:, si, kc, :], kc == 0, kc == KC - 1,
                               [t_wsl[si], t_cond], [t_P[5 + half]])
                        stream.release()
                        fw.dma("sp", tmpf[:, half * 2, :],
                               bc_d[l, bco + half * 512: bco + (half + 1) * 512].partition_broadcast(128),
                               writes=[t_tmpf[half * 2]])
                        fw.dma("sp", tmpf[:, half * 2 + 1, :],
                               bc_d[l, bcp + half * 512: bcp + (half + 1) * 512].partition_broadcast(128),
                               writes=[t_tmpf[half * 2 + 1]])
                        E("dve", lambda e: e.tensor_tensor(out=gp[:, l, gi, half * 512:(half + 1) * 512],
                                                           in0=pb[:, :], in1=tmpf[:, half * 2, :], op=ALU.add),
                          reads=[t_P[5 + half], t_tmpf[half * 2]], writes=[t_gp[l][gi]])
                        E("dve", lambda e: e.tensor_tensor(out=gp[:, l, gi, half * 512:(half + 1) * 512],
                                                           in0=gp[:, l, gi, half * 512:(half + 1) * 512],
                                                           in1=tmpf[:, half * 2 + 1, :], op=ALU.mult),
                          reads=[t_gp[l][gi], t_tmpf[half * 2 + 1]], writes=[t_gp[l][gi]])
            for l in range(nlayers):
                E("dve", lambda e: e.memset(Tm[:, l], 0.0), writes=t_Tm[l])
                E("dve", lambda e: e.memset(Cb[:, l], 0.0), writes=t_Cb[l])
                E("dve", lambda e: e.memset(nm[:, l], 0.0), writes=[t_nm[l]])
                E("dve", lambda e: e.memset(nbw[:, l], 0.0), writes=[t_nb[l]])
                E("dve", lambda e: e.memset(elp[:, l], 1.0), writes=[t_elp[l]])
                E("dve", lambda e: e.memset(hq[:, l], 0.0), writes=[t_hq[l]])
                E("dve", lambda e: e.memset(hg[:, l], 0.0), writes=[t_hg[l]])
                E("dve", lambda e: e.memset(hf[:, l], 0.0), writes=[t_hf[l]])

        def queue_setup_weights(s):
            for l in range(nlayers):
                for c0 in (0, 1024, 3072, 4096, 2048, 5120):
                    for half in range(2):
                        stream.add(ada_d[l, :, c0 + half * 512: c0 + (half + 1) * 512], KC, 512)

        def queue_layer_weights(l):
            stream.add(win_d[l, :, 4096:4104], KC, 8)
            for j in range(4):
                stream.add(win_d[l, :, j * 512:(j + 1) * 512], KC, 512)
            for j in range(2):
                stream.add(win_d[l, :, 4104 + j * 512: 4104 + (j + 1) * 512], KC, 512)
                stream.add(win_d[l, :, 5128 + j * 512: 5128 + (j + 1) * 512], KC, 512)
            for j in range(4):
                stream.add(win_d[l, :, 2048 + j * 512: 2048 + (j + 1) * 512], KC, 512)
            for nh in range(2):
                for kg in range(2):
                    stream.add(wout_d[l, kg * 1024:(kg + 1) * 1024, nh * 512:(nh + 1) * 512], KC, 512)
            for j in range(6):
                w_ = 512 if j < 5 else 256
                stream.add(fup_d[l, :, j * 512: j * 512 + w_], KC, w_)
                stream.add(fup_d[l, :, DFF + j * 512: DFF + j * 512 + w_], KC, w_)
            for nh in range(2):
                for kg in range(3):
                    nk = 8 if kg < 2 else 6
                    stream.add(fdn_d[l, kg * 1024: kg * 1024 + nk * 128, nh * 512:(nh + 1) * 512], nk, 512)

        cin_ctr = [0]
        dg_ctr = [0]
        tmp_ctr = [0]

        def next_tmp():
            i = tmp_ctr[0] % 4
            tmp_ctr[0] += 1
            return i

        def prenorm(l, which):
            vs, vg = (0, 1) if which == 0 else (2, 3)
            for tt in range(TT):
                E("act", lambda e: e.activation(out=junk[:], in_=xb[:, tt, :], func=AF.Square,
                                                accum_out=sm[:, tt:tt + 1]),
                  reads=[t_xb[tt]], writes=[t_junk, t_sm])
            chk(1.2)
            E("act", lambda e: e.activation(out=sm[:, 4:8], in_=sm[:, 0:4], func=AF.Sqrt, scale=1.0 / D, bias=EPS),
              reads=[t_sm], writes=[t_sm])
            E("dve", lambda e: e.reciprocal(out=sm[:, 8:12], in_=sm[:, 4:8]), reads=[t_sm], writes=[t_sm])
            chk(1.4)
            for tt in range(TT):
                b = tt % 2
                E("dve", lambda e: e.tensor_scalar(out=xn[:, b, :], in0=xb[:, tt, :], scalar1=sm[:, 8 + tt:9 + tt],
                                                   scalar2=None, op0=ALU.mult),
                  reads=[t_xb[tt], t_sm], writes=[t_xn[b]])
                chk(1.6)
                pi = (tt % 2) * 2
                for kc in range(KC):
                    pq = pi + kc // 4
                    mm(P[pq][:, (kc % 4) * 128:(kc % 4 + 1) * 128], xn[:, b, kc * 128:(kc + 1) * 128], ident[:],
                       True, True, [t_xn[b], t_const], [t_P[pq]])
                chk(1.8)
                for kc in range(KC):
                    pq = pi + kc // 4
                    src = P[pq][:, (kc % 4) * 128:(kc % 4 + 1) * 128]
                    import os as _os
                    _sel = {"dve": True, "act": False}.get(_os.environ.get("EVAC", ""), kc % 2)
                    E("dve" if _sel else "act", (lambda e: e.tensor_scalar(
                        out=u[:, kc, tt * 128:(tt + 1) * 128], in0=src,
                        scalar1=modv[:, l, vg, kc:kc + 1], scalar2=modv[:, l, vs, kc:kc + 1],
                        op0=ALU.mult, op1=ALU.add)) if _sel else (lambda e: e.activation(
                            out=u[:, kc, tt * 128:(tt + 1) * 128], in_=src,
                            func=AF.Identity, scale=modv[:, l, vg, kc:kc + 1], bias=modv[:, l, vs, kc:kc + 1])),
                      reads=[t_P[pq], t_modv], writes=[t_u[kc]])

        def build_diag(l, woff, ntap, chunk, tap0, ntp):
            i = dg_ctr[0] % NDG
            dg_ctr[0] += 1
            c0 = woff + chunk * ntap + tap0
            E("pool", lambda e: e.tensor_tensor(
                out=dg[:, i, 0:ntp, :], in0=identf[:].unsqueeze(1).broadcast_to([128, ntp, 128]),
                in1=pv[:, l, c0:c0 + ntp].unsqueeze(2).broadcast_to([128, ntp, 128]), op=ALU.mult),
              reads=[t_const, t_pv], writes=[t_dg[i]])
            return i

        def conv_pe(l, pout, t_pout, ci, j, woff, ntap, chunk):
            done = 0
            while done < ntap:
                ntp = min(DGT, ntap - done)
                di = build_diag(l, woff, ntap, chunk, done, ntp)
                for k in range(ntp):
                    kk = done + k
                    mm(pout[:, :], dg[:, di, k, :], cin[:, ci, j, kk:kk + T], kk == 0, kk == ntap - 1,
                       [t_dg[di], t_cin[ci]], [t_pout])
                done += ntp

        def post_sq(pi):
            E("act", lambda e: e.activation(out=junk[:, 0:512], in_=P[pi][:, :], func=AF.Square,
                                            accum_out=sm2[:, pi:pi + 1]), reads=[t_P[pi]], writes=[t_junk, t_sm2])

        def post_norm_all(l, gi):
            E("dve", lambda e: e.tensor_tensor(out=sm[:, 168:172], in0=sm2[:, 0:4], in1=sm2[:, 4:8], op=ALU.add),
              reads=[t_sm2], writes=[t_sm])
            E("act", lambda e: e.activation(out=sm[:, 172:176], in_=sm[:, 168:172], func=AF.Sqrt, scale=1.0 / D, bias=EPS),
              reads=[t_sm], writes=[t_sm])
            E("dve", lambda e: e.reciprocal(out=sm[:, 176:180], in_=sm[:, 172:176]), reads=[t_sm], writes=[t_sm])
            for tt in range(TT):
                for half in range(2):
                    pp, tp = P[half * 4 + tt], t_P[half * 4 + tt]
                    ti = next_tmp()
                    E("dve", lambda e: e.scalar_tensor_tensor(
                        out=tmpf[:, ti, :], in0=pp[:, :], scalar=sm[:, 176 + tt:177 + tt],
                        in1=gp[:, l, gi, half * 512:(half + 1) * 512], op0=ALU.mult, op1=ALU.mult),
                      reads=[tp, t_sm, t_gp[l][gi]], writes=[t_tmpf[ti]])
                    E("dve", lambda e: e.tensor_tensor(out=xb[:, tt, half * 512:(half + 1) * 512],
                                                        in0=xb[:, tt, half * 512:(half + 1) * 512],
                                                        in1=tmpf[:, ti, :], op=ALU.add),
                      reads=[t_xb[tt], t_tmpf[ti]], writes=[t_xb[tt]])

        def chk(ph):
            if ph >= stop:
                raise _Stop()

        def block_layer(l):
            chk(1)
            prenorm(l, 0)
            chk(2)
            si = stream.take()
            for tt in range(TT):
                for kc in range(KC):
                    mm(P[4][:, tt * 8:(tt + 1) * 8], u[:, kc, tt * 128:(tt + 1) * 128], wsl[:, si, kc, 0:8],
                       kc == 0, kc == KC - 1, [t_u[kc], t_wsl[si]], [t_P[4]])
            stream.release()
            G = sm[:, 32:64]
            E("dve", lambda e: e.tensor_tensor(out=G, in0=P[4][:, 0:32], in1=gb[:, l, :], op=ALU.add),
              reads=[t_P[4], t_gb], writes=[t_sm])
            Gv = G.rearrange("p (t g) -> p t g", g=8)
            LF = sm[:, 64:80].rearrange("p (t h) -> p t h", h=4)
            E("act", lambda e: e.activation(out=LF, in_=Gv[:, :, 4:8], func=AF.Exp, scale=-1.0),
              reads=[t_sm], writes=[t_sm])
            E("act", lambda e: e.activation(out=LF, in_=LF, func=AF.Ln, bias=1.0, scale=1.0),
              reads=[t_sm], writes=[t_sm])
            mm(P[5][:, 0:16], mask[:], sm[:, 64:80], True, True, [t_const, t_sm], [t_P[5]])
            mm(P[5][:, 16:32], onesf[:], sm[:, 64:80], True, True, [t_const, t_sm], [t_P[5]])
            A = sm[:, 80:96]
            Ee = sm[:, 96:112]
            EL = sm[:, 112:128]
            E("dve", lambda e: e.tensor_tensor(out=A.rearrange("p (t h) -> p t h", h=4), in0=Gv[:, :, 0:4],
                                               in1=P[5][:, 0:16].rearrange("p (t h) -> p t h", h=4), op=ALU.add),
              reads=[t_sm, t_P[5]], writes=[t_sm])
            E("act", lambda e: e.activation(out=A, in_=A, func=AF.Exp, bias=sm[:, 128:129], scale=1.0),
              reads=[t_sm], writes=[t_sm])
            E("act", lambda e: e.activation(out=sm[:, 96:128], in_=P[5][:, 0:32], func=AF.Exp, scale=-1.0),
              reads=[t_P[5]], writes=[t_sm])
            E("dve", lambda e: e.tensor_copy(out=vp[:, :, :, DH:DH + 2],
                                             in_=A.rearrange("p (t h o) -> p t h o", h=4, o=1).broadcast_to([128, TT, H, 2])),
              reads=[t_sm], writes=t_vp)

            chk(3)
            pend = None
            for sj in range(4):
                si = stream.take()
                ci = cin_ctr[0] % NCIN
                cin_ctr[0] += 1
                E("dve", lambda e: e.tensor_copy(out=cin[:, ci, :, 0:3], in_=hq[:, l, sj * 4:(sj + 1) * 4, :]),
                  reads=[t_hq[l]], writes=[t_cin[ci]])
                for j in range(4):
                    c = sj * 4 + j
                    pa = c % 2
                    for kc in range(KC):
                        mm(P[pa][:, :], wsl[:, si, kc, j * 128:(j + 1) * 128], u[:, kc, :], kc == 0, kc == KC - 1,
                           [t_wsl[si], t_u[kc]], [t_P[pa]])
                    E("act", lambda e: e.activation(out=cin[:, ci, j, 3:3 + T], in_=P[pa][:, :], func=AF.Copy),
                      reads=[t_P[pa]], writes=[t_cin[ci]])
                stream.release()
                E("dve", lambda e: e.tensor_copy(out=hq[:, l, sj * 4:(sj + 1) * 4, :], in_=cin[:, ci, :, T:T + 3]),
                  reads=[t_cin[ci]], writes=[t_hq[l]])
                for j in range(4):
                    c = sj * 4 + j
                    pc = 2 + c % 2
                    conv_pe(l, P[pc], t_P[pc], ci, j, _off["qkw"], 4, c)
                    E("act", lambda e: e.activation(out=R1[:, c, :], in_=P[pc][:, :], func=AF.Silu,
                                                    bias=pv[:, l, _off["qkb"] + c:_off["qkb"] + c + 1], scale=1.0),
                      reads=[t_P[pc], t_pv], writes=[t_R1[c]])

            chk(4)
            cf = {}

            def cf_proj(c):
                j = c % 4
                if j == 0:
                    cf["sa"] = stream.take()
                    cf["sg"] = stream.take()
                sa, sg = cf["sa"], cf["sg"]
                ci = cin_ctr[0] % NCIN
                cin_ctr[0] += 1
                cf[("ci", c)] = ci
                pa_, pg_ = (0, 1) if c % 2 == 0 else (4, 5)
                E("dve", lambda e: e.tensor_copy(out=cin[:, ci, 0, 0:30], in_=hg[:, l, c, :]),
                  reads=[t_hg[l]], writes=[t_cin[ci]])
                for kc in range(KC):
                    mm(P[pa_][:, :], wsl[:, sa, kc, j * 128:(j + 1) * 128], u[:, kc, :], kc == 0, kc == KC - 1,
                       [t_wsl[sa], t_u[kc]], [t_P[pa_]])
                for kc in range(KC):
                    mm(P[pg_][:, :], wsl[:, sg, kc, j * 128:(j + 1) * 128], u[:, kc, :], kc == 0, kc == KC - 1,
                       [t_wsl[sg], t_u[kc]], [t_P[pg_]])
                if j == 3:
                    stream.release(2)
                ti = next_tmp()
                E("act", lambda e: e.activation(out=tmpf[:, ti, :], in_=P[pg_][:, :], func=AF.Sigmoid),
                  reads=[t_P[pg_]], writes=[t_tmpf[ti]])
                E("dve", lambda e: e.tensor_tensor(out=cin[:, ci, 0, 30:30 + T], in0=P[pa_][:, :],
                                                   in1=tmpf[:, ti, :], op=ALU.mult),
                  reads=[t_P[pa_], t_tmpf[ti]], writes=[t_cin[ci]])
                E("dve", lambda e: e.tensor_copy(out=hg[:, l, c, :], in_=cin[:, ci, 0, T:T + 30]),
                  reads=[t_cin[ci]], writes=[t_hg[l]])

            def cf_conv(c):
                ci = cf[("ci", c)]
                pc = 2 + c % 2
                conv_pe(l, P[pc], t_P[pc], ci, 0, _off["cvw"], 31, c)
                bias_ = pv[:, l, _off["cvb"] + c:_off["cvb"] + c + 1]
                E("act", lambda e: e.activation(out=my[:, c, :], in_=P[pc][:, :], func=AF.Identity, bias=bias_, scale=1.0),
                  reads=[t_P[pc], t_pv], writes=[t_my[c]])
                t2 = next_tmp()
                cf[("t2", c)] = t2
                ysq = tmpf[:, t2, :].bitcast(BF16)[:, 0:T]
                E("act", lambda e: e.activation(out=ysq, in_=P[pc][:, :], func=AF.Square, bias=bias_, scale=1.0),
                  reads=[t_P[pc], t_pv], writes=[t_tmpf[t2]])

            def cf_stats(c):
                t2 = cf[("t2", c)]
                ysq = tmpf[:, t2, :].bitcast(BF16)[:, 0:T]
                mm(P[6][:, :], onesln[:], my[:, c, :], c == 0, c == KC - 1, [t_const, t_my[c]], [t_P[6]])
                mm(P[7][:, :], onesln[:], ysq, c == 0, c == KC - 1, [t_const, t_tmpf[t2]], [t_P[7]])

            cf_proj(0)
            for c in range(KC):
                if c + 1 < KC:
                    cf_proj(c + 1)
                cf_conv(c)
                if c >= 1:
                    cf_stats(c - 1)
            cf_stats(KC - 1)
            pass
            E("act", lambda e: e.activation(out=stat[:, 0, :], in_=P[6][:, :], func=AF.Copy),
              reads=[t_P[6]], writes=[t_stat])
            ti = next_tmp()
            E("act", lambda e: e.activation(out=tmpf[:, ti, :], in_=P[6][:, :], func=AF.Square),
              reads=[t_P[6]], writes=[t_tmpf[ti]])
            E("dve", lambda e: e.tensor_tensor(out=tmpf[:, ti, :], in0=P[7][:, :], in1=tmpf[:, ti, :], op=ALU.subtract),
              reads=[t_P[7], t_tmpf[ti]], writes=[t_tmpf[ti]])
            E("act", lambda e: e.activation(out=tmpf[:, ti, :], in_=tmpf[:, ti, :], func=AF.Sqrt, bias=EPS, scale=1.0),
              reads=[t_tmpf[ti]], writes=[t_tmpf[ti]])
            E("dve", lambda e: e.reciprocal(out=stat[:, 1, :], in_=tmpf[:, ti, :]),
              reads=[t_tmpf[ti]], writes=[t_stat])
            for c in range(KC):
                ti = next_tmp()
                E("dve", lambda e: e.tensor_tensor(out=tmpf[:, ti, :], in0=my[:, c, :], in1=stat[:, 0, :], op=ALU.subtract),
                  reads=[t_my[c], t_stat], writes=[t_tmpf[ti]])
                E("dve", lambda e: e.tensor_tensor(out=tmpf[:, ti, :], in0=tmpf[:, ti, :], in1=stat[:, 1, :], op=ALU.mult),
                  reads=[t_tmpf[ti], t_stat], writes=[t_tmpf[ti]])
                E("act", lambda e: e.activation(out=my[:, c, :], in_=tmpf[:, ti, :], func=AF.Silu,
                                                scale=pv[:, l, _off["lng"] + c:_off["lng"] + c + 1],
                                                bias=pv[:, l, _off["lnb"] + c:_off["lnb"] + c + 1]),
                  reads=[t_tmpf[ti], t_pv], writes=[t_my[c]])

            chk(5)
            for sj in range(4):
                si = stream.take()
                for tt in range(TT):
                    pa = (0, 1, 4, 5)[(sj * TT + tt) % 4]
                    for kc in range(KC):
                        mm(P[pa][:, :], u[:, kc, tt * 128:(tt + 1) * 128], wsl[:, si, kc, :], kc == 0, kc == KC - 1,
                           [t_u[kc], t_wsl[si]], [t_P[pa]])
                    if sj < 2:
                        for hh in range(2):
                            h = sj * 2 + hh
                            E("act" if hh else "dve", (lambda e: e.activation(
                                out=vp[:, tt, h, 0:DH], in_=P[pa][:, hh * DH:(hh + 1) * DH], func=AF.Identity,
                                scale=sm[:, 80 + tt * 4 + h:81 + tt * 4 + h])) if hh else (lambda e: e.tensor_scalar(
                                    out=vp[:, tt, h, 0:DH], in0=P[pa][:, hh * DH:(hh + 1) * DH],
                                    scalar1=sm[:, 80 + tt * 4 + h:81 + tt * 4 + h], scalar2=None, op0=ALU.mult)),
                              reads=[t_P[pa], t_sm], writes=[t_vp[tt]])
                    else:
                        E("act", lambda e: e.activation(out=go[:, tt, (sj - 2) * 512:(sj - 1) * 512], in_=P[pa][:, :],
                                                        func=AF.Sigmoid), reads=[t_P[pa]], writes=[t_go[tt]])
                stream.release()

            chk(6)
            for tt in range(TT):
                ts = slice(tt * 128, (tt + 1) * 128)
                for c in range(KC):
                    mm(P[c // 4][:, (c % 4) * 128:(c % 4 + 1) * 128], R1[:, 8 + c, ts], ident[:], True, True,
                       [t_R1[8 + c], t_const], [t_P[c // 4]])
                for a_ in range(2):
                    E("act" if a_ else "dve", (lambda e: e.activation(out=R1[:, 16 + 2 * tt + a_, :], in_=P[a_][:, :], func=AF.Copy))
                      if a_ else (lambda e: e.tensor_copy(out=R1[:, 16 + 2 * tt + a_, :], in_=P[a_][:, :])),
                      reads=[t_P[a_]], writes=[t_R1[16 + 2 * tt + a_]])
                ktf = lambda h, dc: R1[:, 16 + 2 * tt + h // 2, (h % 2) * 256 + dc * 128:(h % 2) * 256 + (dc + 1) * 128]
                for h in range(H):
                    for dc in range(2):
                        mm(P[2][:, h * 128:(h + 1) * 128], R1[:, 8 + 2 * h + dc, ts], R1[:, 2 * h + dc, ts],
                           dc == 0, dc == 1, [t_R1[8 + 2 * h + dc], t_R1[2 * h + dc]], [t_P[2]])
                Sts = []
                for h in range(H):
                    ti = next_tmp()
                    St = tmpf[:, ti, :].bitcast(BF16)[:, 0:128]
                    Sts.append((St, ti))
                    E("dve", lambda e: e.tensor_tensor(out=St, in0=P[2][:, h * 128:(h + 1) * 128], in1=mask[:], op=ALU.mult),
                      reads=[t_P[2], t_const], writes=[t_tmpf[ti]])
                for h in range(H):
                    St, ti = Sts[h]
                    pn = P[4 + h // 2]
                    tpn = t_P[4 + h // 2]
                    no = (h % 2) * DH
                    for dc in range(2):
                        mm(pn[:, no:no + DH], R1[:, 2 * h + dc, ts], Cb[:, l, h, dc * DH:(dc + 1) * DH], dc == 0, False,
                           [t_R1[2 * h + dc], t_Cb[l][h]], [tpn])
                    mm(pn[:, no:no + DH], St, vp[:, tt, h, 0:DH], False, True, [t_tmpf[ti], t_vp[tt]], [tpn])
                for h in range(H):
                    St, ti = Sts[h]
                    for dc in range(2):
                        mm(P[3][:, h * 2:h * 2 + 2], R1[:, 2 * h + dc, ts], nbp(l, h, dc), dc == 0, False, [t_R1[2 * h + dc], t_nb[l]], [t_P[3]])
                    mm(P[3][:, h * 2:h * 2 + 2], St, vp[:, tt, h, DH:DH + 2], False, True, [t_tmpf[ti], t_vp[tt]], [t_P[3]])
                for h in range(H):
                    for dc in range(2):
                        mm(P[3][:, 16 + h * 4 + dc * 2:18 + h * 4 + dc * 2], ktf(h, dc), vp[:, tt, h, DH:DH + 2], True, True,
                           [t_R1[16 + 2 * tt + h // 2], t_vp[tt]], [t_P[3]])
                elp4 = elp[:, l, :] if tt == 0 else sm[:, 112 + (tt - 1) * 4:112 + tt * 4]
                t_elprev = t_elp[l] if tt == 0 else t_sm
                el4 = sm[:, 112 + tt * 4:116 + tt * 4]
                for h in range(H):
                    idx = tt * 4 + h
                    pu = P[6 + h % 2]
                    tpu = t_P[6 + h % 2]
                    for dc in range(2):
                        mm(pu[:, dc * DH:(dc + 1) * DH], ktf(h, dc), vp[:, tt, h, 0:DH], True, True,
                           [t_R1[16 + 2 * tt + h // 2], t_vp[tt]], [tpu])
                    E("dve", lambda e: e.scalar_tensor_tensor(out=Tm[:, l, h, :], in0=Tm[:, l, h, :], scalar=elp4[:, h:h + 1],
                                                              in1=pu[:, :], op0=ALU.mult, op1=ALU.add),
                      reads=[t_Tm[l][h], t_elprev, tpu], writes=[t_Tm[l][h]])
                    E("act", lambda e: e.activation(out=Cb[:, l, h, :], in_=Tm[:, l, h, :], func=AF.Identity,
                                                    scale=sm[:, 112 + idx:113 + idx]),
                      reads=[t_Tm[l][h], t_sm], writes=[t_Cb[l][h]])
                nsl = P[3][:, 16:32].rearrange("p (h d two) -> p h d two", h=4, two=2)[:, :, :, 0]
                E("dve", lambda e: e.tensor_tensor(out=nm[:, l], in0=nm[:, l], in1=elp4.unsqueeze(2).broadcast_to([128, H, 2]),
                                                   op=ALU.mult), reads=[t_nm[l], t_elprev], writes=[t_nm[l]])
                E("dve", lambda e: e.tensor_tensor(out=nm[:, l], in0=nm[:, l], in1=nsl, op=ALU.add),
                  reads=[t_nm[l], t_P[3]], writes=[t_nm[l]])
                E("dve", lambda e: e.tensor_tensor(out=nbw[:, l], in0=nm[:, l].unsqueeze(3).broadcast_to([128, H, 2, 2]),
                                                   in1=el4.unsqueeze(2).unsqueeze(3).broadcast_to([128, H, 2, 2]), op=ALU.mult),
                  reads=[t_nm[l], t_sm], writes=[t_nb[l]])
                e4 = sm[:, 96 + tt * 4:100 + tt * 4]
                den = P[3][:, 0:8].rearrange("p (h two) -> p h two", two=2)[:, :, 0]
                W0 = sm[:, 136:140]
                W1 = sm[:, 140:144]
                W2 = sm[:, 144:148]
                E("dve", lambda e: e.tensor_tensor(out=W0, in0=den, in1=e4, op=ALU.mult), reads=[t_P[3], t_sm], writes=[t_sm])
                E("act", lambda e: e.activation(out=W0, in_=W0, func=AF.Abs), reads=[t_sm], writes=[t_sm])
                E("dve", lambda e: e.tensor_scalar_max(out=W0, in0=W0, scalar1=1.0), reads=[t_sm], writes=[t_sm])
                E("dve", lambda e: e.reciprocal(out=W0, in_=W0), reads=[t_sm], writes=[t_sm])
                E("dve", lambda e: e.tensor_tensor(out=W0, in0=W0, in1=e4, op=ALU.mult), reads=[t_sm], writes=[t_sm])
                for h in range(H):
                    pn = P[4 + h // 2]
                    no = (h % 2) * DH
                    E("act", lambda e: e.activation(out=junk[:, 0:DH], in_=pn[:, no:no + DH], func=AF.Square,
                                                    accum_out=sm[:, 148 + h:149 + h]),
                      reads=[t_P[4 + h // 2]], writes=[t_junk, t_sm])
                SS = sm[:, 148:152]
                E("dve", lambda e: e.tensor_tensor(out=W1, in0=W0, in1=W0, op=ALU.mult), reads=[t_sm], writes=[t_sm])
                E("dve", lambda e: e.tensor_tensor(out=W1, in0=W1, in1=SS, op=ALU.mult), reads=[t_sm], writes=[t_sm])
                E("act", lambda e: e.activation(out=W1, in_=W1, func=AF.Sqrt, scale=1.0 / DH, bias=EPS), reads=[t_sm], writes=[t_sm])
                E("dve", lambda e: e.reciprocal(out=W1, in_=W1), reads=[t_sm], writes=[t_sm])
                E("dve", lambda e: e.tensor_tensor(out=W2, in0=W1, in1=W0, op=ALU.mult), reads=[t_sm], writes=[t_sm])
                hb = tt % 2
                for h in range(H):
                    pn = P[4 + h // 2]
                    no = (h % 2) * DH
                    E("dve", lambda e: e.scalar_tensor_tensor(out=hfin[:, hb, h * DH:(h + 1) * DH], in0=pn[:, no:no + DH],
                                                              scalar=sm[:, 144 + h:145 + h], in1=go[:, tt, h * DH:(h + 1) * DH],
                                                              op0=ALU.mult, op1=ALU.mult),
                      reads=[t_P[4 + h // 2], t_sm, t_go[tt]], writes=[t_hfin[hb]])
                for c in range(KC):
                    mm(P[c // 4][:, (c % 4) * 128:(c % 4 + 1) * 128], hfin[:, hb, c * 128:(c + 1) * 128], ident[:], True, True,
                       [t_hfin[hb], t_const], [t_P[c // 4]])
                for c in range(KC):
                    src = P[c // 4][:, (c % 4) * 128:(c % 4 + 1) * 128]
                    E("dve" if c % 2 else "act", (lambda e: e.tensor_scalar(
                        out=u[:, c, ts], in0=src,
                        scalar1=pv[:, l, _off["mlg"] + c:_off["mlg"] + c + 1], scalar2=None, op0=ALU.mult)) if c % 2 else (
                        lambda e: e.activation(out=u[:, c, ts], in_=src, func=AF.Identity,
                                               scale=pv[:, l, _off["mlg"] + c:_off["mlg"] + c + 1])),
                      reads=[t_P[c // 4], t_pv], writes=[t_u[c]])
            E("dve", lambda e: e.tensor_copy(out=elp[:, l, :], in_=sm[:, 124:128]), reads=[t_sm], writes=[t_elp[l]])

            chk(7)
            for nh in range(2):
                sis = [stream.take(), stream.take()]
                for tt in range(TT):
                    pi = nh * 4 + tt
                    for kg in range(2):
                        si = sis[kg]
                        for kc in range(KC):
                            src = (u[:, kc, tt * 128:(tt + 1) * 128], t_u[kc]) if kg == 0 else \
                                (my[:, kc, tt * 128:(tt + 1) * 128], t_my[kc])
                            mm(P[pi][:, :], src[0], wsl[:, si, kc, :], kg == 0 and kc == 0, kg == 1 and kc == KC - 1,
                               [src[1], t_wsl[si]], [t_P[pi]])
                    post_sq(pi)
                stream.release(2)
            post_norm_all(l, 0)

            chk(8)
            prenorm(l, 1)
            pend = []
            for sj in range(6):
                sa = stream.take()
                sg = stream.take()
                nch = 4 if sj < 5 else 2
                ca = cin_ctr[0] % NCIN
                cin_ctr[0] += 1
                cg = cin_ctr[0] % NCIN
                cin_ctr[0] += 1
                c0 = sj * 4
                E("dve", lambda e: e.tensor_copy(out=cin[:, ca, 0:nch, 0:2], in_=hf[:, l, c0:c0 + nch, :]),
                  reads=[t_hf[l]], writes=[t_cin[ca]])
                E("dve", lambda e: e.tensor_copy(out=cin[:, cg, 0:nch, 0:2], in_=hf[:, l, NFC + c0:NFC + c0 + nch, :]),
                  reads=[t_hf[l]], writes=[t_cin[cg]])
                for j in range(nch):
                    for (sw, cb, pp) in ((sa, ca, 0 if j % 2 == 0 else 6), (sg, cg, 1 if j % 2 == 0 else 7)):
                        for kc in range(KC):
                            mm(P[pp][:, :], wsl[:, sw, kc, j * 128:(j + 1) * 128], u[:, kc, :], kc == 0, kc == KC - 1,
                               [t_wsl[sw], t_u[kc]], [t_P[pp]])
                        E("act" if pp % 2 else "dve", (lambda e: e.activation(out=cin[:, cb, j, 2:2 + T], in_=P[pp][:, :], func=AF.Copy))
                          if pp % 2 else (lambda e: e.tensor_copy(out=cin[:, cb, j, 2:2 + T], in_=P[pp][:, :])),
                          reads=[t_P[pp]], writes=[t_cin[cb]])
                stream.release(2)
                E("dve", lambda e: e.tensor_copy(out=hf[:, l, c0:c0 + nch, :], in_=cin[:, ca, 0:nch, T:T + 2]),
                  reads=[t_cin[ca]], writes=[t_hf[l]])
                E("dve", lambda e: e.tensor_copy(out=hf[:, l, NFC + c0:NFC + c0 + nch, :], in_=cin[:, cg, 0:nch, T:T + 2]),
                  reads=[t_cin[cg]], writes=[t_hf[l]])
                for j in range(nch):
                    c = c0 + j
                    qa, qg = (2, 3) if c % 2 == 0 else (4, 5)
                    conv_pe(l, P[qa], t_P[qa], ca, j, _off["fw"], 3, c)
                    conv_pe(l, P[qg], t_P[qg], cg, j, _off["fw"], 3, NFC + c)
                    ti = next_tmp()
                    E("act", lambda e: e.activation(out=tmpf[:, ti, :], in_=P[qg][:, :], func=AF.Silu,
                                                    bias=pv[:, l, _off["fb"] + NFC + c:_off["fb"] + NFC + c + 1], scale=1.0),
                      reads=[t_P[qg], t_pv], writes=[t_tmpf[ti]])
                    E("dve", lambda e: e.scalar_tensor_tensor(out=R1[:, c, :], in0=P[qa][:, :],
                                                              scalar=pv[:, l, _off["fb"] + c:_off["fb"] + c + 1],
                                                              in1=tmpf[:, ti, :], op0=ALU.add, op1=ALU.mult),
                      reads=[t_P[qa], t_pv, t_tmpf[ti]], writes=[t_R1[c]])
            for nh in range(2):
                sis = [stream.take(), stream.take(), stream.take()]
                for tt in range(TT):
                    pi = nh * 4 + tt
                    for kg in range(3):
                        nk = 8 if kg < 2 else 6
                        si = sis[kg]
                        for kc in range(nk):
                            cidx = kg * 8 + kc
                            mm(P[pi][:, :], R1[:, cidx, tt * 128:(tt + 1) * 128], wsl[:, si, kc, :],
                               cidx == 0, cidx == NFC - 1, [t_R1[cidx], t_wsl[si]], [t_P[pi]])
                    post_sq(pi)
                stream.release(3)
            post_norm_all(l, 1)

        def nbp(l, h, dc):
            return nbw[:, l, h, dc, :]

        def nb2(l, h):
            return nbw[:, l, h, :, :]

        E("dve", lambda e: e.memset(sm[:, 128:129], float(-np.log(16.0))), writes=[t_sm])

        for s in range(nseq):
            queue_setup_weights(s)
            for b in range(nblk):
                for l in range(nlayers):
                    queue_layer_weights(l)
        for s in range(nseq):
            setup_seq(s)
            for b in range(nblk):
                for tt in range(TT):
                    fw.dma("sp", xb[:, tt, :], x_d[s, b * T + tt * 128: b * T + (tt + 1) * 128, :], writes=[t_xb[tt]])
                try:
                    for l in range(nlayers):
                        block_layer(l)
                except _Stop:
                    pass
                for tt in range(TT):
                    out_toks.append(fw.dma("sp", y_d[s, b * T + tt * 128: b * T + (tt + 1) * 128, :], xb[:, tt, :],
                                           reads=[t_xb[tt]]))
        fw.finish(out_toks)
        print("instructions:", fw.ninst, fw.per, "sems:", fw.nsems, "slabs:", len(stream.items))
    return nc


def _fm(v):
    return np.ascontiguousarray(v.reshape(-1, 128).T)


def _fmw(w):
    K, C = w.shape
    return np.ascontiguousarray(w.T.reshape(C // 128, 128, K).transpose(1, 0, 2).reshape(128, -1))


def pack_params(inp, layers):
    pvs, bcs = [], []
    for l in layers:
        ab = inp["ada_b"][l]
        cols = [
            _fm(inp["mix_pre_g"][l]), _fmw(inp["qk_conv_w"][l]), _fm(inp["qk_conv_b"][l]),
            _fmw(inp["cv_dw_w"][l]), _fm(inp["cv_dw_b"][l]), _fm(inp["cv_ln_g"][l]), _fm(inp["cv_ln_b"][l]),
            _fm(inp["ffn_pre_g"][l]), _fmw(inp["ffn_conv_w"][l]), _fm(inp["ffn_conv_b"][l]),
            _fm(inp["ml_norm_g"][l]),
            _fm(ab[0:1024]), _fm(ab[1024:2048]), _fm(ab[3072:4096]), _fm(ab[4096:5120]),
        ]
        pvs.append(np.concatenate(cols, axis=1))
        gbias = np.concatenate([inp["igate_b"][l], inp["fgate_b"][l]])
        bcs.append(np.concatenate([inp["mix_post_g"][l], inp["ffn_post_g"][l], ab[2048:3072], ab[5120:6144],
                                   np.tile(gbias, 4)]))
    pvec = np.ascontiguousarray(np.stack(pvs)).astype(np.float32)
    bcv = np.ascontiguousarray(np.stack(bcs)).astype(np.float32)
    assert pvec.shape[2] == NPV and bcv.shape[1] == NBC
    return pvec, bcv


def make_in_maps(inp, ncores, nseq, nblk, layers):
    pvec, bcv = pack_params(inp, layers)
    S = nblk * T
    L = list(layers)
    shared = {
        "ada_w": np.ascontiguousarray(inp["ada_w"][L]), "w_in": np.ascontiguousarray(inp["w_in"][L]),
        "w_out": np.ascontiguousarray(inp["w_out"][L]), "ffn_up": np.ascontiguousarray(inp["ffn_up"][L]),
        "ffn_down": np.ascontiguousarray(inp["ffn_down"][L]), "pvec": pvec, "bcv": bcv,
    }
    maps = []
    for c in range(ncores):
        xs = np.ascontiguousarray(inp["x"][c * nseq:(c + 1) * nseq, :S, :])
        cc = inp["c"][c * nseq:(c + 1) * nseq]
        cT = np.ascontiguousarray(cc.reshape(nseq, KC, 128).transpose(2, 0, 1).reshape(128, nseq * KC))
        m = dict(shared)
        m["x"] = xs
        m["cT"] = cT
        maps.append(m)
    return maps


_NC_CACHE = {}


def kernel(**inputs):
    inp = {k: np.asarray(v, dtype=np.float32) for k, v in inputs.items()}
    ncores, nseq, nblk = 8, 2, 4
    key = (nseq, nblk, 2)
    if key not in _NC_CACHE:
        _NC_CACHE[key] = build(nseq, nblk, 2)
    nc = _NC_CACHE[key]
    maps = make_in_maps(inp, ncores, nseq, nblk, (0, 1))
    res = run_bass_kernel_spmd(nc, maps, core_ids=list(range(ncores)))
    out = np.concatenate([np.asarray(r["y"]).reshape(nseq, SEQ, D) for r in res.results], axis=0)
    return out.astype(np.float32)
```

```python
import numpy as np
from contextlib import ExitStack
import concourse.bass as bass
import concourse.mybir as mybir
from concourse.bass_utils import run_bass_kernel_spmd

F32 = mybir.dt.float32
BF16 = mybir.dt.bfloat16
AF = mybir.ActivationFunctionType
ALU = mybir.AluOpType

D = 1024
KC = 8
T = 512
TT = 4
H = 4
DH = 256
SEQ = 2048
N_IN = 6152
DFF = 2816
NFC = 22
EPS = 1e-6

_off = {}
_n = 0
for _name, _w in (("pre1", 8), ("qkw", 64), ("qkb", 16), ("cvw", 248), ("cvb", 8), ("lng", 8),
                  ("lnb", 8), ("pre2", 8), ("fw", 132), ("fb", 44), ("mlg", 8), ("adab", 32)):
    _off[_name] = _n
    _n += _w
NPV = _n
BC_POST1, BC_POST2, BC_ADAG1, BC_ADAG2, BC_GB = 0, 1024, 2048, 3072, 4096
NBC = 4096 + 32


class Tk:
    __slots__ = ("name", "w", "r", "dsem", "dcount", "excl")

    def __init__(self, name, excl=False):
        self.name = name
        self.excl = excl
        self.w = None
        self.r = {}
        self.dsem = None
        self.dcount = 0


class Eng:
    def __init__(self, fw, name, handle):
        self.fw = fw
        self.name = name
        self.h = handle
        self.sem = None
        self.count = 0
        self.waited = {}
        self.nsem = 0

    def newsem(self):
        self.sem = self.fw.alloc_sem(f"{self.name}_e{self.nsem}")
        self.nsem += 1
        self.count = 0


class FW:
    EPOCH = 12000
    STRICT = True

    def __init__(self, nc, stack):
        self.nc = nc
        self.stack = stack
        self.nsems = 0
        self.engs = {}
        for name, h in (("pe", nc.tensor), ("act", nc.scalar), ("dve", nc.vector),
                        ("pool", nc.gpsimd), ("sp", nc.sync)):
            e = Eng(self, name, h)
            e.newsem()
            self.engs[name] = e
        self.ninst = 0
        self.per = {k: 0 for k in self.engs}

    def alloc_sem(self, name):
        self.nsems += 1
        return self.stack.enter_context(self.nc.semaphore(name))

    def _wait(self, eng, tok, kind):
        sem, val, en = tok
        if en == eng.name and kind != "raw" and (eng.name == "pe" or not self.STRICT):
            return
        if eng.waited.get(sem, 0) >= val:
            return
        eng.h.wait_ge(sem, val)
        eng.waited[sem] = val
        self.ninst += 1

    def _deps(self, eng, reads, writes):
        for t in reads:
            if t.w is not None:
                self._wait(eng, t.w, "raw")
            if t.excl:
                for sem, (val, en) in t.r.items():
                    if en != eng.name:
                        self._wait(eng, (sem, val, en), "raw")
        for t in writes:
            if t.w is not None:
                self._wait(eng, t.w, "waw")
            for sem, (val, en) in t.r.items():
                self._wait(eng, (sem, val, en), "war")

    def op(self, engname, fn, reads=(), writes=()):
        eng = self.engs[engname]
        self._deps(eng, reads, writes)
        inst = fn(eng.h)
        if eng.count >= self.EPOCH:
            eng.newsem()
        inst.then_inc(eng.sem, 1)
        eng.count += 1
        self.ninst += 1
        self.per[engname] += 1
        tok = (eng.sem, eng.count, eng.name)
        for t in reads:
            t.r[eng.sem] = (eng.count, eng.name)
        for t in writes:
            t.w = tok
            t.r = {}
        return inst

    def dma(self, qname, out, in_, reads=(), writes=(), **kw):
        eng = self.engs[qname]
        for t in reads:
            if t.w is not None:
                self._wait(eng, t.w, "raw")
        for t in writes:
            if t.w is not None:
                self._wait(eng, t.w, "raw")
            for sem, (val, en) in t.r.items():
                self._wait(eng, (sem, val, en), "raw")
        owner = (list(writes) + list(reads))[0]
        if owner.dsem is None:
            owner.dsem = self.alloc_sem("d_" + owner.name)
        inst = eng.h.dma_start(out=out, in_=in_, **kw)
        inst.then_inc(owner.dsem, 16)
        owner.dcount += 16
        self.ninst += 1
        tok = (owner.dsem, owner.dcount, "dma")
        for t in reads:
            t.r[owner.dsem] = (owner.dcount, "dma")
        for t in writes:
            t.w = tok
            t.r = {}
        return tok

    def finish(self, toks):
        eng = self.engs["sp"]
        for tok in toks:
            self._wait(eng, tok, "raw")


class _Stop(Exception):
    pass


def build(nseq=2, nblk=4, nlayers=2, dbg=(), stop=99):
    S = nblk * T
    nc = bass.Bass("TRN2", target_bir_lowering=False)
    dr = lambda name, shape, kind="ExternalInput": nc.dram_tensor(name, shape, F32, kind=kind).ap()
    x_d = dr("x", [nseq, S, D])
    c_d = dr("cT", [128, nseq * KC])
    ada_d = dr("ada_w", [nlayers, D, 6 * D])
    win_d = dr("w_in", [nlayers, D, N_IN])
    wout_d = dr("w_out", [nlayers, 2 * D, D])
    fup_d = dr("ffn_up", [nlayers, D, 2 * DFF])
    fdn_d = dr("ffn_down", [nlayers, DFF, D])
    pv_d = dr("pvec", [nlayers, 128, NPV])
    bc_d = dr("bcv", [nlayers, NBC])
    y_d = dr("y", [nseq, S, D], kind="ExternalOutput")
    dbg_d = {name: dr("dbg_" + name, shape, kind="ExternalOutput") for name, shape in dbg}

    st = ExitStack()
    with st:
        sb = lambda name, shape, dt=F32: st.enter_context(nc.sbuf_tensor(name, shape, dt))
        NSLOT = 4
        wsl = sb("wsl", [128, NSLOT, KC, 512], BF16)
        xb = sb("xb", [128, TT, D])
        xn = sb("xn", [128, 2, D], BF16)
        u = sb("u", [128, KC, T], BF16)
        R1 = sb("R1", [128, 24, T], BF16)
        vp = sb("vp", [128, TT, H, DH + 2], BF16)
        go = sb("go", [128, TT, D], BF16)
        NCIN = 2
        CINW = 544
        cin = sb("cin", [128, NCIN, 4, CINW], BF16)
        NDG = 3
        DGT = 16
        dg = sb("dg", [128, NDG, DGT, 128], BF16)
        tmpf = sb("tmpf", [128, 4, T])
        stat = sb("stat", [128, 2, T])
        my = sb("my", [128, KC, T], BF16)
        hfin = sb("hfin", [128, 2, D], BF16)
        Tm = sb("Tm", [128, nlayers, H, 2 * DH])
        Cb = sb("Cb", [128, nlayers, H, 2 * DH], BF16)
        nm = sb("nm", [128, nlayers, H, 2])
        nbw = sb("nbw", [128, nlayers, H, 2, 2], BF16)
        elp = sb("elp", [128, nlayers, H])
        hq = sb("hq", [128, nlayers, 16, 3], BF16)
        hg = sb("hg", [128, nlayers, 8, 30], BF16)
        hf = sb("hf", [128, nlayers, 2 * NFC, 2], BF16)
        gp = sb("gp", [128, nlayers, 2, D])
        pv = sb("pv", [128, nlayers, NPV])
        gb = sb("gb", [128, nlayers, 32])
        modv = sb("modv", [128, nlayers, 4, KC])
        junk = sb("junk", [128, D], BF16)
        ident = sb("ident", [128, 128], BF16)
        identf = sb("identf", [128, 128])
        mask = sb("mask", [128, 128])
        onesf = sb("onesf", [128, 128])
        onesln = sb("onesln", [128, 128], BF16)
        cT = sb("cTs", [128, nseq * KC])
        condb = sb("condb", [128, KC, 2], BF16)
        condbc = sb("condbc", [128, KC, 128], BF16)
        sm = sb("sm", [128, 512])
        sm2 = sb("sm2", [128, 8])
        numsb = sb("numsb", [128, D])
        P = [st.enter_context(nc.psum_tensor(f"P{i}", [128, 512], F32)) for i in range(8)]
        PB = [p[:].bitcast(BF16) for p in P]

        fw = FW(nc, st)
        st.enter_context(nc.Block())

        tk = lambda n: Tk(n)
        t_wsl = [tk(f"wsl{i}") for i in range(NSLOT)]
        t_xb = [tk(f"xb{i}") for i in range(TT)]
        t_xn = [tk(f"xn{i}") for i in range(2)]
        t_u = [tk(f"u{i}") for i in range(KC)]
        t_R1 = [tk(f"R1_{i}") for i in range(24)]
        t_vp = [tk(f"vp{i}") for i in range(TT)]
        t_go = [tk(f"go{i}") for i in range(TT)]
        t_cin = [tk(f"cin{i}") for i in range(NCIN)]
        t_dg = [tk(f"dg{i}") for i in range(NDG)]
        t_tmpf = [tk(f"tmpf{i}") for i in range(4)]
        t_stat = tk("stat")
        t_my = [tk(f"my{i}") for i in range(KC)]
        t_hfin = [tk(f"hfin{i}") for i in range(2)]
        t_Tm = [[tk(f"Tm{l}_{h}") for h in range(H)] for l in range(nlayers)]
        t_Cb = [[tk(f"Cb{l}_{h}") for h in range(H)] for l in range(nlayers)]
        t_nm = [tk(f"nm{l}") for l in range(nlayers)]
        t_nb = [tk(f"nb{l}") for l in range(nlayers)]
        t_elp = [tk(f"elp{l}") for l in range(nlayers)]
        t_hq = [tk(f"hq{l}") for l in range(nlayers)]
        t_hg = [tk(f"hg{l}") for l in range(nlayers)]
        t_hf = [tk(f"hf{l}") for l in range(nlayers)]
        t_gp = [[tk(f"gp{l}_{j}") for j in range(2)] for l in range(nlayers)]
        t_pv = tk("pv")
        t_gb = tk("gb")
        t_modv = tk("modv")
        t_junk = tk("junk")
        t_const = tk("const")
        t_cT = tk("cT")
        t_cond = tk("cond")
        t_sm = tk("sm")
        t_sm2 = tk("sm2")
        t_numsb = tk("numsb")
        t_P = [Tk(f"P{i}", excl=True) for i in range(8)]
        out_toks = []

        E = fw.op

        def mm(out, lhsT, rhs, start, stop, reads, writes):
            return E("pe", lambda e: e.matmul(out, lhsT=lhsT, rhs=rhs, start=start, stop=stop),
                     reads, writes)

        def dump(name, src_ap, reads):
            if name in dbg_d:
                fw.dma("sp", dbg_d[name], src_ap, reads=reads)

        E("dve", lambda e: e.memset(identf[:], 0.0), writes=[t_const])
        E("pool", lambda e: e.affine_select(out=identf[:], in_=identf[:], pattern=[[-1, 128]],
                                            compare_op=ALU.not_equal, fill=1.0, base=0,
                                            channel_multiplier=1), reads=[t_const], writes=[t_const])
        E("dve", lambda e: e.tensor_copy(out=ident[:], in_=identf[:]), reads=[t_const], writes=[t_const])
        E("dve", lambda e: e.memset(onesf[:], 1.0), writes=[t_const])
        E("dve", lambda e: e.memset(onesln[:], 1.0 / D), writes=[t_const])
        E("dve", lambda e: e.memset(mask[:], 1.0), writes=[t_const])
        E("pool", lambda e: e.affine_select(out=mask[:], in_=mask[:], pattern=[[1, 128]],
                                            compare_op=ALU.is_ge, fill=0.0, base=0,
                                            channel_multiplier=-1), reads=[t_const], writes=[t_const])
        E("dve", lambda e: e.memset(junk[:], 0.0), writes=[t_junk])
        for l in range(nlayers):
            fw.dma("sp", pv[:, l, :], pv_d[l], writes=[t_pv])
            fw.dma("sp", gb[:, l, :], bc_d[l, BC_GB:BC_GB + 32].partition_broadcast(128), writes=[t_gb])
        fw.dma("sp", cT[:], c_d[:, :], writes=[t_cT])

        slot_ctr = [0]

        def load_slab(src2d, nkc, ncols):
            i = slot_ctr[0] % NSLOT
            slot_ctr[0] += 1
            fw.dma("pool", wsl[:, i, 0:nkc, 0:ncols],
                   src2d.rearrange("(kc p) n -> p kc n", p=128), writes=[t_wsl[i]])
            return i

        class Stream:
            def __init__(self):
                self.items = []
                self.issued = 0
                self.taken = 0
                self.released = 0
                self.slots = []

            def add(self, src2d, nkc, ncols):
                self.items.append((src2d, nkc, ncols))

            def pump(self):
                while self.issued < len(self.items) and self.issued < self.released + NSLOT:
                    self.slots.append(load_slab(*self.items[self.issued]))
                    self.issued += 1

            def take(self):
                self.pump()
                assert self.issued > self.taken
                i = self.slots[self.taken]
                self.taken += 1
                return i

            def release(self, n=1):
                self.released += n
                self.pump()

        stream = Stream()

        def setup_seq(s):
            E("act", lambda e: e.activation(out=condb[:, :, 0], in_=cT[:, s * KC:(s + 1) * KC], func=AF.Silu),
              reads=[t_cT], writes=[t_cond])
            E("act", lambda e: e.activation(out=condb[:, :, 1], in_=cT[:, s * KC:(s + 1) * KC], func=AF.Silu),
              reads=[t_cT], writes=[t_cond])
            E("dve", lambda e: e.tensor_copy(out=condbc[:], in_=condb[:, :, 0:1].broadcast_to([128, KC, 128])),
              reads=[t_cond], writes=[t_cond])
            for l in range(nlayers):
                for vi, c0 in enumerate((0, 1024, 3072, 4096)):
                    for half in range(2):
                        si = stream.take()
                        for j in range(4):
                            cc = half * 4 + j
                            col = (vi * KC + cc) * 2
                            for kc in range(KC):
                                mm(P[4][:, col:col + 2], wsl[:, si, kc, j * 128:(j + 1) * 128],
                                   condb[:, kc, :], kc == 0, kc == KC - 1,
                                   [t_wsl[si], t_cond], [t_P[4]])
                        stream.release()
                pview = P[4][:, 0:64].rearrange("p (v c two) -> p v c two", v=4, two=2)[:, :, :, 0]
                E("dve", lambda e: e.tensor_tensor(
                    out=modv[:, l, :, :], in0=pview,
                    in1=pv[:, l, _off["adab"]:_off["adab"] + 32].rearrange("p (v c) -> p v c", v=4),
                    op=ALU.add), reads=[t_P[4], t_pv], writes=[t_modv])
                for vi, pre in ((1, "pre1"), (3, "pre2")):
                    E("dve", lambda e: e.scalar_tensor_tensor(
                        out=modv[:, l, vi, :], in0=modv[:, l, vi, :], scalar=1.0,
                        in1=pv[:, l, _off[pre]:_off[pre] + 8], op0=ALU.add, op1=ALU.mult),
                      reads=[t_modv, t_pv], writes=[t_modv])
                for gi, (c0, bco, bcp) in enumerate(((2048, BC_ADAG1, BC_POST1), (5120, BC_ADAG2, BC_POST2))):
                    for half in range(2):
                        si = stream.take()
                        pb = P[5 + half]
                        for kc in range(KC):
                            mm(pb[:, :], condbc[:, kc, :], wsl[:, si, kc, :], kc == 0, kc == KC - 1,
                               [t_wsl[si], t_cond], [t_P[5 + half]])
                        stream.release()
                        fw.dma("sp", tmpf[:, half * 2, :],
                               bc_d[l, bco + half * 512: bco + (half + 1) * 512].partition_broadcast(128),
                               writes=[t_tmpf[half * 2]])
                        fw.dma("sp", tmpf[:, half * 2 + 1, :],
                               bc_d[l, bcp + half * 512: bcp + (half + 1) * 512].partition_broadcast(128),
                               writes=[t_tmpf[half * 2 + 1]])
                        E("dve", lambda e: e.tensor_tensor(out=gp[:, l, gi, half * 512:(half + 1) * 512],
                                                           in0=pb[:, :], in1=tmpf[:, half * 2, :], op=ALU.add),
                          reads=[t_P[5 + half], t_tmpf[half * 2]], writes=[t_gp[l][gi]])
                        E("dve", lambda e: e.tensor_tensor(out=gp[:, l, gi, half * 512:(half + 1) * 512],
                                                           in0=gp[:, l, gi, half * 512:(half + 1) * 512],
                                                           in1=tmpf[:, half * 2 + 1, :], op=ALU.mult),
                          reads=[t_gp[l][gi], t_tmpf[half * 2 + 1]], writes=[t_gp[l][gi]])
            for l in range(nlayers):
                E("dve", lambda e: e.memset(Tm[:, l], 0.0), writes=t_Tm[l])
                E("dve", lambda e: e.memset(Cb[:, l], 0.0), writes=t_Cb[l])
                E("dve", lambda e: e.memset(nm[:, l], 0.0), writes=[t_nm[l]])
                E("dve", lambda e: e.memset(nbw[:, l], 0.0), writes=[t_nb[l]])
                E("dve", lambda e: e.memset(elp[:, l], 1.0), writes=[t_elp[l]])
                E("dve", lambda e: e.memset(hq[:, l], 0.0), writes=[t_hq[l]])
                E("dve", lambda e: e.memset(hg[:, l], 0.0), writes=[t_hg[l]])
                E("dve", lambda e: e.memset(hf[:, l], 0.0), writes=[t_hf[l]])

        def queue_setup_weights(s):
            for l in range(nlayers):
                for c0 in (0, 1024, 3072, 4096, 2048, 5120):
                    for half in range(2):
                        stream.add(ada_d[l, :, c0 + half * 512: c0 + (half + 1) * 512], KC, 512)

        def queue_layer_weights(l):
            stream.add(win_d[l, :, 4096:4104], KC, 8)
            for j in range(4):
                stream.add(win_d[l, :, j * 512:(j + 1) * 512], KC, 512)
            for j in range(2):
                stream.add(win_d[l, :, 4104 + j * 512: 4104 + (j + 1) * 512], KC, 512)
                stream.add(win_d[l, :, 5128 + j * 512: 5128 + (j + 1) * 512], KC, 512)
            for j in range(4):
                stream.add(win_d[l, :, 2048 + j * 512: 2048 + (j + 1) * 512], KC, 512)
            for nh in range(2):
                for kg in range(2):
                    stream.add(wout_d[l, kg * 1024:(kg + 1) * 1024, nh * 512:(nh + 1) * 512], KC, 512)
            for j in range(6):
                w_ = 512 if j < 5 else 256
                stream.add(fup_d[l, :, j * 512: j * 512 + w_], KC, w_)
                stream.add(fup_d[l, :, DFF + j * 512: DFF + j * 512 + w_], KC, w_)
            for nh in range(2):
                for kg in range(3):
                    nk = 8 if kg < 2 else 6
                    stream.add(fdn_d[l, kg * 1024: kg * 1024 + nk * 128, nh * 512:(nh + 1) * 512], nk, 512)

        cin_ctr = [0]
        dg_ctr = [0]
        tmp_ctr = [0]

        def next_tmp():
            i = tmp_ctr[0] % 4
            tmp_ctr[0] += 1
            return i

        def prenorm(l, which):
            vs, vg = (0, 1) if which == 0 else (2, 3)
            for tt in range(TT):
                E("act", lambda e: e.activation(out=junk[:], in_=xb[:, tt, :], func=AF.Square,
                                                accum_out=sm[:, tt:tt + 1]),
                  reads=[t_xb[tt]], writes=[t_junk, t_sm])
            chk(1.2)
            E("act", lambda e: e.activation(out=sm[:, 4:8], in_=sm[:, 0:4], func=AF.Sqrt, scale=1.0 / D, bias=EPS),
              reads=[t_sm], writes=[t_sm])
            E("dve", lambda e: e.reciprocal(out=sm[:, 8:12], in_=sm[:, 4:8]), reads=[t_sm], writes=[t_sm])
            chk(1.4)
            for tt in range(TT):
                b = tt % 2
                E("dve", lambda e: e.tensor_scalar(out=xn[:, b, :], in0=xb[:, tt, :], scalar1=sm[:, 8 + tt:9 + tt],
                                                   scalar2=None, op0=ALU.mult),
                  reads=[t_xb[tt], t_sm], writes=[t_xn[b]])
                chk(1.6)
                pi = (tt % 2) * 2
                for kc in range(KC):
                    pq = pi + kc // 4
                    mm(P[pq][:, (kc % 4) * 128:(kc % 4 + 1) * 128], xn[:, b, kc * 128:(kc + 1) * 128], ident[:],
                       True, True, [t_xn[b], t_const], [t_P[pq]])
                chk(1.8)
                for kc in range(KC):
                    pq = pi + kc // 4
                    src = P[pq][:, (kc % 4) * 128:(kc % 4 + 1) * 128]
                    import os as _os
                    _sel = {"dve": True, "act": False}.get(_os.environ.get("EVAC", ""), kc % 2)
                    E("dve" if _sel else "act", (lambda e: e.tensor_scalar(
                        out=u[:, kc, tt * 128:(tt + 1) * 128], in0=src,
                        scalar1=modv[:, l, vg, kc:kc + 1], scalar2=modv[:, l, vs, kc:kc + 1],
                        op0=ALU.mult, op1=ALU.add)) if _sel else (lambda e: e.activation(
                            out=u[:, kc, tt * 128:(tt + 1) * 128], in_=src,
                            func=AF.Identity, scale=modv[:, l, vg, kc:kc + 1], bias=modv[:, l, vs, kc:kc + 1])),
                      reads=[t_P[pq], t_modv], writes=[t_u[kc]])

        def build_diag(l, woff, ntap, chunk, tap0, ntp):
            i = dg_ctr[0] % NDG
            dg_ctr[0] += 1
            c0 = woff + chunk * ntap + tap0
            E("pool", lambda e: e.tensor_tensor(
                out=dg[:, i, 0:ntp, :], in0=identf[:].unsqueeze(1).broadcast_to([128, ntp, 128]),
                in1=pv[:, l, c0:c0 + ntp].unsqueeze(2).broadcast_to([128, ntp, 128]), op=ALU.mult),
              reads=[t_const, t_pv], writes=[t_dg[i]])
            return i

        def conv_pe(l, pout, t_pout, ci, j, woff, ntap, chunk):
            done = 0
            while done < ntap:
                ntp = min(DGT, ntap - done)
                di = build_diag(l, woff, ntap, chunk, done, ntp)
                for k in range(ntp):
                    kk = done + k
                    mm(pout[:, :], dg[:, di, k, :], cin[:, ci, j, kk:kk + T], kk == 0, kk == ntap - 1,
                       [t_dg[di], t_cin[ci]], [t_pout])
                done += ntp

        def post_sq(pi):
            E("act", lambda e: e.activation(out=junk[:, 0:512], in_=P[pi][:, :], func=AF.Square,
                                            accum_out=sm2[:, pi:pi + 1]), reads=[t_P[pi]], writes=[t_junk, t_sm2])

        def post_norm_all(l, gi):
            E("dve", lambda e: e.tensor_tensor(out=sm[:, 168:172], in0=sm2[:, 0:4], in1=sm2[:, 4:8], op=ALU.add),
              reads=[t_sm2], writes=[t_sm])
            E("act", lambda e: e.activation(out=sm[:, 172:176], in_=sm[:, 168:172], func=AF.Sqrt, scale=1.0 / D, bias=EPS),
              reads=[t_sm], writes=[t_sm])
            E("dve", lambda e: e.reciprocal(out=sm[:, 176:180], in_=sm[:, 172:176]), reads=[t_sm], writes=[t_sm])
            for tt in range(TT):
                for half in range(2):
                    pp, tp = P[half * 4 + tt], t_P[half * 4 + tt]
                    ti = next_tmp()
                    E("dve", lambda e: e.scalar_tensor_tensor(
                        out=tmpf[:, ti, :], in0=pp[:, :], scalar=sm[:, 176 + tt:177 + tt],
                        in1=gp[:, l, gi, half * 512:(half + 1) * 512], op0=ALU.mult, op1=ALU.mult),
                      reads=[tp, t_sm, t_gp[l][gi]], writes=[t_tmpf[ti]])
                    E("dve", lambda e: e.tensor_tensor(out=xb[:, tt, half * 512:(half + 1) * 512],
                                                        in0=xb[:, tt, half * 512:(half + 1) * 512],
                                                        in1=tmpf[:, ti, :], op=ALU.add),
                      reads=[t_xb[tt], t_tmpf[ti]], writes=[t_xb[tt]])

        def chk(ph):
            if ph >= stop:
                raise _Stop()

        def block_layer(l):
            chk(1)
            prenorm(l, 0)
            chk(2)
            si = stream.take()
            for tt in range(TT):
                for kc in range(KC):
                    mm(P[4][:, tt * 8:(tt + 1) * 8], u[:, kc, tt * 128:(tt + 1) * 128], wsl[:, si, kc, 0:8],
                       kc == 0, kc == KC - 1, [t_u[kc], t_wsl[si]], [t_P[4]])
            stream.release()
            G = sm[:, 32:64]
            E("dve", lambda e: e.tensor_tensor(out=G, in0=P[4][:, 0:32], in1=gb[:, l, :], op=ALU.add),
              reads=[t_P[4], t_gb], writes=[t_sm])
            Gv = G.rearrange("p (t g) -> p t g", g=8)
            LF = sm[:, 64:80].rearrange("p (t h) -> p t h", h=4)
            E("act", lambda e: e.activation(out=LF, in_=Gv[:, :, 4:8], func=AF.Exp, scale=-1.0),
              reads=[t_sm], writes=[t_sm])
            E("act", lambda e: e.activation(out=LF, in_=LF, func=AF.Ln, bias=1.0, scale=1.0),
              reads=[t_sm], writes=[t_sm])
            mm(P[5][:, 0:16], mask[:], sm[:, 64:80], True, True, [t_const, t_sm], [t_P[5]])
            mm(P[5][:, 16:32], onesf[:], sm[:, 64:80], True, True, [t_const, t_sm], [t_P[5]])
            A = sm[:, 80:96]
            Ee = sm[:, 96:112]
            EL = sm[:, 112:128]
            E("dve", lambda e: e.tensor_tensor(out=A.rearrange("p (t h) -> p t h", h=4), in0=Gv[:, :, 0:4],
                                               in1=P[5][:, 0:16].rearrange("p (t h) -> p t h", h=4), op=ALU.add),
              reads=[t_sm, t_P[5]], writes=[t_sm])
            E("act", lambda e: e.activation(out=A, in_=A, func=AF.Exp, bias=sm[:, 128:129], scale=1.0),
              reads=[t_sm], writes=[t_sm])
            E("act", lambda e: e.activation(out=sm[:, 96:128], in_=P[5][:, 0:32], func=AF.Exp, scale=-1.0),
              reads=[t_P[5]], writes=[t_sm])
            E("dve", lambda e: e.tensor_copy(out=vp[:, :, :, DH:DH + 2],
                                             in_=A.rearrange("p (t h o) -> p t h o", h=4, o=1).broadcast_to([128, TT, H, 2])),
              reads=[t_sm], writes=t_vp)

            chk(3)
            pend = None
            for sj in range(4):
                si = stream.take()
                ci = cin_ctr[0] % NCIN
                cin_ctr[0] += 1
                E("dve", lambda e: e.tensor_copy(out=cin[:, ci, :, 0:3], in_=hq[:, l, sj * 4:(sj + 1) * 4, :]),
                  reads=[t_hq[l]], writes=[t_cin[ci]])
                for j in range(4):
                    c = sj * 4 + j
                    pa = c % 2
                    for kc in range(KC):
                        mm(P[pa][:, :], wsl[:, si, kc, j * 128:(j + 1) * 128], u[:, kc, :], kc == 0, kc == KC - 1,
                           [t_wsl[si], t_u[kc]], [t_P[pa]])
                    E("act", lambda e: e.activation(out=cin[:, ci, j, 3:3 + T], in_=P[pa][:, :], func=AF.Copy),
                      reads=[t_P[pa]], writes=[t_cin[ci]])
                stream.release()
                E("dve", lambda e: e.tensor_copy(out=hq[:, l, sj * 4:(sj + 1) * 4, :], in_=cin[:, ci, :, T:T + 3]),
                  reads=[t_cin[ci]], writes=[t_hq[l]])
                for j in range(4):
                    c = sj * 4 + j
                    pc = 2 + c % 2
                    conv_pe(l, P[pc], t_P[pc], ci, j, _off["qkw"], 4, c)
                    E("act", lambda e: e.activation(out=R1[:, c, :], in_=P[pc][:, :], func=AF.Silu,
                                                    bias=pv[:, l, _off["qkb"] + c:_off["qkb"] + c + 1], scale=1.0),
                      reads=[t_P[pc], t_pv], writes=[t_R1[c]])

            chk(4)
            cf = {}

            def cf_proj(c):
                j = c % 4
                if j == 0:
                    cf["sa"] = stream.take()
                    cf["sg"] = stream.take()
                sa, sg = cf["sa"], cf["sg"]
                ci = cin_ctr[0] % NCIN
                cin_ctr[0] += 1
                cf[("ci", c)] = ci
                pa_, pg_ = (0, 1) if c % 2 == 0 else (4, 5)
                E("dve", lambda e: e.tensor_copy(out=cin[:, ci, 0, 0:30], in_=hg[:, l, c, :]),
                  reads=[t_hg[l]], writes=[t_cin[ci]])
                for kc in range(KC):
                    mm(P[pa_][:, :], wsl[:, sa, kc, j * 128:(j + 1) * 128], u[:, kc, :], kc == 0, kc == KC - 1,
                       [t_wsl[sa], t_u[kc]], [t_P[pa_]])
                for kc in range(KC):
                    mm(P[pg_][:, :], wsl[:, sg, kc, j * 128:(j + 1) * 128], u[:, kc, :], kc == 0, kc == KC - 1,
                       [t_wsl[sg], t_u[kc]], [t_P[pg_]])
                if j == 3:
                    stream.release(2)
                ti = next_tmp()
                E("act", lambda e: e.activation(out=tmpf[:, ti, :], in_=P[pg_][:, :], func=AF.Sigmoid),
                  reads=[t_P[pg_]], writes=[t_tmpf[ti]])
                E("dve", lambda e: e.tensor_tensor(out=cin[:, ci, 0, 30:30 + T], in0=P[pa_][:, :],
                                                   in1=tmpf[:, ti, :], op=ALU.mult),
                  reads=[t_P[pa_], t_tmpf[ti]], writes=[t_cin[ci]])
                E("dve", lambda e: e.tensor_copy(out=hg[:, l, c, :], in_=cin[:, ci, 0, T:T + 30]),
                  reads=[t_cin[ci]], writes=[t_hg[l]])

            def cf_conv(c):
                ci = cf[("ci", c)]
                pc = 2 + c % 2
                conv_pe(l, P[pc], t_P[pc], ci, 0, _off["cvw"], 31, c)
                bias_ = pv[:, l, _off["cvb"] + c:_off["cvb"] + c + 1]
                E("act", lambda e: e.activation(out=my[:, c, :], in_=P[pc][:, :], func=AF.Identity, bias=bias_, scale=1.0),
                  reads=[t_P[pc], t_pv], writes=[t_my[c]])
                t2 = next_tmp()
                cf[("t2", c)] = t2
                ysq = tmpf[:, t2, :].bitcast(BF16)[:, 0:T]
                E("act", lambda e: e.activation(out=ysq, in_=P[pc][:, :], func=AF.Square, bias=bias_, scale=1.0),
                  reads=[t_P[pc], t_pv], writes=[t_tmpf[t2]])

            def cf_stats(c):
                t2 = cf[("t2", c)]
                ysq = tmpf[:, t2, :].bitcast(BF16)[:, 0:T]
                mm(P[6][:, :], onesln[:], my[:, c, :], c == 0, c == KC - 1, [t_const, t_my[c]], [t_P[6]])
                mm(P[7][:, :], onesln[:], ysq, c == 0, c == KC - 1, [t_const, t_tmpf[t2]], [t_P[7]])

            cf_proj(0)
            for c in range(KC):
                if c + 1 < KC:
                    cf_proj(c + 1)
                cf_conv(c)
                if c >= 1:
                    cf_stats(c - 1)
            cf_stats(KC - 1)
            pass
            E("act", lambda e: e.activation(out=stat[:, 0, :], in_=P[6][:, :], func=AF.Copy),
              reads=[t_P[6]], writes=[t_stat])
            ti = next_tmp()
            E("act", lambda e: e.activation(out=tmpf[:, ti, :], in_=P[6][:, :], func=AF.Square),
              reads=[t_P[6]], writes=[t_tmpf[ti]])
            E("dve", lambda e: e.tensor_tensor(out=tmpf[:, ti, :], in0=P[7][:, :], in1=tmpf[:, ti, :], op=ALU.subtract),
              reads=[t_P[7], t_tmpf[ti]], writes=[t_tmpf[ti]])
            E("act", lambda e: e.activation(out=tmpf[:, ti, :], in_=tmpf[:, ti, :], func=AF.Sqrt, bias=EPS, scale=1.0),
              reads=[t_tmpf[ti]], writes=[t_tmpf[ti]])
            E("dve", lambda e: e.reciprocal(out=stat[:, 1, :], in_=tmpf[:, ti, :]),
              reads=[t_tmpf[ti]], writes=[t_stat])
            for c in range(KC):
                ti = next_tmp()
                E("dve", lambda e: e.tensor_tensor(out=tmpf[:, ti, :], in0=my[:, c, :], in1=stat[:, 0, :], op=ALU.subtract),
                  reads=[t_my[c], t_stat], writes=[t_tmpf[ti]])
                E("dve", lambda e: e.tensor_tensor(out=tmpf[:, ti, :], in0=tmpf[:, ti, :], in1=stat[:, 1, :], op=ALU.mult),
                  reads=[t_tmpf[ti], t_stat], writes=[t_tmpf[ti]])
                E("act", lambda e: e.activation(out=my[:, c, :], in_=tmpf[:, ti, :], func=AF.Silu,
                                                scale=pv[:, l, _off["lng"] + c:_off["lng"] + c + 1],
                                                bias=pv[:, l, _off["lnb"] + c:_off["lnb"] + c + 1]),
                  reads=[t_tmpf[ti], t_pv], writes=[t_my[c]])

            chk(5)
            for sj in range(4):
                si = stream.take()
                for tt in range(TT):
                    pa = (0, 1, 4, 5)[(sj * TT + tt) % 4]
                    for kc in range(KC):
                        mm(P[pa][:, :], u[:, kc, tt * 128:(tt + 1) * 128], wsl[:, si, kc, :], kc == 0, kc == KC - 1,
                           [t_u[kc], t_wsl[si]], [t_P[pa]])
                    if sj < 2:
                        for hh in range(2):
                            h = sj * 2 + hh
                            E("act" if hh else "dve", (lambda e: e.activation(
                                out=vp[:, tt, h, 0:DH], in_=P[pa][:, hh * DH:(hh + 1) * DH], func=AF.Identity,
                                scale=sm[:, 80 + tt * 4 + h:81 + tt * 4 + h])) if hh else (lambda e: e.tensor_scalar(
                                    out=vp[:, tt, h, 0:DH], in0=P[pa][:, hh * DH:(hh + 1) * DH],
                                    scalar1=sm[:, 80 + tt * 4 + h:81 + tt * 4 + h], scalar2=None, op0=ALU.mult)),
                              reads=[t_P[pa], t_sm], writes=[t_vp[tt]])
                    else:
                        E("act", lambda e: e.activation(out=go[:, tt, (sj - 2) * 512:(sj - 1) * 512], in_=P[pa][:, :],
                                                        func=AF.Sigmoid), reads=[t_P[pa]], writes=[t_go[tt]])
                stream.release()

            chk(6)
            ml = {}

            def ktf(tt, h, dc):
                return R1[:, 16 + 2 * tt + h // 2, (h % 2) * 256 + dc * 128:(h % 2) * 256 + (dc + 1) * 128]

            def st_T(tt):
                ts = slice(tt * 128, (tt + 1) * 128)
                for c in range(KC):
                    mm(P[c // 4][:, (c % 4) * 128:(c % 4 + 1) * 128], R1[:, 8 + c, ts], ident[:], True, True,
                       [t_R1[8 + c], t_const], [t_P[c // 4]])
                for a_ in range(2):
                    E("act" if a_ else "dve", (lambda e: e.activation(out=R1[:, 16 + 2 * tt + a_, :], in_=P[a_][:, :], func=AF.Copy))
                      if a_ else (lambda e: e.tensor_copy(out=R1[:, 16 + 2 * tt + a_, :], in_=P[a_][:, :])),
                      reads=[t_P[a_]], writes=[t_R1[16 + 2 * tt + a_]])
                for h in range(H):
                    for dc in range(2):
                        mm(P[2][:, h * 128:(h + 1) * 128], R1[:, 8 + 2 * h + dc, ts], R1[:, 2 * h + dc, ts],
                           dc == 0, dc == 1, [t_R1[8 + 2 * h + dc], t_R1[2 * h + dc]], [t_P[2]])
                Sts = []
                for h in range(H):
                    ti = next_tmp()
                    St = tmpf[:, ti, :].bitcast(BF16)[:, 0:128]
                    Sts.append((St, ti))
                    E("dve", lambda e: e.tensor_tensor(out=St, in0=P[2][:, h * 128:(h + 1) * 128], in1=mask[:], op=ALU.mult),
                      reads=[t_P[2], t_const], writes=[t_tmpf[ti]])
                ml[("Sts", tt)] = Sts

            def st_N(tt):
                ts = slice(tt * 128, (tt + 1) * 128)
                Sts = ml[("Sts", tt)]
                for h in range(H):
                    St, ti = Sts[h]
                    pn = P[4 + h // 2]
                    tpn = t_P[4 + h // 2]
                    no = (h % 2) * DH
                    for dc in range(2):
                        mm(pn[:, no:no + DH], R1[:, 2 * h + dc, ts], Cb[:, l, h, dc * DH:(dc + 1) * DH], dc == 0, False,
                           [t_R1[2 * h + dc], t_Cb[l][h]], [tpn])
                    mm(pn[:, no:no + DH], St, vp[:, tt, h, 0:DH], False, True, [t_tmpf[ti], t_vp[tt]], [tpn])
                for h in range(H):
                    St, ti = Sts[h]
                    for dc in range(2):
                        mm(P[3][:, h * 2:h * 2 + 2], R1[:, 2 * h + dc, ts], nbp(l, h, dc), dc == 0, False, [t_R1[2 * h + dc], t_nb[l]], [t_P[3]])
                    mm(P[3][:, h * 2:h * 2 + 2], St, vp[:, tt, h, DH:DH + 2], False, True, [t_tmpf[ti], t_vp[tt]], [t_P[3]])
                E("act", lambda e: e.activation(out=numsb[:, 0:512], in_=P[4][:, :], func=AF.Copy), reads=[t_P[4]], writes=[t_numsb])
                E("dve", lambda e: e.tensor_copy(out=numsb[:, 512:1024], in_=P[5][:, :]), reads=[t_P[5]], writes=[t_numsb])
                E("dve", lambda e: e.tensor_copy(out=sm[:, 184:192], in_=P[3][:, 0:8]), reads=[t_P[3]], writes=[t_sm])

            def st_U(tt):
                for h in range(H):
                    for dc in range(2):
                        mm(P[3][:, 16 + h * 4 + dc * 2:18 + h * 4 + dc * 2], ktf(tt, h, dc), vp[:, tt, h, DH:DH + 2], True, True,
                           [t_R1[16 + 2 * tt + h // 2], t_vp[tt]], [t_P[3]])
                elp4 = elp[:, l, :] if tt == 0 else sm[:, 112 + (tt - 1) * 4:112 + tt * 4]
                t_elprev = t_elp[l] if tt == 0 else t_sm
                el4 = sm[:, 112 + tt * 4:116 + tt * 4]
                for h in range(H):
                    idx = tt * 4 + h
                    pu = P[6 + h % 2]
                    tpu = t_P[6 + h % 2]
                    for dc in range(2):
                        mm(pu[:, dc * DH:(dc + 1) * DH], ktf(tt, h, dc), vp[:, tt, h, 0:DH], True, True,
                           [t_R1[16 + 2 * tt + h // 2], t_vp[tt]], [tpu])
                    E("dve", lambda e: e.scalar_tensor_tensor(out=Tm[:, l, h, :], in0=Tm[:, l, h, :], scalar=elp4[:, h:h + 1],
                                                              in1=pu[:, :], op0=ALU.mult, op1=ALU.add),
                      reads=[t_Tm[l][h], t_elprev, tpu], writes=[t_Tm[l][h]])
                    E("act", lambda e: e.activation(out=Cb[:, l, h, :], in_=Tm[:, l, h, :], func=AF.Identity,
                                                    scale=sm[:, 112 + idx:113 + idx]),
                      reads=[t_Tm[l][h], t_sm], writes=[t_Cb[l][h]])
                nsl = P[3][:, 16:32].rearrange("p (h d two) -> p h d two", h=4, two=2)[:, :, :, 0]
                E("dve", lambda e: e.tensor_tensor(out=nm[:, l], in0=nm[:, l], in1=elp4.unsqueeze(2).broadcast_to([128, H, 2]),
                                                   op=ALU.mult), reads=[t_nm[l], t_elprev], writes=[t_nm[l]])
                E("dve", lambda e: e.tensor_tensor(out=nm[:, l], in0=nm[:, l], in1=nsl, op=ALU.add),
                  reads=[t_nm[l], t_P[3]], writes=[t_nm[l]])
                E("dve", lambda e: e.tensor_tensor(out=nbw[:, l], in0=nm[:, l].unsqueeze(3).broadcast_to([128, H, 2, 2]),
                                                   in1=el4.unsqueeze(2).unsqueeze(3).broadcast_to([128, H, 2, 2]), op=ALU.mult),
                  reads=[t_nm[l], t_sm], writes=[t_nb[l]])

            def st_F1(tt):
                e4 = sm[:, 96 + tt * 4:100 + tt * 4]
                den = sm[:, 184:192].rearrange("p (h two) -> p h two", two=2)[:, :, 0]
                W0 = sm[:, 136:140]
                W1 = sm[:, 140:144]
                W2 = sm[:, 144:148]
                E("dve", lambda e: e.tensor_tensor(out=W0, in0=den, in1=e4, op=ALU.mult), reads=[t_sm], writes=[t_sm])
                E("act", lambda e: e.activation(out=W0, in_=W0, func=AF.Abs), reads=[t_sm], writes=[t_sm])
                E("dve", lambda e: e.tensor_scalar_max(out=W0, in0=W0, scalar1=1.0), reads=[t_sm], writes=[t_sm])
                E("dve", lambda e: e.reciprocal(out=W0, in_=W0), reads=[t_sm], writes=[t_sm])
                E("dve", lambda e: e.tensor_tensor(out=W0, in0=W0, in1=e4, op=ALU.mult), reads=[t_sm], writes=[t_sm])
                for h in range(H):
                    E("act", lambda e: e.activation(out=junk[:, 0:DH], in_=numsb[:, h * DH:(h + 1) * DH], func=AF.Square,
                                                    accum_out=sm[:, 148 + h:149 + h]),
                      reads=[t_numsb], writes=[t_junk, t_sm])
                SS = sm[:, 148:152]
                E("dve", lambda e: e.tensor_tensor(out=W1, in0=W0, in1=W0, op=ALU.mult), reads=[t_sm], writes=[t_sm])
                E("dve", lambda e: e.tensor_tensor(out=W1, in0=W1, in1=SS, op=ALU.mult), reads=[t_sm], writes=[t_sm])
                E("act", lambda e: e.activation(out=W1, in_=W1, func=AF.Sqrt, scale=1.0 / DH, bias=EPS), reads=[t_sm], writes=[t_sm])
                E("dve", lambda e: e.reciprocal(out=W1, in_=W1), reads=[t_sm], writes=[t_sm])
                E("dve", lambda e: e.tensor_tensor(out=W2, in0=W1, in1=W0, op=ALU.mult), reads=[t_sm], writes=[t_sm])
                hb = tt % 2
                for h in range(H):
                    E("dve", lambda e: e.scalar_tensor_tensor(out=hfin[:, hb, h * DH:(h + 1) * DH], in0=numsb[:, h * DH:(h + 1) * DH],
                                                              scalar=sm[:, 144 + h:145 + h], in1=go[:, tt, h * DH:(h + 1) * DH],
                                                              op0=ALU.mult, op1=ALU.mult),
                      reads=[t_numsb, t_sm, t_go[tt]], writes=[t_hfin[hb]])

            def st_F2(tt):
                ts = slice(tt * 128, (tt + 1) * 128)
                hb = tt % 2
                for c in range(KC):
                    mm(P[c // 4][:, (c % 4) * 128:(c % 4 + 1) * 128], hfin[:, hb, c * 128:(c + 1) * 128], ident[:], True, True,
                       [t_hfin[hb], t_const], [t_P[c // 4]])
                for c in range(KC):
                    src = P[c // 4][:, (c % 4) * 128:(c % 4 + 1) * 128]
                    E("dve" if c % 2 else "act", (lambda e: e.tensor_scalar(
                        out=u[:, c, ts], in0=src,
                        scalar1=pv[:, l, _off["mlg"] + c:_off["mlg"] + c + 1], scalar2=None, op0=ALU.mult)) if c % 2 else (
                        lambda e: e.activation(out=u[:, c, ts], in_=src, func=AF.Identity,
                                               scale=pv[:, l, _off["mlg"] + c:_off["mlg"] + c + 1])),
                      reads=[t_P[c // 4], t_pv], writes=[t_u[c]])

            st_T(0)
            for tt in range(TT):
                st_N(tt)
                st_U(tt)
                if tt + 1 < TT:
                    st_T(tt + 1)
                st_F1(tt)
                if tt >= 1:
                    st_F2(tt - 1)
            st_F2(TT - 1)
            E("dve", lambda e: e.tensor_copy(out=elp[:, l, :], in_=sm[:, 124:128]), reads=[t_sm], writes=[t_elp[l]])

            chk(7)
            for nh in range(2):
                sis = [stream.take(), stream.take()]
                for tt in range(TT):
                    pi = nh * 4 + tt
                    for kg in range(2):
                        si = sis[kg]
                        for kc in range(KC):
                            src = (u[:, kc, tt * 128:(tt + 1) * 128], t_u[kc]) if kg == 0 else \
                                (my[:, kc, tt * 128:(tt + 1) * 128], t_my[kc])
                            mm(P[pi][:, :], src[0], wsl[:, si, kc, :], kg == 0 and kc == 0, kg == 1 and kc == KC - 1,
                               [src[1], t_wsl[si]], [t_P[pi]])
                    post_sq(pi)
                stream.release(2)
            post_norm_all(l, 0)

            chk(8)
            prenorm(l, 1)
            pend = []
            for sj in range(6):
                sa = stream.take()
                sg = stream.take()
                nch = 4 if sj < 5 else 2
                ca = cin_ctr[0] % NCIN
                cin_ctr[0] += 1
                cg = cin_ctr[0] % NCIN
                cin_ctr[0] += 1
                c0 = sj * 4
                E("dve", lambda e: e.tensor_copy(out=cin[:, ca, 0:nch, 0:2], in_=hf[:, l, c0:c0 + nch, :]),
                  reads=[t_hf[l]], writes=[t_cin[ca]])
                E("dve", lambda e: e.tensor_copy(out=cin[:, cg, 0:nch, 0:2], in_=hf[:, l, NFC + c0:NFC + c0 + nch, :]),
                  reads=[t_hf[l]], writes=[t_cin[cg]])
                for j in range(nch):
                    for (sw, cb, pp) in ((sa, ca, 0 if j % 2 == 0 else 6), (sg, cg, 1 if j % 2 == 0 else 7)):
                        for kc in range(KC):
                            mm(P[pp][:, :], wsl[:, sw, kc, j * 128:(j + 1) * 128], u[:, kc, :], kc == 0, kc == KC - 1,
                               [t_wsl[sw], t_u[kc]], [t_P[pp]])
                        E("act" if pp % 2 else "dve", (lambda e: e.activation(out=cin[:, cb, j, 2:2 + T], in_=P[pp][:, :], func=AF.Copy))
                          if pp % 2 else (lambda e: e.tensor_copy(out=cin[:, cb, j, 2:2 + T], in_=P[pp][:, :])),
                          reads=[t_P[pp]], writes=[t_cin[cb]])
                stream.release(2)
                E("dve", lambda e: e.tensor_copy(out=hf[:, l, c0:c0 + nch, :], in_=cin[:, ca, 0:nch, T:T + 2]),
                  reads=[t_cin[ca]], writes=[t_hf[l]])
                E("dve", lambda e: e.tensor_copy(out=hf[:, l, NFC + c0:NFC + c0 + nch, :], in_=cin[:, cg, 0:nch, T:T + 2]),
                  reads=[t_cin[cg]], writes=[t_hf[l]])
                for j in range(nch):
                    c = c0 + j
                    qa, qg = (2, 3) if c % 2 == 0 else (4, 5)
                    conv_pe(l, P[qa], t_P[qa], ca, j, _off["fw"], 3, c)
                    conv_pe(l, P[qg], t_P[qg], cg, j, _off["fw"], 3, NFC + c)
                    ti = next_tmp()
                    E("act", lambda e: e.activation(out=tmpf[:, ti, :], in_=P[qg][:, :], func=AF.Silu,
                                                    bias=pv[:, l, _off["fb"] + NFC + c:_off["fb"] + NFC + c + 1], scale=1.0),
                      reads=[t_P[qg], t_pv], writes=[t_tmpf[ti]])
                    E("dve", lambda e: e.scalar_tensor_tensor(out=R1[:, c, :], in0=P[qa][:, :],
                                                              scalar=pv[:, l, _off["fb"] + c:_off["fb"] + c + 1],
                                                              in1=tmpf[:, ti, :], op0=ALU.add, op1=ALU.mult),
                      reads=[t_P[qa], t_pv, t_tmpf[ti]], writes=[t_R1[c]])
            for nh in range(2):
                sis = [stream.take(), stream.take(), stream.take()]
                for tt in range(TT):
                    pi = nh * 4 + tt
                    for kg in range(3):
                        nk = 8 if kg < 2 else 6
                        si = sis[kg]
                        for kc in range(nk):
                            cidx = kg * 8 + kc
                            mm(P[pi][:, :], R1[:, cidx, tt * 128:(tt + 1) * 128], wsl[:, si, kc, :],
                               cidx == 0, cidx == NFC - 1, [t_R1[cidx], t_wsl[si]], [t_P[pi]])
                    post_sq(pi)
                stream.release(3)
            post_norm_all(l, 1)

        def nbp(l, h, dc):
            return nbw[:, l, h, dc, :]

        def nb2(l, h):
            return nbw[:, l, h, :, :]

        E("dve", lambda e: e.memset(sm[:, 128:129], float(-np.log(16.0))), writes=[t_sm])

        for s in range(nseq):
            queue_setup_weights(s)
            for b in range(nblk):
                for l in range(nlayers):
                    queue_layer_weights(l)
        for s in range(nseq):
            setup_seq(s)
            for b in range(nblk):
                for tt in range(TT):
                    fw.dma("sp", xb[:, tt, :], x_d[s, b * T + tt * 128: b * T + (tt + 1) * 128, :], writes=[t_xb[tt]])
                try:
                    for l in range(nlayers):
                        block_layer(l)
                except _Stop:
                    pass
                for tt in range(TT):
                    out_toks.append(fw.dma("sp", y_d[s, b * T + tt * 128: b * T + (tt + 1) * 128, :], xb[:, tt, :],
                                           reads=[t_xb[tt]]))
        fw.finish(out_toks)
        print("instructions:", fw.ninst, fw.per, "sems:", fw.nsems, "slabs:", len(stream.items))
    return nc


def _fm(v):
    return np.ascontiguousarray(v.reshape(-1, 128).T)


def _fmw(w):
    K, C = w.shape
    return np.ascontiguousarray(w.T.reshape(C // 128, 128, K).transpose(1, 0, 2).reshape(128, -1))


def pack_params(inp, layers):
    pvs, bcs = [], []
    for l in layers:
        ab = inp["ada_b"][l]
        cols = [
            _fm(inp["mix_pre_g"][l]), _fmw(inp["qk_conv_w"][l]), _fm(inp["qk_conv_b"][l]),
            _fmw(inp["cv_dw_w"][l]), _fm(inp["cv_dw_b"][l]), _fm(inp["cv_ln_g"][l]), _fm(inp["cv_ln_b"][l]),
            _fm(inp["ffn_pre_g"][l]), _fmw(inp["ffn_conv_w"][l]), _fm(inp["ffn_conv_b"][l]),
            _fm(inp["ml_norm_g"][l]),
            _fm(ab[0:1024]), _fm(ab[1024:2048]), _fm(ab[3072:4096]), _fm(ab[4096:5120]),
        ]
        pvs.append(np.concatenate(cols, axis=1))
        gbias = np.concatenate([inp["igate_b"][l], inp["fgate_b"][l]])
        bcs.append(np.concatenate([inp["mix_post_g"][l], inp["ffn_post_g"][l], ab[2048:3072], ab[5120:6144],
                                   np.tile(gbias, 4)]))
    pvec = np.ascontiguousarray(np.stack(pvs)).astype(np.float32)
    bcv = np.ascontiguousarray(np.stack(bcs)).astype(np.float32)
    assert pvec.shape[2] == NPV and bcv.shape[1] == NBC
    return pvec, bcv


def make_in_maps(inp, ncores, nseq, nblk, layers):
    pvec, bcv = pack_params(inp, layers)
    S = nblk * T
    L = list(layers)
    shared = {
        "ada_w": np.ascontiguousarray(inp["ada_w"][L]), "w_in": np.ascontiguousarray(inp["w_in"][L]),
        "w_out": np.ascontiguousarray(inp["w_out"][L]), "ffn_up": np.ascontiguousarray(inp["ffn_up"][L]),
        "ffn_down": np.ascontiguousarray(inp["ffn_down"][L]), "pvec": pvec, "bcv": bcv,
    }
    maps = []
    for c in range(ncores):
        xs = np.ascontiguousarray(inp["x"][c * nseq:(c + 1) * nseq, :S, :])
        cc = inp["c"][c * nseq:(c + 1) * nseq]
        cT = np.ascontiguousarray(cc.reshape(nseq, KC, 128).transpose(2, 0, 1).reshape(128, nseq * KC))
        m = dict(shared)
        m["x"] = xs
        m["cT"] = cT
        maps.append(m)
    return maps


_NC_CACHE = {}


def kernel(**inputs):
    inp = {k: np.asarray(v, dtype=np.float32) for k, v in inputs.items()}
    ncores, nseq, nblk = 8, 2, 4
    key = (nseq, nblk, 2)
    if key not in _NC_CACHE:
        _NC_CACHE[key] = build(nseq, nblk, 2)
    nc = _NC_CACHE[key]
    maps = make_in_maps(inp, ncores, nseq, nblk, (0, 1))
    res = run_bass_kernel_spmd(nc, maps, core_ids=list(range(ncores)))
    out = np.concatenate([np.asarray(r["y"]).reshape(nseq, SEQ, D) for r in res.results], axis=0)
    return out.astype(np.float32)
```

```python
import numpy as np
from contextlib import ExitStack
import concourse.bass as bass
import concourse.mybir as mybir
from concourse.bass_utils import run_bass_kernel_spmd

F32 = mybir.dt.float32
BF16 = mybir.dt.bfloat16
AF = mybir.ActivationFunctionType
ALU = mybir.AluOpType

D = 1024
KC = 8
T = 512
TT = 4
H = 4
DH = 256
SEQ = 2048
N_IN = 6152
DFF = 2816
NFC = 22
EPS = 1e-6

_off = {}
_n = 0
for _name, _w in (("pre1", 8), ("qkw", 64), ("qkb", 16), ("cvw", 248), ("cvb", 8), ("lng", 8),
                  ("lnb", 8), ("pre2", 8), ("fw", 132), ("fb", 44), ("mlg", 8), ("adab", 32)):
    _off[_name] = _n
    _n += _w
NPV = _n
BC_POST1, BC_POST2, BC_ADAG1, BC_ADAG2, BC_GB = 0, 1024, 2048, 3072, 4096
NBC = 4096 + 32


class Tk:
    __slots__ = ("name", "w", "r", "dsem", "dcount", "excl")

    def __init__(self, name, excl=False):
        self.name = name
        self.excl = excl
        self.w = None
        self.r = {}
        self.dsem = None
        self.dcount = 0


class Eng:
    def __init__(self, fw, name, handle):
        self.fw = fw
        self.name = name
        self.h = handle
        self.sem = None
        self.count = 0
        self.waited = {}
        self.nsem = 0

    def newsem(self):
        self.sem = self.fw.alloc_sem(f"{self.name}_e{self.nsem}")
        self.nsem += 1
        self.count = 0


class FW:
    EPOCH = 12000
    STRICT = True

    def __init__(self, nc, stack):
        self.nc = nc
        self.stack = stack
        self.nsems = 0
        self.engs = {}
        for name, h in (("pe", nc.tensor), ("act", nc.scalar), ("dve", nc.vector),
                        ("pool", nc.gpsimd), ("sp", nc.sync)):
            e = Eng(self, name, h)
            e.newsem()
            self.engs[name] = e
        self.ninst = 0
        self.per = {k: 0 for k in self.engs}

    def alloc_sem(self, name):
        self.nsems += 1
        return self.stack.enter_context(self.nc.semaphore(name))

    def _wait(self, eng, tok, kind):
        sem, val, en = tok
        if en == eng.name and kind != "raw" and (eng.name == "pe" or not self.STRICT):
            return
        if eng.waited.get(sem, 0) >= val:
            return
        eng.h.wait_ge(sem, val)
        eng.waited[sem] = val
        self.ninst += 1

    def _deps(self, eng, reads, writes):
        for t in reads:
            if t.w is not None:
                self._wait(eng, t.w, "raw")
            if t.excl:
                for sem, (val, en) in t.r.items():
                    if en != eng.name:
                        self._wait(eng, (sem, val, en), "raw")
        for t in writes:
            if t.w is not None:
                self._wait(eng, t.w, "waw")
            for sem, (val, en) in t.r.items():
                self._wait(eng, (sem, val, en), "war")

    def op(self, engname, fn, reads=(), writes=()):
        eng = self.engs[engname]
        self._deps(eng, reads, writes)
        inst = fn(eng.h)
        if eng.count >= self.EPOCH:
            eng.newsem()
        inst.then_inc(eng.sem, 1)
        eng.count += 1
        self.ninst += 1
        self.per[engname] += 1
        tok = (eng.sem, eng.count, eng.name)
        for t in reads:
            t.r[eng.sem] = (eng.count, eng.name)
        for t in writes:
            t.w = tok
            t.r = {}
        return inst

    def dma(self, qname, out, in_, reads=(), writes=(), **kw):
        eng = self.engs[qname]
        for t in reads:
            if t.w is not None:
                self._wait(eng, t.w, "raw")
        for t in writes:
            if t.w is not None:
                self._wait(eng, t.w, "raw")
            for sem, (val, en) in t.r.items():
                self._wait(eng, (sem, val, en), "raw")
        owner = (list(writes) + list(reads))[0]
        if owner.dsem is None:
            owner.dsem = self.alloc_sem("d_" + owner.name)
        inst = eng.h.dma_start(out=out, in_=in_, **kw)
        inst.then_inc(owner.dsem, 16)
        owner.dcount += 16
        self.ninst += 1
        tok = (owner.dsem, owner.dcount, "dma")
        for t in reads:
            t.r[owner.dsem] = (owner.dcount, "dma")
        for t in writes:
            t.w = tok
            t.r = {}
        return tok

    def finish(self, toks):
        eng = self.engs["sp"]
        for tok in toks:
            self._wait(eng, tok, "raw")


class _Stop(Exception):
    pass


def build(nseq=2, nblk=4, nlayers=2, dbg=(), stop=99):
    S = nblk * T
    nc = bass.Bass("TRN2", target_bir_lowering=False)
    dr = lambda name, shape, kind="ExternalInput": nc.dram_tensor(name, shape, F32, kind=kind).ap()
    x_d = dr("x", [nseq, S, D])
    c_d = dr("cT", [128, nseq * KC])
    ada_d = dr("ada_w", [nlayers, D, 6 * D])
    win_d = dr("w_in", [nlayers, D, N_IN])
    wout_d = dr("w_out", [nlayers, 2 * D, D])
    fup_d = dr("ffn_up", [nlayers, D, 2 * DFF])
    fdn_d = dr("ffn_down", [nlayers, DFF, D])
    pv_d = dr("pvec", [nlayers, 128, NPV])
    bc_d = dr("bcv", [nlayers, NBC])
    y_d = dr("y", [nseq, S, D], kind="ExternalOutput")
    dbg_d = {name: dr("dbg_" + name, shape, kind="ExternalOutput") for name, shape in dbg}

    st = ExitStack()
    with st:
        sb = lambda name, shape, dt=F32: st.enter_context(nc.sbuf_tensor(name, shape, dt))
        NSLOT = 4
        wsl = sb("wsl", [128, NSLOT, KC, 512], BF16)
        xb = sb("xb", [128, TT, D])
        xn = sb("xn", [128, 2, D], BF16)
        u = sb("u", [128, KC, T], BF16)
        R1 = sb("R1", [128, 24, T], BF16)
        vp = sb("vp", [128, TT, H, DH + 2], BF16)
        go = sb("go", [128, TT, D], BF16)
        NCIN = 2
        CINW = 544
        cin = sb("cin", [128, NCIN, 4, CINW], BF16)
        NDG = 3
        DGT = 16
        dg = sb("dg", [128, NDG, DGT, 128], BF16)
        tmpf = sb("tmpf", [128, 4, T])
        stat = sb("stat", [128, 2, T])
        my = sb("my", [128, KC, T], BF16)
        hfin = sb("hfin", [128, 2, D], BF16)
        Tm = sb("Tm", [128, nlayers, H, 2 * DH])
        Cb = sb("Cb", [128, nlayers, H, 2 * DH], BF16)
        nm = sb("nm", [128, nlayers, H, 2])
        nbw = sb("nbw", [128, nlayers, H, 2, 2], BF16)
        elp = sb("elp", [128, nlayers, H])
        hq = sb("hq", [128, nlayers, 16, 3], BF16)
        hg = sb("hg", [128, nlayers, 8, 30], BF16)
        hf = sb("hf", [128, nlayers, 2 * NFC, 2], BF16)
        gp = sb("gp", [128, nlayers, 2, D])
        pv = sb("pv", [128, nlayers, NPV])
        gb = sb("gb", [128, nlayers, 32])
        modv = sb("modv", [128, nlayers, 4, KC])
        junk = sb("junk", [128, D], BF16)
        ident = sb("ident", [128, 128], BF16)
        identf = sb("identf", [128, 128])
        mask = sb("mask", [128, 128])
        onesf = sb("onesf", [128, 128])
        onesln = sb("onesln", [128, 128], BF16)
        cT = sb("cTs", [128, nseq * KC])
        condb = sb("condb", [128, KC, 2], BF16)
        condbc = sb("condbc", [128, KC, 128], BF16)
        sm = sb("sm", [128, 512])
        sm2 = sb("sm2", [128, 8])
        numsb = sb("numsb", [128, D])
        P = [st.enter_context(nc.psum_tensor(f"P{i}", [128, 512], F32)) for i in range(8)]
        PB = [p[:].bitcast(BF16) for p in P]

        fw = FW(nc, st)
        st.enter_context(nc.Block())

        tk = lambda n: Tk(n)
        t_wsl = [tk(f"wsl{i}") for i in range(NSLOT)]
        t_xb = [tk(f"xb{i}") for i in range(TT)]
        t_xn = [tk(f"xn{i}") for i in range(2)]
        t_u = [tk(f"u{i}") for i in range(KC)]
        t_R1 = [tk(f"R1_{i}") for i in range(24)]
        t_vp = [tk(f"vp{i}") for i in range(TT)]
        t_go = [tk(f"go{i}") for i in range(TT)]
        t_cin = [tk(f"cin{i}") for i in range(NCIN)]
        t_dg = [tk(f"dg{i}") for i in range(NDG)]
        t_tmpf = [tk(f"tmpf{i}") for i in range(4)]
        t_stat = tk("stat")
        t_my = [tk(f"my{i}") for i in range(KC)]
        t_hfin = [tk(f"hfin{i}") for i in range(2)]
        t_Tm = [[tk(f"Tm{l}_{h}") for h in range(H)] for l in range(nlayers)]
        t_Cb = [[tk(f"Cb{l}_{h}") for h in range(H)] for l in range(nlayers)]
        t_nm = [tk(f"nm{l}") for l in range(nlayers)]
        t_nb = [tk(f"nb{l}") for l in range(nlayers)]
        t_elp = [tk(f"elp{l}") for l in range(nlayers)]
        t_hq = [tk(f"hq{l}") for l in range(nlayers)]
        t_hg = [tk(f"hg{l}") for l in range(nlayers)]
        t_hf = [tk(f"hf{l}") for l in range(nlayers)]
        t_gp = [[tk(f"gp{l}_{j}") for j in range(2)] for l in range(nlayers)]
        t_pv = tk("pv")
        t_gb = tk("gb")
        t_modv = tk("modv")
        t_junk = tk("junk")
        t_const = tk("const")
        t_cT = tk("cT")
        t_cond = tk("cond")
        t_sm = tk("sm")
        t_sm2 = tk("sm2")
        t_numsb = tk("numsb")
        t_P = [Tk(f"P{i}", excl=True) for i in range(8)]
        out_toks = []

        E = fw.op

        def mm(out, lhsT, rhs, start, stop, reads, writes):
            return E("pe", lambda e: e.matmul(out, lhsT=lhsT, rhs=rhs, start=start, stop=stop),
                     reads, writes)

        def dump(name, src_ap, reads):
            if name in dbg_d:
                fw.dma("sp", dbg_d[name], src_ap, reads=reads)

        E("dve", lambda e: e.memset(identf[:], 0.0), writes=[t_const])
        E("pool", lambda e: e.affine_select(out=identf[:], in_=identf[:], pattern=[[-1, 128]],
                                            compare_op=ALU.not_equal, fill=1.0, base=0,
                                            channel_multiplier=1), reads=[t_const], writes=[t_const])
        E("dve", lambda e: e.tensor_copy(out=ident[:], in_=identf[:]), reads=[t_const], writes=[t_const])
        E("dve", lambda e: e.memset(onesf[:], 1.0), writes=[t_const])
        E("dve", lambda e: e.memset(onesln[:], 1.0 / D), writes=[t_const])
        E("dve", lambda e: e.memset(mask[:], 1.0), writes=[t_const])
        E("pool", lambda e: e.affine_select(out=mask[:], in_=mask[:], pattern=[[1, 128]],
                                            compare_op=ALU.is_ge, fill=0.0, base=0,
                                            channel_multiplier=-1), reads=[t_const], writes=[t_const])
        E("dve", lambda e: e.memset(junk[:], 0.0), writes=[t_junk])
        for l in range(nlayers):
            fw.dma("sp", pv[:, l, :], pv_d[l], writes=[t_pv])
            fw.dma("sp", gb[:, l, :], bc_d[l, BC_GB:BC_GB + 32].partition_broadcast(128), writes=[t_gb])
        fw.dma("sp", cT[:], c_d[:, :], writes=[t_cT])

        slot_ctr = [0]

        def load_slab(src2d, nkc, ncols):
            i = slot_ctr[0] % NSLOT
            slot_ctr[0] += 1
            fw.dma("pool", wsl[:, i, 0:nkc, 0:ncols],
                   src2d.rearrange("(kc p) n -> p kc n", p=128), writes=[t_wsl[i]])
            return i

        class Stream:
            def __init__(self):
                self.items = []
                self.issued = 0
                self.taken = 0
                self.released = 0
                self.slots = []

            def add(self, src2d, nkc, ncols):
                self.items.append((src2d, nkc, ncols))

            def pump(self):
                while self.issued < len(self.items) and self.issued < self.released + NSLOT:
                    self.slots.append(load_slab(*self.items[self.issued]))
                    self.issued += 1

            def take(self):
                self.pump()
                assert self.issued > self.taken
                i = self.slots[self.taken]
                self.taken += 1
                return i

            def release(self, n=1):
                self.released += n
                self.pump()

        stream = Stream()

        def setup_seq(s):
            E("act", lambda e: e.activation(out=condb[:, :, 0], in_=cT[:, s * KC:(s + 1) * KC], func=AF.Silu),
              reads=[t_cT], writes=[t_cond])
            E("act", lambda e: e.activation(out=condb[:, :, 1], in_=cT[:, s * KC:(s + 1) * KC], func=AF.Silu),
              reads=[t_cT], writes=[t_cond])
            E("dve", lambda e: e.tensor_copy(out=condbc[:], in_=condb[:, :, 0:1].broadcast_to([128, KC, 128])),
              reads=[t_cond], writes=[t_cond])
            for l in range(nlayers):
                for vi, c0 in enumerate((0, 1024, 3072, 4096)):
                    for half in range(2):
                        si = stream.take()
                        for j in range(4):
                            cc = half * 4 + j
                            col = (vi * KC + cc) * 2
                            for kc in range(KC):
                                mm(P[4][:, col:col + 2], wsl[:, si, kc, j * 128:(j + 1) * 128],
                                   condb[:, kc, :], kc == 0, kc == KC - 1,
                                   [t_wsl[si], t_cond], [t_P[4]])
                        stream.release()
                pview = P[4][:, 0:64].rearrange("p (v c two) -> p v c two", v=4, two=2)[:, :, :, 0]
                E("dve", lambda e: e.tensor_tensor(
                    out=modv[:, l, :, :], in0=pview,
                    in1=pv[:, l, _off["adab"]:_off["adab"] + 32].rearrange("p (v c) -> p v c", v=4),
                    op=ALU.add), reads=[t_P[4], t_pv], writes=[t_modv])
                for vi, pre in ((1, "pre1"), (3, "pre2")):
                    E("dve", lambda e: e.scalar_tensor_tensor(
                        out=modv[:, l, vi, :], in0=modv[:, l, vi, :], scalar=1.0,
                        in1=pv[:, l, _off[pre]:_off[pre] + 8], op0=ALU.add, op1=ALU.mult),
                      reads=[t_modv, t_pv], writes=[t_modv])
                for gi, (c0, bco, bcp) in enumerate(((2048, BC_ADAG1, BC_POST1), (5120, BC_ADAG2, BC_POST2))):
                    for half in range(2):
                        si = stream.take()
                        pb = P[5 + half]
                        for kc in range(KC):
                            mm(pb[:, :], condbc[:, kc, :], wsl[:, si, kc, :], kc == 0, kc == KC - 1,
                               [t_wsl[si], t_cond], [t_P[5 + half]])
                        stream.release()
                        fw.dma("sp", tmpf[:, half * 2, :],
                               bc_d[l, bco + half * 512: bco + (half + 1) * 512].partition_broadcast(128),
                               writes=[t_tmpf[half * 2]])
                        fw.dma("sp", tmpf[:, half * 2 + 1, :],
                               bc_d[l, bcp + half * 512: bcp + (half + 1) * 512].partition_broadcast(128),
                               writes=[t_tmpf[half * 2 + 1]])
                        E("dve", lambda e: e.tensor_tensor(out=gp[:, l, gi, half * 512:(half + 1) * 512],
                                                           in0=pb[:, :], in1=tmpf[:, half * 2, :], op=ALU.add),
                          reads=[t_P[5 + half], t_tmpf[half * 2]], writes=[t_gp[l][gi]])
                        E("dve", lambda e: e.tensor_tensor(out=gp[:, l, gi, half * 512:(half + 1) * 512],
                                                           in0=gp[:, l, gi, half * 512:(half + 1) * 512],
                                                           in1=tmpf[:, half * 2 + 1, :], op=ALU.mult),
                          reads=[t_gp[l][gi], t_tmpf[half * 2 + 1]], writes=[t_gp[l][gi]])
            for l in range(nlayers):
                E("dve", lambda e: e.memset(Tm[:, l], 0.0), writes=t_Tm[l])
                E("dve", lambda e: e.memset(Cb[:, l], 0.0), writes=t_Cb[l])
                E("dve", lambda e: e.memset(nm[:, l], 0.0), writes=[t_nm[l]])
                E("dve", lambda e: e.memset(nbw[:, l], 0.0), writes=[t_nb[l]])
                E("dve", lambda e: e.memset(elp[:, l], 1.0), writes=[t_elp[l]])
                E("dve", lambda e: e.memset(hq[:, l], 0.0), writes=[t_hq[l]])
                E("dve", lambda e: e.memset(hg[:, l], 0.0), writes=[t_hg[l]])
                E("dve", lambda e: e.memset(hf[:, l], 0.0), writes=[t_hf[l]])

        def queue_setup_weights(s):
            for l in range(nlayers):
                for c0 in (0, 1024, 3072, 4096, 2048, 5120):
                    for half in range(2):
                        stream.add(ada_d[l, :, c0 + half * 512: c0 + (half + 1) * 512], KC, 512)

        def queue_layer_weights(l):
            stream.add(win_d[l, :, 4096:4104], KC, 8)
            for j in range(4):
                stream.add(win_d[l, :, j * 512:(j + 1) * 512], KC, 512)
            for j in range(2):
                stream.add(win_d[l, :, 4104 + j * 512: 4104 + (j + 1) * 512], KC, 512)
                stream.add(win_d[l, :, 5128 + j * 512: 5128 + (j + 1) * 512], KC, 512)
            for j in range(4):
                stream.add(win_d[l, :, 2048 + j * 512: 2048 + (j + 1) * 512], KC, 512)
            for nh in range(2):
                for kg in range(2):
                    stream.add(wout_d[l, kg * 1024:(kg + 1) * 1024, nh * 512:(nh + 1) * 512], KC, 512)
            for j in range(6):
                w_ = 512 if j < 5 else 256
                stream.add(fup_d[l, :, j * 512: j * 512 + w_], KC, w_)
                stream.add(fup_d[l, :, DFF + j * 512: DFF + j * 512 + w_], KC, w_)
            for nh in range(2):
                for kg in range(3):
                    nk = 8 if kg < 2 else 6
                    stream.add(fdn_d[l, kg * 1024: kg * 1024 + nk * 128, nh * 512:(nh + 1) * 512], nk, 512)

        cin_ctr = [0]
        dg_ctr = [0]
        tmp_ctr = [0]

        def next_tmp():
            i = tmp_ctr[0] % 4
            tmp_ctr[0] += 1
            return i

        def prenorm(l, which):
            vs, vg = (0, 1) if which == 0 else (2, 3)
            for tt in range(TT):
                E("act", lambda e: e.activation(out=junk[:], in_=xb[:, tt, :], func=AF.Square,
                                                accum_out=sm[:, tt:tt + 1]),
                  reads=[t_xb[tt]], writes=[t_junk, t_sm])
            chk(1.2)
            E("act", lambda e: e.activation(out=sm[:, 4:8], in_=sm[:, 0:4], func=AF.Sqrt, scale=1.0 / D, bias=EPS),
              reads=[t_sm], writes=[t_sm])
            E("dve", lambda e: e.reciprocal(out=sm[:, 8:12], in_=sm[:, 4:8]), reads=[t_sm], writes=[t_sm])
            chk(1.4)
            for tt in range(TT):
                b = tt % 2
                E("dve", lambda e: e.tensor_scalar(out=xn[:, b, :], in0=xb[:, tt, :], scalar1=sm[:, 8 + tt:9 + tt],
                                                   scalar2=None, op0=ALU.mult),
                  reads=[t_xb[tt], t_sm], writes=[t_xn[b]])
                chk(1.6)
                pi = (tt % 2) * 2
                for kc in range(KC):
                    pq = pi + kc // 4
                    mm(P[pq][:, (kc % 4) * 128:(kc % 4 + 1) * 128], xn[:, b, kc * 128:(kc + 1) * 128], ident[:],
                       True, True, [t_xn[b], t_const], [t_P[pq]])
                chk(1.8)
                for kc in range(KC):
                    pq = pi + kc // 4
                    src = P[pq][:, (kc % 4) * 128:(kc % 4 + 1) * 128]
                    import os as _os
                    _sel = {"dve": True, "act": False}.get(_os.environ.get("EVAC", ""), kc >= 4)
                    E("dve" if _sel else "act", (lambda e: e.tensor_scalar(
                        out=u[:, kc, tt * 128:(tt + 1) * 128], in0=src,
                        scalar1=modv[:, l, vg, kc:kc + 1], scalar2=modv[:, l, vs, kc:kc + 1],
                        op0=ALU.mult, op1=ALU.add)) if _sel else (lambda e: e.activation(
                            out=u[:, kc, tt * 128:(tt + 1) * 128], in_=src,
                            func=AF.Identity, scale=modv[:, l, vg, kc:kc + 1], bias=modv[:, l, vs, kc:kc + 1])),
                      reads=[t_P[pq], t_modv], writes=[t_u[kc]])

        def build_diag(l, woff, ntap, chunk, tap0, ntp):
            i = dg_ctr[0] % NDG
            dg_ctr[0] += 1
            c0 = woff + chunk * ntap + tap0
            E("pool", lambda e: e.tensor_tensor(
                out=dg[:, i, 0:ntp, :], in0=identf[:].unsqueeze(1).broadcast_to([128, ntp, 128]),
                in1=pv[:, l, c0:c0 + ntp].unsqueeze(2).broadcast_to([128, ntp, 128]), op=ALU.mult),
              reads=[t_const, t_pv], writes=[t_dg[i]])
            return i

        def conv_pe(l, pout, t_pout, ci, j, woff, ntap, chunk):
            done = 0
            while done < ntap:
                ntp = min(DGT, ntap - done)
                di = build_diag(l, woff, ntap, chunk, done, ntp)
                for k in range(ntp):
                    kk = done + k
                    mm(pout[:, :], dg[:, di, k, :], cin[:, ci, j, kk:kk + T], kk == 0, kk == ntap - 1,
                       [t_dg[di], t_cin[ci]], [t_pout])
                done += ntp

        def post_sq(pi):
            E("act", lambda e: e.activation(out=junk[:, 0:512], in_=P[pi][:, :], func=AF.Square,
                                            accum_out=sm2[:, pi:pi + 1]), reads=[t_P[pi]], writes=[t_junk, t_sm2])

        def post_norm_all(l, gi):
            E("dve", lambda e: e.tensor_tensor(out=sm[:, 168:172], in0=sm2[:, 0:4], in1=sm2[:, 4:8], op=ALU.add),
              reads=[t_sm2], writes=[t_sm])
            E("act", lambda e: e.activation(out=sm[:, 172:176], in_=sm[:, 168:172], func=AF.Sqrt, scale=1.0 / D, bias=EPS),
              reads=[t_sm], writes=[t_sm])
            E("dve", lambda e: e.reciprocal(out=sm[:, 176:180], in_=sm[:, 172:176]), reads=[t_sm], writes=[t_sm])
            for tt in range(TT):
                for half in range(2):
                    pp, tp = P[half * 4 + tt], t_P[half * 4 + tt]
                    ti = next_tmp()
                    E("dve", lambda e: e.scalar_tensor_tensor(
                        out=tmpf[:, ti, :], in0=pp[:, :], scalar=sm[:, 176 + tt:177 + tt],
                        in1=gp[:, l, gi, half * 512:(half + 1) * 512], op0=ALU.mult, op1=ALU.mult),
                      reads=[tp, t_sm, t_gp[l][gi]], writes=[t_tmpf[ti]])
                    E("dve", lambda e: e.tensor_tensor(out=xb[:, tt, half * 512:(half + 1) * 512],
                                                        in0=xb[:, tt, half * 512:(half + 1) * 512],
                                                        in1=tmpf[:, ti, :], op=ALU.add),
                      reads=[t_xb[tt], t_tmpf[ti]], writes=[t_xb[tt]])

        def chk(ph):
            if ph >= stop:
                raise _Stop()

        def block_layer(l):
            chk(1)
            prenorm(l, 0)
            chk(2)
            si = stream.take()
            for tt in range(TT):
                for kc in range(KC):
                    mm(P[4][:, tt * 8:(tt + 1) * 8], u[:, kc, tt * 128:(tt + 1) * 128], wsl[:, si, kc, 0:8],
                       kc == 0, kc == KC - 1, [t_u[kc], t_wsl[si]], [t_P[4]])
            stream.release()
            G = sm[:, 32:64]
            E("dve", lambda e: e.tensor_tensor(out=G, in0=P[4][:, 0:32], in1=gb[:, l, :], op=ALU.add),
              reads=[t_P[4], t_gb], writes=[t_sm])
            Gv = G.rearrange("p (t g) -> p t g", g=8)
            LF = sm[:, 64:80].rearrange("p (t h) -> p t h", h=4)
            E("act", lambda e: e.activation(out=LF, in_=Gv[:, :, 4:8], func=AF.Exp, scale=-1.0),
              reads=[t_sm], writes=[t_sm])
            E("act", lambda e: e.activation(out=LF, in_=LF, func=AF.Ln, bias=1.0, scale=1.0),
              reads=[t_sm], writes=[t_sm])
            mm(P[5][:, 0:16], mask[:], sm[:, 64:80], True, True, [t_const, t_sm], [t_P[5]])
            mm(P[5][:, 16:32], onesf[:], sm[:, 64:80], True, True, [t_const, t_sm], [t_P[5]])
            A = sm[:, 80:96]
            Ee = sm[:, 96:112]
            EL = sm[:, 112:128]
            E("dve", lambda e: e.tensor_tensor(out=A.rearrange("p (t h) -> p t h", h=4), in0=Gv[:, :, 0:4],
                                               in1=P[5][:, 0:16].rearrange("p (t h) -> p t h", h=4), op=ALU.add),
              reads=[t_sm, t_P[5]], writes=[t_sm])
            E("act", lambda e: e.activation(out=A, in_=A, func=AF.Exp, bias=sm[:, 128:129], scale=1.0),
              reads=[t_sm], writes=[t_sm])
            E("act", lambda e: e.activation(out=sm[:, 96:128], in_=P[5][:, 0:32], func=AF.Exp, scale=-1.0),
              reads=[t_P[5]], writes=[t_sm])
            E("dve", lambda e: e.tensor_copy(out=vp[:, :, :, DH:DH + 2],
                                             in_=A.rearrange("p (t h o) -> p t h o", h=4, o=1).broadcast_to([128, TT, H, 2])),
              reads=[t_sm], writes=t_vp)

            chk(3)
            pend = None
            for sj in range(4):
                si = stream.take()
                ci = cin_ctr[0] % NCIN
                cin_ctr[0] += 1
                E("dve", lambda e: e.tensor_copy(out=cin[:, ci, :, 0:3], in_=hq[:, l, sj * 4:(sj + 1) * 4, :]),
                  reads=[t_hq[l]], writes=[t_cin[ci]])
                for j in range(4):
                    c = sj * 4 + j
                    pa = c % 2
                    for kc in range(KC):
                        mm(P[pa][:, :], wsl[:, si, kc, j * 128:(j + 1) * 128], u[:, kc, :], kc == 0, kc == KC - 1,
                           [t_wsl[si], t_u[kc]], [t_P[pa]])
                    E("act", lambda e: e.activation(out=cin[:, ci, j, 3:3 + T], in_=P[pa][:, :], func=AF.Copy),
                      reads=[t_P[pa]], writes=[t_cin[ci]])
                stream.release()
                E("dve", lambda e: e.tensor_copy(out=hq[:, l, sj * 4:(sj + 1) * 4, :], in_=cin[:, ci, :, T:T + 3]),
                  reads=[t_cin[ci]], writes=[t_hq[l]])
                for j in range(4):
                    c = sj * 4 + j
                    pc = 2 + c % 2
                    conv_pe(l, P[pc], t_P[pc], ci, j, _off["qkw"], 4, c)
                    E("act", lambda e: e.activation(out=R1[:, c, :], in_=P[pc][:, :], func=AF.Silu,
                                                    bias=pv[:, l, _off["qkb"] + c:_off["qkb"] + c + 1], scale=1.0),
                      reads=[t_P[pc], t_pv], writes=[t_R1[c]])

            chk(4)
            cf = {}

            def cf_proj(c):
                j = c % 4
                if j == 0:
                    cf["sa"] = stream.take()
                    cf["sg"] = stream.take()
                sa, sg = cf["sa"], cf["sg"]
                ci = cin_ctr[0] % NCIN
                cin_ctr[0] += 1
                cf[("ci", c)] = ci
                pa_, pg_ = (0, 1) if c % 2 == 0 else (4, 5)
                E("dve", lambda e: e.tensor_copy(out=cin[:, ci, 0, 0:30], in_=hg[:, l, c, :]),
                  reads=[t_hg[l]], writes=[t_cin[ci]])
                for kc in range(KC):
                    mm(P[pa_][:, :], wsl[:, sa, kc, j * 128:(j + 1) * 128], u[:, kc, :], kc == 0, kc == KC - 1,
                       [t_wsl[sa], t_u[kc]], [t_P[pa_]])
                for kc in range(KC):
                    mm(P[pg_][:, :], wsl[:, sg, kc, j * 128:(j + 1) * 128], u[:, kc, :], kc == 0, kc == KC - 1,
                       [t_wsl[sg], t_u[kc]], [t_P[pg_]])
                if j == 3:
                    stream.release(2)
                ti = next_tmp()
                E("act", lambda e: e.activation(out=tmpf[:, ti, :], in_=P[pg_][:, :], func=AF.Sigmoid),
                  reads=[t_P[pg_]], writes=[t_tmpf[ti]])
                E("dve", lambda e: e.tensor_tensor(out=cin[:, ci, 0, 30:30 + T], in0=P[pa_][:, :],
                                                   in1=tmpf[:, ti, :], op=ALU.mult),
                  reads=[t_P[pa_], t_tmpf[ti]], writes=[t_cin[ci]])
                E("dve", lambda e: e.tensor_copy(out=hg[:, l, c, :], in_=cin[:, ci, 0, T:T + 30]),
                  reads=[t_cin[ci]], writes=[t_hg[l]])

            def cf_conv(c):
                ci = cf[("ci", c)]
                pc = 2 + c % 2
                conv_pe(l, P[pc], t_P[pc], ci, 0, _off["cvw"], 31, c)
                bias_ = pv[:, l, _off["cvb"] + c:_off["cvb"] + c + 1]
                E("act", lambda e: e.activation(out=my[:, c, :], in_=P[pc][:, :], func=AF.Identity, bias=bias_, scale=1.0),
                  reads=[t_P[pc], t_pv], writes=[t_my[c]])
                t2 = next_tmp()
                cf[("t2", c)] = t2
                ysq = tmpf[:, t2, :].bitcast(BF16)[:, 0:T]
                E("act", lambda e: e.activation(out=ysq, in_=P[pc][:, :], func=AF.Square, bias=bias_, scale=1.0),
                  reads=[t_P[pc], t_pv], writes=[t_tmpf[t2]])

            def cf_stats(c):
                t2 = cf[("t2", c)]
                ysq = tmpf[:, t2, :].bitcast(BF16)[:, 0:T]
                mm(P[6][:, :], onesln[:], my[:, c, :], c == 0, c == KC - 1, [t_const, t_my[c]], [t_P[6]])
                mm(P[7][:, :], onesln[:], ysq, c == 0, c == KC - 1, [t_const, t_tmpf[t2]], [t_P[7]])

            cf_proj(0)
            for c in range(KC):
                if c + 1 < KC:
                    cf_proj(c + 1)
                cf_conv(c)
                if c >= 1:
                    cf_stats(c - 1)
            cf_stats(KC - 1)
            pass
            E("act", lambda e: e.activation(out=stat[:, 0, :], in_=P[6][:, :], func=AF.Copy),
              reads=[t_P[6]], writes=[t_stat])
            ti = next_tmp()
            E("act", lambda e: e.activation(out=tmpf[:, ti, :], in_=P[6][:, :], func=AF.Square),
              reads=[t_P[6]], writes=[t_tmpf[ti]])
            E("dve", lambda e: e.tensor_tensor(out=tmpf[:, ti, :], in0=P[7][:, :], in1=tmpf[:, ti, :], op=ALU.subtract),
              reads=[t_P[7], t_tmpf[ti]], writes=[t_tmpf[ti]])
            E("act", lambda e: e.activation(out=tmpf[:, ti, :], in_=tmpf[:, ti, :], func=AF.Sqrt, bias=EPS, scale=1.0),
              reads=[t_tmpf[ti]], writes=[t_tmpf[ti]])
            E("dve", lambda e: e.reciprocal(out=stat[:, 1, :], in_=tmpf[:, ti, :]),
              reads=[t_tmpf[ti]], writes=[t_stat])
            for c in range(KC):
                ti = next_tmp()
                E("dve", lambda e: e.tensor_tensor(out=tmpf[:, ti, :], in0=my[:, c, :], in1=stat[:, 0, :], op=ALU.subtract),
                  reads=[t_my[c], t_stat], writes=[t_tmpf[ti]])
                E("dve", lambda e: e.tensor_tensor(out=tmpf[:, ti, :], in0=tmpf[:, ti, :], in1=stat[:, 1, :], op=ALU.mult),
                  reads=[t_tmpf[ti], t_stat], writes=[t_tmpf[ti]])
                E("act", lambda e: e.activation(out=my[:, c, :], in_=tmpf[:, ti, :], func=AF.Silu,
                                                scale=pv[:, l, _off["lng"] + c:_off["lng"] + c + 1],
                                                bias=pv[:, l, _off["lnb"] + c:_off["lnb"] + c + 1]),
                  reads=[t_tmpf[ti], t_pv], writes=[t_my[c]])

            chk(5)
            for sj in range(4):
                si = stream.take()
                for tt in range(TT):
                    pa = (0, 1, 4, 5)[(sj * TT + tt) % 4]
                    for kc in range(KC):
                        mm(P[pa][:, :], u[:, kc, tt * 128:(tt + 1) * 128], wsl[:, si, kc, :], kc == 0, kc == KC - 1,
                           [t_u[kc], t_wsl[si]], [t_P[pa]])
                    if sj < 2:
                        for hh in range(2):
                            h = sj * 2 + hh
                            E("dve", (lambda e: e.activation(
                                out=vp[:, tt, h, 0:DH], in_=P[pa][:, hh * DH:(hh + 1) * DH], func=AF.Identity,
                                scale=sm[:, 80 + tt * 4 + h:81 + tt * 4 + h])) if False else (lambda e: e.tensor_scalar(
                                    out=vp[:, tt, h, 0:DH], in0=P[pa][:, hh * DH:(hh + 1) * DH],
                                    scalar1=sm[:, 80 + tt * 4 + h:81 + tt * 4 + h], scalar2=None, op0=ALU.mult)),
                              reads=[t_P[pa], t_sm], writes=[t_vp[tt]])
                    else:
                        E("act", lambda e: e.activation(out=go[:, tt, (sj - 2) * 512:(sj - 1) * 512], in_=P[pa][:, :],
                                                        func=AF.Sigmoid), reads=[t_P[pa]], writes=[t_go[tt]])
                stream.release()

            chk(6)
            ml = {}

            def ktf(tt, h, dc):
                return R1[:, 16 + 2 * tt + h // 2, (h % 2) * 256 + dc * 128:(h % 2) * 256 + (dc + 1) * 128]

            def st_T(tt):
                ts = slice(tt * 128, (tt + 1) * 128)
                for c in range(KC):
                    mm(P[c // 4][:, (c % 4) * 128:(c % 4 + 1) * 128], R1[:, 8 + c, ts], ident[:], True, True,
                       [t_R1[8 + c], t_const], [t_P[c // 4]])
                for a_ in range(2):
                    E("act" if a_ else "dve", (lambda e: e.activation(out=R1[:, 16 + 2 * tt + a_, :], in_=P[a_][:, :], func=AF.Copy))
                      if a_ else (lambda e: e.tensor_copy(out=R1[:, 16 + 2 * tt + a_, :], in_=P[a_][:, :])),
                      reads=[t_P[a_]], writes=[t_R1[16 + 2 * tt + a_]])
                for h in range(H):
                    for dc in range(2):
                        mm(P[2][:, h * 128:(h + 1) * 128], R1[:, 8 + 2 * h + dc, ts], R1[:, 2 * h + dc, ts],
                           dc == 0, dc == 1, [t_R1[8 + 2 * h + dc], t_R1[2 * h + dc]], [t_P[2]])
                Sts = []
                for h in range(H):
                    ti = next_tmp()
                    St = tmpf[:, ti, :].bitcast(BF16)[:, 0:128]
                    Sts.append((St, ti))
                    E("dve", lambda e: e.tensor_tensor(out=St, in0=P[2][:, h * 128:(h + 1) * 128], in1=mask[:], op=ALU.mult),
                      reads=[t_P[2], t_const], writes=[t_tmpf[ti]])
                ml[("Sts", tt)] = Sts

            def st_N(tt):
                ts = slice(tt * 128, (tt + 1) * 128)
                Sts = ml[("Sts", tt)]
                for h in range(H):
                    St, ti = Sts[h]
                    pn = P[4 + h // 2]
                    tpn = t_P[4 + h // 2]
                    no = (h % 2) * DH
                    for dc in range(2):
                        mm(pn[:, no:no + DH], R1[:, 2 * h + dc, ts], Cb[:, l, h, dc * DH:(dc + 1) * DH], dc == 0, False,
                           [t_R1[2 * h + dc], t_Cb[l][h]], [tpn])
                    mm(pn[:, no:no + DH], St, vp[:, tt, h, 0:DH], False, True, [t_tmpf[ti], t_vp[tt]], [tpn])
                for h in range(H):
                    St, ti = Sts[h]
                    for dc in range(2):
                        mm(P[3][:, h * 2:h * 2 + 2], R1[:, 2 * h + dc, ts], nbp(l, h, dc), dc == 0, False, [t_R1[2 * h + dc], t_nb[l]], [t_P[3]])
                    mm(P[3][:, h * 2:h * 2 + 2], St, vp[:, tt, h, DH:DH + 2], False, True, [t_tmpf[ti], t_vp[tt]], [t_P[3]])
                E("act", lambda e: e.activation(out=numsb[:, 0:512], in_=P[4][:, :], func=AF.Copy), reads=[t_P[4]], writes=[t_numsb])
                E("dve", lambda e: e.tensor_copy(out=numsb[:, 512:1024], in_=P[5][:, :]), reads=[t_P[5]], writes=[t_numsb])
                E("dve", lambda e: e.tensor_copy(out=sm[:, 184:192], in_=P[3][:, 0:8]), reads=[t_P[3]], writes=[t_sm])

            def st_U(tt):
                for h in range(H):
                    for dc in range(2):
                        mm(P[3][:, 16 + h * 4 + dc * 2:18 + h * 4 + dc * 2], ktf(tt, h, dc), vp[:, tt, h, DH:DH + 2], True, True,
                           [t_R1[16 + 2 * tt + h // 2], t_vp[tt]], [t_P[3]])
                elp4 = elp[:, l, :] if tt == 0 else sm[:, 112 + (tt - 1) * 4:112 + tt * 4]
                t_elprev = t_elp[l] if tt == 0 else t_sm
                el4 = sm[:, 112 + tt * 4:116 + tt * 4]
                for h in range(H):
                    idx = tt * 4 + h
                    pu = P[6 + h % 2]
                    tpu = t_P[6 + h % 2]
                    for dc in range(2):
                        mm(pu[:, dc * DH:(dc + 1) * DH], ktf(tt, h, dc), vp[:, tt, h, 0:DH], True, True,
                           [t_R1[16 + 2 * tt + h // 2], t_vp[tt]], [tpu])
                    E("dve", lambda e: e.scalar_tensor_tensor(out=Tm[:, l, h, :], in0=Tm[:, l, h, :], scalar=elp4[:, h:h + 1],
                                                              in1=pu[:, :], op0=ALU.mult, op1=ALU.add),
                      reads=[t_Tm[l][h], t_elprev, tpu], writes=[t_Tm[l][h]])
                    E("act", lambda e: e.activation(out=Cb[:, l, h, :], in_=Tm[:, l, h, :], func=AF.Identity,
                                                    scale=sm[:, 112 + idx:113 + idx]),
                      reads=[t_Tm[l][h], t_sm], writes=[t_Cb[l][h]])
                nsl = P[3][:, 16:32].rearrange("p (h d two) -> p h d two", h=4, two=2)[:, :, :, 0]
                E("dve", lambda e: e.tensor_tensor(out=nm[:, l], in0=nm[:, l], in1=elp4.unsqueeze(2).broadcast_to([128, H, 2]),
                                                   op=ALU.mult), reads=[t_nm[l], t_elprev], writes=[t_nm[l]])
                E("dve", lambda e: e.tensor_tensor(out=nm[:, l], in0=nm[:, l], in1=nsl, op=ALU.add),
                  reads=[t_nm[l], t_P[3]], writes=[t_nm[l]])
                E("dve", lambda e: e.tensor_tensor(out=nbw[:, l], in0=nm[:, l].unsqueeze(3).broadcast_to([128, H, 2, 2]),
                                                   in1=el4.unsqueeze(2).unsqueeze(3).broadcast_to([128, H, 2, 2]), op=ALU.mult),
                  reads=[t_nm[l], t_sm], writes=[t_nb[l]])

            def st_F1(tt):
                e4 = sm[:, 96 + tt * 4:100 + tt * 4]
                den = sm[:, 184:192].rearrange("p (h two) -> p h two", two=2)[:, :, 0]
                W0 = sm[:, 136:140]
                W1 = sm[:, 140:144]
                W2 = sm[:, 144:148]
                E("dve", lambda e: e.tensor_tensor(out=W0, in0=den, in1=e4, op=ALU.mult), reads=[t_sm], writes=[t_sm])
                E("act", lambda e: e.activation(out=W0, in_=W0, func=AF.Abs), reads=[t_sm], writes=[t_sm])
                E("dve", lambda e: e.tensor_scalar_max(out=W0, in0=W0, scalar1=1.0), reads=[t_sm], writes=[t_sm])
                E("dve", lambda e: e.reciprocal(out=W0, in_=W0), reads=[t_sm], writes=[t_sm])
                E("dve", lambda e: e.tensor_tensor(out=W0, in0=W0, in1=e4, op=ALU.mult), reads=[t_sm], writes=[t_sm])
                for h in range(H):
                    E("act", lambda e: e.activation(out=junk[:, 0:DH], in_=numsb[:, h * DH:(h + 1) * DH], func=AF.Square,
                                                    accum_out=sm[:, 148 + h:149 + h]),
                      reads=[t_numsb], writes=[t_junk, t_sm])
                SS = sm[:, 148:152]
                E("dve", lambda e: e.tensor_tensor(out=W1, in0=W0, in1=W0, op=ALU.mult), reads=[t_sm], writes=[t_sm])
                E("dve", lambda e: e.tensor_tensor(out=W1, in0=W1, in1=SS, op=ALU.mult), reads=[t_sm], writes=[t_sm])
                E("act", lambda e: e.activation(out=W1, in_=W1, func=AF.Sqrt, scale=1.0 / DH, bias=EPS), reads=[t_sm], writes=[t_sm])
                E("dve", lambda e: e.reciprocal(out=W1, in_=W1), reads=[t_sm], writes=[t_sm])
                E("dve", lambda e: e.tensor_tensor(out=W2, in0=W1, in1=W0, op=ALU.mult), reads=[t_sm], writes=[t_sm])
                hb = tt % 2
                for h in range(H):
                    E("dve", lambda e: e.scalar_tensor_tensor(out=hfin[:, hb, h * DH:(h + 1) * DH], in0=numsb[:, h * DH:(h + 1) * DH],
                                                              scalar=sm[:, 144 + h:145 + h], in1=go[:, tt, h * DH:(h + 1) * DH],
                                                              op0=ALU.mult, op1=ALU.mult),
                      reads=[t_numsb, t_sm, t_go[tt]], writes=[t_hfin[hb]])

            def st_F2(tt):
                ts = slice(tt * 128, (tt + 1) * 128)
                hb = tt % 2
                for c in range(KC):
                    mm(P[c // 4][:, (c % 4) * 128:(c % 4 + 1) * 128], hfin[:, hb, c * 128:(c + 1) * 128], ident[:], True, True,
                       [t_hfin[hb], t_const], [t_P[c // 4]])
                for c in range(KC):
                    src = P[c // 4][:, (c % 4) * 128:(c % 4 + 1) * 128]
                    E("dve" if c >= 4 else "act", (lambda e: e.tensor_scalar(
                        out=u[:, c, ts], in0=src,
                        scalar1=pv[:, l, _off["mlg"] + c:_off["mlg"] + c + 1], scalar2=None, op0=ALU.mult)) if c >= 4 else (
                        lambda e: e.activation(out=u[:, c, ts], in_=src, func=AF.Identity,
                                               scale=pv[:, l, _off["mlg"] + c:_off["mlg"] + c + 1])),
                      reads=[t_P[c // 4], t_pv], writes=[t_u[c]])

            st_T(0)
            for tt in range(TT):
                st_N(tt)
                st_U(tt)
                if tt + 1 < TT:
                    st_T(tt + 1)
                st_F1(tt)
                if tt >= 1:
                    st_F2(tt - 1)
            st_F2(TT - 1)
            E("dve", lambda e: e.tensor_copy(out=elp[:, l, :], in_=sm[:, 124:128]), reads=[t_sm], writes=[t_elp[l]])

            chk(7)
            for nh in range(2):
                sis = [stream.take(), stream.take()]
                for tt in range(TT):
                    pi = nh * 4 + tt
                    for kg in range(2):
                        si = sis[kg]
                        for kc in range(KC):
                            src = (u[:, kc, tt * 128:(tt + 1) * 128], t_u[kc]) if kg == 0 else \
                                (my[:, kc, tt * 128:(tt + 1) * 128], t_my[kc])
                            mm(P[pi][:, :], src[0], wsl[:, si, kc, :], kg == 0 and kc == 0, kg == 1 and kc == KC - 1,
                               [src[1], t_wsl[si]], [t_P[pi]])
                    post_sq(pi)
                stream.release(2)
            post_norm_all(l, 0)

            chk(8)
            prenorm(l, 1)
            pend = []
            for sj in range(6):
                sa = stream.take()
                sg = stream.take()
                nch = 4 if sj < 5 else 2
                ca = cin_ctr[0] % NCIN
                cin_ctr[0] += 1
                cg = cin_ctr[0] % NCIN
                cin_ctr[0] += 1
                c0 = sj * 4
                E("dve", lambda e: e.tensor_copy(out=cin[:, ca, 0:nch, 0:2], in_=hf[:, l, c0:c0 + nch, :]),
                  reads=[t_hf[l]], writes=[t_cin[ca]])
                E("dve", lambda e: e.tensor_copy(out=cin[:, cg, 0:nch, 0:2], in_=hf[:, l, NFC + c0:NFC + c0 + nch, :]),
                  reads=[t_hf[l]], writes=[t_cin[cg]])
                for j in range(nch):
                    for (sw, cb, pp) in ((sa, ca, 0 if j % 2 == 0 else 6), (sg, cg, 1 if j % 2 == 0 else 7)):
                        for kc in range(KC):
                            mm(P[pp][:, :], wsl[:, sw, kc, j * 128:(j + 1) * 128], u[:, kc, :], kc == 0, kc == KC - 1,
                               [t_wsl[sw], t_u[kc]], [t_P[pp]])
                        E("act" if pp % 2 else "dve", (lambda e: e.activation(out=cin[:, cb, j, 2:2 + T], in_=P[pp][:, :], func=AF.Copy))
                          if pp % 2 else (lambda e: e.tensor_copy(out=cin[:, cb, j, 2:2 + T], in_=P[pp][:, :])),
                          reads=[t_P[pp]], writes=[t_cin[cb]])
                stream.release(2)
                E("dve", lambda e: e.tensor_copy(out=hf[:, l, c0:c0 + nch, :], in_=cin[:, ca, 0:nch, T:T + 2]),
                  reads=[t_cin[ca]], writes=[t_hf[l]])
                E("dve", lambda e: e.tensor_copy(out=hf[:, l, NFC + c0:NFC + c0 + nch, :], in_=cin[:, cg, 0:nch, T:T + 2]),
                  reads=[t_cin[cg]], writes=[t_hf[l]])
                for j in range(nch):
                    c = c0 + j
                    qa, qg = (2, 3) if c % 2 == 0 else (4, 5)
                    conv_pe(l, P[qa], t_P[qa], ca, j, _off["fw"], 3, c)
                    conv_pe(l, P[qg], t_P[qg], cg, j, _off["fw"], 3, NFC + c)
                    ti = next_tmp()
                    E("act", lambda e: e.activation(out=tmpf[:, ti, :], in_=P[qg][:, :], func=AF.Silu,
                                                    bias=pv[:, l, _off["fb"] + NFC + c:_off["fb"] + NFC + c + 1], scale=1.0),
                      reads=[t_P[qg], t_pv], writes=[t_tmpf[ti]])
                    E("dve", lambda e: e.scalar_tensor_tensor(out=R1[:, c, :], in0=P[qa][:, :],
                                                              scalar=pv[:, l, _off["fb"] + c:_off["fb"] + c + 1],
                                                              in1=tmpf[:, ti, :], op0=ALU.add, op1=ALU.mult),
                      reads=[t_P[qa], t_pv, t_tmpf[ti]], writes=[t_R1[c]])
            for nh in range(2):
                sis = [stream.take(), stream.take(), stream.take()]
                for tt in range(TT):
                    pi = nh * 4 + tt
                    for kg in range(3):
                        nk = 8 if kg < 2 else 6
                        si = sis[kg]
                        for kc in range(nk):
                            cidx = kg * 8 + kc
                            mm(P[pi][:, :], R1[:, cidx, tt * 128:(tt + 1) * 128], wsl[:, si, kc, :],
                               cidx == 0, cidx == NFC - 1, [t_R1[cidx], t_wsl[si]], [t_P[pi]])
                    post_sq(pi)
                stream.release(3)
            post_norm_all(l, 1)

        def nbp(l, h, dc):
            return nbw[:, l, h, dc, :]

        def nb2(l, h):
            return nbw[:, l, h, :, :]

        E("dve", lambda e: e.memset(sm[:, 128:129], float(-np.log(16.0))), writes=[t_sm])

        for s in range(nseq):
            queue_setup_weights(s)
            for b in range(nblk):
                for l in range(nlayers):
                    queue_layer_weights(l)
        for s in range(nseq):
            setup_seq(s)
            for b in range(nblk):
                for tt in range(TT):
                    fw.dma("sp", xb[:, tt, :], x_d[s, b * T + tt * 128: b * T + (tt + 1) * 128, :], writes=[t_xb[tt]])
                try:
                    for l in range(nlayers):
                        block_layer(l)
                except _Stop:
                    pass
                for tt in range(TT):
                    out_toks.append(fw.dma("sp", y_d[s, b * T + tt * 128: b * T + (tt + 1) * 128, :], xb[:, tt, :],
                                           reads=[t_xb[tt]]))
        fw.finish(out_toks)
        print("instructions:", fw.ninst, fw.per, "sems:", fw.nsems, "slabs:", len(stream.items))
    return nc


def _fm(v):
    return np.ascontiguousarray(v.reshape(-1, 128).T)


def _fmw(w):
    K, C = w.shape
    return np.ascontiguousarray(w.T.reshape(C // 128, 128, K).transpose(1, 0, 2).reshape(128, -1))


def pack_params(inp, layers):
    pvs, bcs = [], []
    for l in layers:
        ab = inp["ada_b"][l]
        cols = [
            _fm(inp["mix_pre_g"][l]), _fmw(inp["qk_conv_w"][l]), _fm(inp["qk_conv_b"][l]),
            _fmw(inp["cv_dw_w"][l]), _fm(inp["cv_dw_b"][l]), _fm(inp["cv_ln_g"][l]), _fm(inp["cv_ln_b"][l]),
            _fm(inp["ffn_pre_g"][l]), _fmw(inp["ffn_conv_w"][l]), _fm(inp["ffn_conv_b"][l]),
            _fm(inp["ml_norm_g"][l]),
            _fm(ab[0:1024]), _fm(ab[1024:2048]), _fm(ab[3072:4096]), _fm(ab[4096:5120]),
        ]
        pvs.append(np.concatenate(cols, axis=1))
        gbias = np.concatenate([inp["igate_b"][l], inp["fgate_b"][l]])
        bcs.append(np.concatenate([inp["mix_post_g"][l], inp["ffn_post_g"][l], ab[2048:3072], ab[5120:6144],
                                   np.tile(gbias, 4)]))
    pvec = np.ascontiguousarray(np.stack(pvs)).astype(np.float32)
    bcv = np.ascontiguousarray(np.stack(bcs)).astype(np.float32)
    assert pvec.shape[2] == NPV and bcv.shape[1] == NBC
    return pvec, bcv


def make_in_maps(inp, ncores, nseq, nblk, layers):
    pvec, bcv = pack_params(inp, layers)
    S = nblk * T
    L = list(layers)
    shared = {
        "ada_w": np.ascontiguousarray(inp["ada_w"][L]), "w_in": np.ascontiguousarray(inp["w_in"][L]),
        "w_out": np.ascontiguousarray(inp["w_out"][L]), "ffn_up": np.ascontiguousarray(inp["ffn_up"][L]),
        "ffn_down": np.ascontiguousarray(inp["ffn_down"][L]), "pvec": pvec, "bcv": bcv,
    }
    maps = []
    for c in range(ncores):
        xs = np.ascontiguousarray(inp["x"][c * nseq:(c + 1) * nseq, :S, :])
        cc = inp["c"][c * nseq:(c + 1) * nseq]
        cT = np.ascontiguousarray(cc.reshape(nseq, KC, 128).transpose(2, 0, 1).reshape(128, nseq * KC))
        m = dict(shared)
        m["x"] = xs
        m["cT"] = cT
        maps.append(m)
    return maps


_NC_CACHE = {}


def kernel(**inputs):
    inp = {k: np.asarray(v, dtype=np.float32) for k, v in inputs.items()}
    ncores, nseq, nblk = 8, 2, 4
    key = (nseq, nblk, 2)
    if key not in _NC_CACHE:
        _NC_CACHE[key] = build(nseq, nblk, 2)
    nc = _NC_CACHE[key]
    maps = make_in_maps(inp, ncores, nseq, nblk, (0, 1))
    res = run_bass_kernel_spmd(nc, maps, core_ids=list(range(ncores)))
    out = np.concatenate([np.asarray(r["y"]).reshape(nseq, SEQ, D) for r in res.results], axis=0)
    return out.astype(np.float32)
```

```python
import numpy as np
from contextlib import ExitStack
import concourse.bass as bass
import concourse.mybir as mybir
from concourse.bass_utils import run_bass_kernel_spmd

F32 = mybir.dt.float32
BF16 = mybir.dt.bfloat16
AF = mybir.ActivationFunctionType
ALU = mybir.AluOpType

D = 1024
KC = 8
T = 512
TT = 4
H = 4
DH = 256
SEQ = 2048
N_IN = 6152
DFF = 2816
NFC = 22
EPS = 1e-6

_off = {}
_n = 0
for _name, _w in (("pre1", 8), ("qkw", 64), ("qkb", 16), ("cvw", 248), ("cvb", 8), ("lng", 8),
                  ("lnb", 8), ("pre2", 8), ("fw", 132), ("fb", 44), ("mlg", 8), ("adab", 32)):
    _off[_name] = _n
    _n += _w
NPV = _n
BC_POST1, BC_POST2, BC_ADAG1, BC_ADAG2, BC_GB = 0, 1024, 2048, 3072, 4096
NBC = 4096 + 32


class Tk:
    __slots__ = ("name", "w", "r", "dsem", "dcount", "excl")

    def __init__(self, name, excl=False):
        self.name = name
        self.excl = excl
        self.w = None
        self.r = {}
        self.dsem = None
        self.dcount = 0


class Eng:
    def __init__(self, fw, name, handle):
        self.fw = fw
        self.name = name
        self.h = handle
        self.sem = None
        self.count = 0
        self.waited = {}
        self.nsem = 0

    def newsem(self):
        self.sem = self.fw.alloc_sem(f"{self.name}_e{self.nsem}")
        self.nsem += 1
        self.count = 0


class FW:
    EPOCH = 12000
    STRICT = True

    def __init__(self, nc, stack):
        self.nc = nc
        self.stack = stack
        self.nsems = 0
        self.engs = {}
        for name, h in (("pe", nc.tensor), ("act", nc.scalar), ("dve", nc.vector),
                        ("pool", nc.gpsimd), ("sp", nc.sync)):
            e = Eng(self, name, h)
            e.newsem()
            self.engs[name] = e
        self.ninst = 0
        self.per = {k: 0 for k in self.engs}

    def alloc_sem(self, name):
        self.nsems += 1
        return self.stack.enter_context(self.nc.semaphore(name))

    def _wait(self, eng, tok, kind):
        sem, val, en = tok
        if en == eng.name and kind != "raw" and (eng.name == "pe" or not self.STRICT):
            return
        if eng.waited.get(sem, 0) >= val:
            return
        eng.h.wait_ge(sem, val)
        eng.waited[sem] = val
        self.ninst += 1

    def _deps(self, eng, reads, writes):
        for t in reads:
            if t.w is not None:
                self._wait(eng, t.w, "raw")
            if t.excl:
                for sem, (val, en) in t.r.items():
                    if en != eng.name:
                        self._wait(eng, (sem, val, en), "raw")
        for t in writes:
            if t.w is not None:
                self._wait(eng, t.w, "waw")
            for sem, (val, en) in t.r.items():
                self._wait(eng, (sem, val, en), "war")

    def op(self, engname, fn, reads=(), writes=()):
        eng = self.engs[engname]
        self._deps(eng, reads, writes)
        inst = fn(eng.h)
        if eng.count >= self.EPOCH:
            eng.newsem()
        inst.then_inc(eng.sem, 1)
        eng.count += 1
        self.ninst += 1
        self.per[engname] += 1
        tok = (eng.sem, eng.count, eng.name)
        for t in reads:
            t.r[eng.sem] = (eng.count, eng.name)
        for t in writes:
            t.w = tok
            t.r = {}
        return inst

    def dma(self, qname, out, in_, reads=(), writes=(), **kw):
        eng = self.engs[qname]
        for t in reads:
            if t.w is not None:
                self._wait(eng, t.w, "raw")
        for t in writes:
            if t.w is not None:
                self._wait(eng, t.w, "raw")
            for sem, (val, en) in t.r.items():
                self._wait(eng, (sem, val, en), "raw")
        owner = (list(writes) + list(reads))[0]
        if owner.dsem is None:
            owner.dsem = self.alloc_sem("d_" + owner.name)
        inst = eng.h.dma_start(out=out, in_=in_, **kw)
        inst.then_inc(owner.dsem, 16)
        owner.dcount += 16
        self.ninst += 1
        tok = (owner.dsem, owner.dcount, "dma")
        for t in reads:
            t.r[owner.dsem] = (owner.dcount, "dma")
        for t in writes:
            t.w = tok
            t.r = {}
        return tok

    def finish(self, toks):
        eng = self.engs["sp"]
        for tok in toks:
            self._wait(eng, tok, "raw")


class _Stop(Exception):
    pass


def build(nseq=2, nblk=4, nlayers=2, dbg=(), stop=99):
    S = nblk * T
    nc = bass.Bass("TRN2", target_bir_lowering=False)
    dr = lambda name, shape, kind="ExternalInput": nc.dram_tensor(name, shape, F32, kind=kind).ap()
    x_d = dr("x", [nseq, S, D])
    c_d = dr("cT", [128, nseq * KC])
    ada_d = dr("ada_w", [nlayers, D, 6 * D])
    win_d = dr("w_in", [nlayers, D, N_IN])
    wout_d = dr("w_out", [nlayers, 2 * D, D])
    fup_d = dr("ffn_up", [nlayers, D, 2 * DFF])
    fdn_d = dr("ffn_down", [nlayers, DFF, D])
    pv_d = dr("pvec", [nlayers, 128, NPV])
    bc_d = dr("bcv", [nlayers, NBC])
    y_d = dr("y", [nseq, S, D], kind="ExternalOutput")
    dbg_d = {name: dr("dbg_" + name, shape, kind="ExternalOutput") for name, shape in dbg}

    st = ExitStack()
    with st:
        sb = lambda name, shape, dt=F32: st.enter_context(nc.sbuf_tensor(name, shape, dt))
        NSLOT = 5
        wsl = sb("wsl", [128, NSLOT, KC, 512], BF16)
        xb = sb("xb", [128, TT, D])
        xn = sb("xn", [128, 2, D], BF16)
        u = sb("u", [128, KC, T], BF16)
        R1 = sb("R1", [128, 24, T], BF16)
        vp = sb("vp", [128, TT, H, DH + 2], BF16)
        go = sb("go", [128, TT, D], BF16)
        NCIN = 2
        CINW = 544
        cin = sb("cin", [128, NCIN, 4, CINW], BF16)
        NDG = 2
        DGT = 16
        dg = sb("dg", [128, NDG, DGT, 128], BF16)
        tmpf = sb("tmpf", [128, 4, T])
        stat = sb("stat", [128, 2, T])
        my = sb("my", [128, KC, T], BF16)
        hfin = sb("hfin", [128, 2, D], BF16)
        Tm = sb("Tm", [128, nlayers, H, 2 * DH])
        Cb = sb("Cb", [128, nlayers, H, 2 * DH], BF16)
        nm = sb("nm", [128, nlayers, H, 2])
        nbw = sb("nbw", [128, nlayers, H, 2, 2], BF16)
        elp = sb("elp", [128, nlayers, H])
        hq = sb("hq", [128, nlayers, 16, 3], BF16)
        hg = sb("hg", [128, nlayers, 8, 30], BF16)
        hf = sb("hf", [128, nlayers, 2 * NFC, 2], BF16)
        gp = sb("gp", [128, nlayers, 2, D])
        pv = sb("pv", [128, nlayers, NPV])
        gb = sb("gb", [128, nlayers, 32])
        modv = sb("modv", [128, nlayers, 4, KC])
        junk = sb("junk", [128, D], BF16)
        ident = sb("ident", [128, 128], BF16)
        identf = sb("identf", [128, 128])
        mask = sb("mask", [128, 128])
        onesf = sb("onesf", [128, 128])
        onesln = sb("onesln", [128, 128], BF16)
        cT = sb("cTs", [128, nseq * KC])
        condb = sb("condb", [128, KC, 2], BF16)
        condbc = sb("condbc", [128, KC, 128], BF16)
        sm = sb("sm", [128, 512])
        sm2 = sb("sm2", [128, 8])
        numsb = sb("numsb", [128, D])
        P = [st.enter_context(nc.psum_tensor(f"P{i}", [128, 512], F32)) for i in range(8)]
        PB = [p[:].bitcast(BF16) for p in P]

        fw = FW(nc, st)
        st.enter_context(nc.Block())

        tk = lambda n: Tk(n)
        t_wsl = [tk(f"wsl{i}") for i in range(NSLOT)]
        t_xb = [tk(f"xb{i}") for i in range(TT)]
        t_xn = [tk(f"xn{i}") for i in range(2)]
        t_u = [tk(f"u{i}") for i in range(KC)]
        t_R1 = [tk(f"R1_{i}") for i in range(24)]
        t_vp = [tk(f"vp{i}") for i in range(TT)]
        t_go = [tk(f"go{i}") for i in range(TT)]
        t_cin = [tk(f"cin{i}") for i in range(NCIN)]
        t_dg = [tk(f"dg{i}") for i in range(NDG)]
        t_tmpf = [tk(f"tmpf{i}") for i in range(4)]
        t_stat = tk("stat")
        t_my = [tk(f"my{i}") for i in range(KC)]
        t_hfin = [tk(f"hfin{i}") for i in range(2)]
        t_Tm = [[tk(f"Tm{l}_{h}") for h in range(H)] for l in range(nlayers)]
        t_Cb = [[tk(f"Cb{l}_{h}") for h in range(H)] for l in range(nlayers)]
        t_nm = [tk(f"nm{l}") for l in range(nlayers)]
        t_nb = [tk(f"nb{l}") for l in range(nlayers)]
        t_elp = [tk(f"elp{l}") for l in range(nlayers)]
        t_hq = [tk(f"hq{l}") for l in range(nlayers)]
        t_hg = [tk(f"hg{l}") for l in range(nlayers)]
        t_hf = [tk(f"hf{l}") for l in range(nlayers)]
        t_gp = [[tk(f"gp{l}_{j}") for j in range(2)] for l in range(nlayers)]
        t_pv = tk("pv")
        t_gb = tk("gb")
        t_modv = tk("modv")
        t_junk = tk("junk")
        t_const = tk("const")
        t_cT = tk("cT")
        t_cond = tk("cond")
        t_sm = tk("sm")
        t_sm2 = tk("sm2")
        t_numsb = tk("numsb")
        t_P = [Tk(f"P{i}", excl=True) for i in range(8)]
        out_toks = []

        E = fw.op

        def mm(out, lhsT, rhs, start, stop, reads, writes):
            return E("pe", lambda e: e.matmul(out, lhsT=lhsT, rhs=rhs, start=start, stop=stop),
                     reads, writes)

        def dump(name, src_ap, reads):
            if name in dbg_d:
                fw.dma("sp", dbg_d[name], src_ap, reads=reads)

        E("dve", lambda e: e.memset(identf[:], 0.0), writes=[t_const])
        E("pool", lambda e: e.affine_select(out=identf[:], in_=identf[:], pattern=[[-1, 128]],
                                            compare_op=ALU.not_equal, fill=1.0, base=0,
                                            channel_multiplier=1), reads=[t_const], writes=[t_const])
        E("dve", lambda e: e.tensor_copy(out=ident[:], in_=identf[:]), reads=[t_const], writes=[t_const])
        E("dve", lambda e: e.memset(onesf[:], 1.0), writes=[t_const])
        E("dve", lambda e: e.memset(onesln[:], 1.0 / D), writes=[t_const])
        E("dve", lambda e: e.memset(mask[:], 1.0), writes=[t_const])
        E("pool", lambda e: e.affine_select(out=mask[:], in_=mask[:], pattern=[[1, 128]],
                                            compare_op=ALU.is_ge, fill=0.0, base=0,
                                            channel_multiplier=-1), reads=[t_const], writes=[t_const])
        E("dve", lambda e: e.memset(junk[:], 0.0), writes=[t_junk])
        for l in range(nlayers):
            fw.dma("sp", pv[:, l, :], pv_d[l], writes=[t_pv])
            fw.dma("sp", gb[:, l, :], bc_d[l, BC_GB:BC_GB + 32].partition_broadcast(128), writes=[t_gb])
        fw.dma("sp", cT[:], c_d[:, :], writes=[t_cT])

        slot_ctr = [0]

        def load_slab(src2d, nkc, ncols):
            i = slot_ctr[0] % NSLOT
            slot_ctr[0] += 1
            fw.dma("pool", wsl[:, i, 0:nkc, 0:ncols],
                   src2d.rearrange("(kc p) n -> p kc n", p=128), writes=[t_wsl[i]])
            return i

        class Stream:
            def __init__(self):
                self.items = []
                self.issued = 0
                self.taken = 0
                self.released = 0
                self.slots = []

            def add(self, src2d, nkc, ncols):
                self.items.append((src2d, nkc, ncols))

            def pump(self):
                while self.issued < len(self.items) and self.issued < self.released + NSLOT:
                    self.slots.append(load_slab(*self.items[self.issued]))
                    self.issued += 1

            def take(self):
                self.pump()
                assert self.issued > self.taken
                i = self.slots[self.taken]
                self.taken += 1
                return i

            def release(self, n=1):
                self.released += n
                self.pump()

        stream = Stream()

        def setup_seq(s):
            E("act", lambda e: e.activation(out=condb[:, :, 0], in_=cT[:, s * KC:(s + 1) * KC], func=AF.Silu),
              reads=[t_cT], writes=[t_cond])
            E("act", lambda e: e.activation(out=condb[:, :, 1], in_=cT[:, s * KC:(s + 1) * KC], func=AF.Silu),
              reads=[t_cT], writes=[t_cond])
            E("dve", lambda e: e.tensor_copy(out=condbc[:], in_=condb[:, :, 0:1].broadcast_to([128, KC, 128])),
              reads=[t_cond], writes=[t_cond])
            for l in range(nlayers):
                for vi, c0 in enumerate((0, 1024, 3072, 4096)):
                    for half in range(2):
                        si = stream.take()
                        for j in range(4):
                            cc = half * 4 + j
                            col = (vi * KC + cc) * 2
                            for kc in range(KC):
                                mm(P[4][:, col:col + 2], wsl[:, si, kc, j * 128:(j + 1) * 128],
                                   condb[:, kc, :], kc == 0, kc == KC - 1,
                                   [t_wsl[si], t_cond], [t_P[4]])
                        stream.release()
                pview = P[4][:, 0:64].rearrange("p (v c two) -> p v c two", v=4, two=2)[:, :, :, 0]
                E("dve", lambda e: e.tensor_tensor(
                    out=modv[:, l, :, :], in0=pview,
                    in1=pv[:, l, _off["adab"]:_off["adab"] + 32].rearrange("p (v c) -> p v c", v=4),
                    op=ALU.add), reads=[t_P[4], t_pv], writes=[t_modv])
                for vi, pre in ((1, "pre1"), (3, "pre2")):
                    E("dve", lambda e: e.scalar_tensor_tensor(
                        out=modv[:, l, vi, :], in0=modv[:, l, vi, :], scalar=1.0,
                        in1=pv[:, l, _off[pre]:_off[pre] + 8], op0=ALU.add, op1=ALU.mult),
                      reads=[t_modv, t_pv], writes=[t_modv])
                for gi, (c0, bco, bcp) in enumerate(((2048, BC_ADAG1, BC_POST1), (5120, BC_ADAG2, BC_POST2))):
                    for half in range(2):
                        si = stream.take()
                        pb = P[5 + half]
                        for kc in range(KC):
                            mm(pb[:, :], condbc[:, kc, :], wsl[:, si, kc, :], kc == 0, kc == KC - 1,
                               [t_wsl[si], t_cond], [t_P[5 + half]])
                        stream.release()
                        fw.dma("sp", tmpf[:, half * 2, :],
                               bc_d[l, bco + half * 512: bco + (half + 1) * 512].partition_broadcast(128),
                               writes=[t_tmpf[half * 2]])
                        fw.dma("sp", tmpf[:, half * 2 + 1, :],
                               bc_d[l, bcp + half * 512: bcp + (half + 1) * 512].partition_broadcast(128),
                               writes=[t_tmpf[half * 2 + 1]])
                        E("dve", lambda e: e.tensor_tensor(out=gp[:, l, gi, half * 512:(half + 1) * 512],
                                                           in0=pb[:, :], in1=tmpf[:, half * 2, :], op=ALU.add),
                          reads=[t_P[5 + half], t_tmpf[half * 2]], writes=[t_gp[l][gi]])
                        E("dve", lambda e: e.tensor_tensor(out=gp[:, l, gi, half * 512:(half + 1) * 512],
                                                           in0=gp[:, l, gi, half * 512:(half + 1) * 512],
                                                           in1=tmpf[:, half * 2 + 1, :], op=ALU.mult),
                          reads=[t_gp[l][gi], t_tmpf[half * 2 + 1]], writes=[t_gp[l][gi]])
            for l in range(nlayers):
                E("dve", lambda e: e.memset(Tm[:, l], 0.0), writes=t_Tm[l])
                E("dve", lambda e: e.memset(Cb[:, l], 0.0), writes=t_Cb[l])
                E("dve", lambda e: e.memset(nm[:, l], 0.0), writes=[t_nm[l]])
                E("dve", lambda e: e.memset(nbw[:, l], 0.0), writes=[t_nb[l]])
                E("dve", lambda e: e.memset(elp[:, l], 1.0), writes=[t_elp[l]])
                E("dve", lambda e: e.memset(hq[:, l], 0.0), writes=[t_hq[l]])
                E("dve", lambda e: e.memset(hg[:, l], 0.0), writes=[t_hg[l]])
                E("dve", lambda e: e.memset(hf[:, l], 0.0), writes=[t_hf[l]])

        def queue_setup_weights(s):
            for l in range(nlayers):
                for c0 in (0, 1024, 3072, 4096, 2048, 5120):
                    for half in range(2):
                        stream.add(ada_d[l, :, c0 + half * 512: c0 + (half + 1) * 512], KC, 512)

        def queue_layer_weights(l):
            stream.add(win_d[l, :, 4096:4104], KC, 8)
            for j in range(4):
                stream.add(win_d[l, :, j * 512:(j + 1) * 512], KC, 512)
            for j in range(2):
                stream.add(win_d[l, :, 4104 + j * 512: 4104 + (j + 1) * 512], KC, 512)
                stream.add(win_d[l, :, 5128 + j * 512: 5128 + (j + 1) * 512], KC, 512)
            for j in range(4):
                stream.add(win_d[l, :, 2048 + j * 512: 2048 + (j + 1) * 512], KC, 512)
            for nh in range(2):
                for kg in range(2):
                    stream.add(wout_d[l, kg * 1024:(kg + 1) * 1024, nh * 512:(nh + 1) * 512], KC, 512)
            for j in range(6):
                w_ = 512 if j < 5 else 256
                stream.add(fup_d[l, :, j * 512: j * 512 + w_], KC, w_)
                stream.add(fup_d[l, :, DFF + j * 512: DFF + j * 512 + w_], KC, w_)
            for nh in range(2):
                for kg in range(3):
                    nk = 8 if kg < 2 else 6
                    stream.add(fdn_d[l, kg * 1024: kg * 1024 + nk * 128, nh * 512:(nh + 1) * 512], nk, 512)

        cin_ctr = [0]
        dg_ctr = [0]
        tmp_ctr = [0]

        def next_tmp():
            i = tmp_ctr[0] % 4
            tmp_ctr[0] += 1
            return i

        def prenorm(l, which):
            vs, vg = (0, 1) if which == 0 else (2, 3)
            for tt in range(TT):
                E("act", lambda e: e.activation(out=junk[:], in_=xb[:, tt, :], func=AF.Square,
                                                accum_out=sm[:, tt:tt + 1]),
                  reads=[t_xb[tt]], writes=[t_junk, t_sm])
            chk(1.2)
            E("act", lambda e: e.activation(out=sm[:, 4:8], in_=sm[:, 0:4], func=AF.Sqrt, scale=1.0 / D, bias=EPS),
              reads=[t_sm], writes=[t_sm])
            E("dve", lambda e: e.reciprocal(out=sm[:, 8:12], in_=sm[:, 4:8]), reads=[t_sm], writes=[t_sm])
            chk(1.4)
            for tt in range(TT):
                b = tt % 2
                E("dve", lambda e: e.tensor_scalar(out=xn[:, b, :], in0=xb[:, tt, :], scalar1=sm[:, 8 + tt:9 + tt],
                                                   scalar2=None, op0=ALU.mult),
                  reads=[t_xb[tt], t_sm], writes=[t_xn[b]])
                chk(1.6)
                pi = (tt % 2) * 2
                for kc in range(KC):
                    pq = pi + kc // 4
                    mm(P[pq][:, (kc % 4) * 128:(kc % 4 + 1) * 128], xn[:, b, kc * 128:(kc + 1) * 128], ident[:],
                       True, True, [t_xn[b], t_const], [t_P[pq]])
                chk(1.8)
                for kc in range(KC):
                    pq = pi + kc // 4
                    src = P[pq][:, (kc % 4) * 128:(kc % 4 + 1) * 128]
                    import os as _os
                    _sel = {"dve": True, "act": False}.get(_os.environ.get("EVAC", ""), kc >= 4)
                    E("dve" if _sel else "act", (lambda e: e.tensor_scalar(
                        out=u[:, kc, tt * 128:(tt + 1) * 128], in0=src,
                        scalar1=modv[:, l, vg, kc:kc + 1], scalar2=modv[:, l, vs, kc:kc + 1],
                        op0=ALU.mult, op1=ALU.add)) if _sel else (lambda e: e.activation(
                            out=u[:, kc, tt * 128:(tt + 1) * 128], in_=src,
                            func=AF.Identity, scale=modv[:, l, vg, kc:kc + 1], bias=modv[:, l, vs, kc:kc + 1])),
                      reads=[t_P[pq], t_modv], writes=[t_u[kc]])

        def build_diag(l, woff, ntap, chunk, tap0, ntp):
            i = dg_ctr[0] % NDG
            dg_ctr[0] += 1
            c0 = woff + chunk * ntap + tap0
            E("pool", lambda e: e.tensor_tensor(
                out=dg[:, i, 0:ntp, :], in0=identf[:].unsqueeze(1).broadcast_to([128, ntp, 128]),
                in1=pv[:, l, c0:c0 + ntp].unsqueeze(2).broadcast_to([128, ntp, 128]), op=ALU.mult),
              reads=[t_const, t_pv], writes=[t_dg[i]])
            return i

        def conv_pe(l, pout, t_pout, ci, j, woff, ntap, chunk):
            done = 0
            while done < ntap:
                ntp = min(DGT, ntap - done)
                di = build_diag(l, woff, ntap, chunk, done, ntp)
                for k in range(ntp):
                    kk = done + k
                    mm(pout[:, :], dg[:, di, k, :], cin[:, ci, j, kk:kk + T], kk == 0, kk == ntap - 1,
                       [t_dg[di], t_cin[ci]], [t_pout])
                done += ntp

        def post_sq(pi):
            E("act", lambda e: e.activation(out=junk[:, 0:512], in_=P[pi][:, :], func=AF.Square,
                                            accum_out=sm2[:, pi:pi + 1]), reads=[t_P[pi]], writes=[t_junk, t_sm2])

        def post_norm_all(l, gi):
            E("dve", lambda e: e.tensor_tensor(out=sm[:, 168:172], in0=sm2[:, 0:4], in1=sm2[:, 4:8], op=ALU.add),
              reads=[t_sm2], writes=[t_sm])
            E("act", lambda e: e.activation(out=sm[:, 172:176], in_=sm[:, 168:172], func=AF.Sqrt, scale=1.0 / D, bias=EPS),
              reads=[t_sm], writes=[t_sm])
            E("dve", lambda e: e.reciprocal(out=sm[:, 176:180], in_=sm[:, 172:176]), reads=[t_sm], writes=[t_sm])
            for tt in range(TT):
                for half in range(2):
                    pp, tp = P[half * 4 + tt], t_P[half * 4 + tt]
                    ti = next_tmp()
                    E("dve", lambda e: e.scalar_tensor_tensor(
                        out=tmpf[:, ti, :], in0=pp[:, :], scalar=sm[:, 176 + tt:177 + tt],
                        in1=gp[:, l, gi, half * 512:(half + 1) * 512], op0=ALU.mult, op1=ALU.mult),
                      reads=[tp, t_sm, t_gp[l][gi]], writes=[t_tmpf[ti]])
                    E("dve", lambda e: e.tensor_tensor(out=xb[:, tt, half * 512:(half + 1) * 512],
                                                        in0=xb[:, tt, half * 512:(half + 1) * 512],
                                                        in1=tmpf[:, ti, :], op=ALU.add),
                      reads=[t_xb[tt], t_tmpf[ti]], writes=[t_xb[tt]])

        def chk(ph):
            if ph >= stop:
                raise _Stop()

        def block_layer(l):
            chk(1)
            prenorm(l, 0)
            chk(2)
            si = stream.take()
            for tt in range(TT):
                for kc in range(KC):
                    mm(P[4][:, tt * 8:(tt + 1) * 8], u[:, kc, tt * 128:(tt + 1) * 128], wsl[:, si, kc, 0:8],
                       kc == 0, kc == KC - 1, [t_u[kc], t_wsl[si]], [t_P[4]])
            stream.release()
            G = sm[:, 32:64]
            E("dve", lambda e: e.tensor_tensor(out=G, in0=P[4][:, 0:32], in1=gb[:, l, :], op=ALU.add),
              reads=[t_P[4], t_gb], writes=[t_sm])
            Gv = G.rearrange("p (t g) -> p t g", g=8)
            LF = sm[:, 64:80].rearrange("p (t h) -> p t h", h=4)
            E("act", lambda e: e.activation(out=LF, in_=Gv[:, :, 4:8], func=AF.Exp, scale=-1.0),
              reads=[t_sm], writes=[t_sm])
            E("act", lambda e: e.activation(out=LF, in_=LF, func=AF.Ln, bias=1.0, scale=1.0),
              reads=[t_sm], writes=[t_sm])
            mm(P[5][:, 0:16], mask[:], sm[:, 64:80], True, True, [t_const, t_sm], [t_P[5]])
            mm(P[5][:, 16:32], onesf[:], sm[:, 64:80], True, True, [t_const, t_sm], [t_P[5]])
            A = sm[:, 80:96]
            Ee = sm[:, 96:112]
            EL = sm[:, 112:128]
            E("dve", lambda e: e.tensor_tensor(out=A.rearrange("p (t h) -> p t h", h=4), in0=Gv[:, :, 0:4],
                                               in1=P[5][:, 0:16].rearrange("p (t h) -> p t h", h=4), op=ALU.add),
              reads=[t_sm, t_P[5]], writes=[t_sm])
            E("act", lambda e: e.activation(out=A, in_=A, func=AF.Exp, bias=sm[:, 128:129], scale=1.0),
              reads=[t_sm], writes=[t_sm])
            E("act", lambda e: e.activation(out=sm[:, 96:128], in_=P[5][:, 0:32], func=AF.Exp, scale=-1.0),
              reads=[t_P[5]], writes=[t_sm])
            E("dve", lambda e: e.tensor_copy(out=vp[:, :, :, DH:DH + 2],
                                             in_=A.rearrange("p (t h o) -> p t h o", h=4, o=1).broadcast_to([128, TT, H, 2])),
              reads=[t_sm], writes=t_vp)

            chk(3)
            pend = None
            for sj in range(4):
                si = stream.take()
                ci = cin_ctr[0] % NCIN
                cin_ctr[0] += 1
                E("dve", lambda e: e.tensor_copy(out=cin[:, ci, :, 0:3], in_=hq[:, l, sj * 4:(sj + 1) * 4, :]),
                  reads=[t_hq[l]], writes=[t_cin[ci]])
                for j in range(4):
                    c = sj * 4 + j
                    pa = c % 2
                    for kc in range(KC):
                        mm(P[pa][:, :], wsl[:, si, kc, j * 128:(j + 1) * 128], u[:, kc, :], kc == 0, kc == KC - 1,
                           [t_wsl[si], t_u[kc]], [t_P[pa]])
                    E("act", lambda e: e.activation(out=cin[:, ci, j, 3:3 + T], in_=P[pa][:, :], func=AF.Copy),
                      reads=[t_P[pa]], writes=[t_cin[ci]])
                stream.release()
                E("dve", lambda e: e.tensor_copy(out=hq[:, l, sj * 4:(sj + 1) * 4, :], in_=cin[:, ci, :, T:T + 3]),
                  reads=[t_cin[ci]], writes=[t_hq[l]])
                for j in range(4):
                    c = sj * 4 + j
                    pc = 2 + c % 2
                    conv_pe(l, P[pc], t_P[pc], ci, j, _off["qkw"], 4, c)
                    E("act", lambda e: e.activation(out=R1[:, c, :], in_=P[pc][:, :], func=AF.Silu,
                                                    bias=pv[:, l, _off["qkb"] + c:_off["qkb"] + c + 1], scale=1.0),
                      reads=[t_P[pc], t_pv], writes=[t_R1[c]])

            chk(4)
            cf = {}

            def cf_proj(c):
                j = c % 4
                if j == 0:
                    cf["sa"] = stream.take()
                    cf["sg"] = stream.take()
                sa, sg = cf["sa"], cf["sg"]
                ci = cin_ctr[0] % NCIN
                cin_ctr[0] += 1
                cf[("ci", c)] = ci
                pa_, pg_ = (0, 1) if c % 2 == 0 else (4, 5)
                E("dve", lambda e: e.tensor_copy(out=cin[:, ci, 0, 0:30], in_=hg[:, l, c, :]),
                  reads=[t_hg[l]], writes=[t_cin[ci]])
                for kc in range(KC):
                    mm(P[pa_][:, :], wsl[:, sa, kc, j * 128:(j + 1) * 128], u[:, kc, :], kc == 0, kc == KC - 1,
                       [t_wsl[sa], t_u[kc]], [t_P[pa_]])
                for kc in range(KC):
                    mm(P[pg_][:, :], wsl[:, sg, kc, j * 128:(j + 1) * 128], u[:, kc, :], kc == 0, kc == KC - 1,
                       [t_wsl[sg], t_u[kc]], [t_P[pg_]])
                if j == 3:
                    stream.release(2)
                ti = next_tmp()
                E("act", lambda e: e.activation(out=tmpf[:, ti, :], in_=P[pg_][:, :], func=AF.Sigmoid),
                  reads=[t_P[pg_]], writes=[t_tmpf[ti]])
                E("dve", lambda e: e.tensor_tensor(out=cin[:, ci, 0, 30:30 + T], in0=P[pa_][:, :],
                                                   in1=tmpf[:, ti, :], op=ALU.mult),
                  reads=[t_P[pa_], t_tmpf[ti]], writes=[t_cin[ci]])
                E("dve", lambda e: e.tensor_copy(out=hg[:, l, c, :], in_=cin[:, ci, 0, T:T + 30]),
                  reads=[t_cin[ci]], writes=[t_hg[l]])

            def cf_conv(c):
                ci = cf[("ci", c)]
                pc = 2 + c % 2
                conv_pe(l, P[pc], t_P[pc], ci, 0, _off["cvw"], 31, c)
                bias_ = pv[:, l, _off["cvb"] + c:_off["cvb"] + c + 1]
                E("act", lambda e: e.activation(out=my[:, c, :], in_=P[pc][:, :], func=AF.Identity, bias=bias_, scale=1.0),
                  reads=[t_P[pc], t_pv], writes=[t_my[c]])
                t2 = next_tmp()
                cf[("t2", c)] = t2
                ysq = tmpf[:, t2, :].bitcast(BF16)[:, 0:T]
                E("act", lambda e: e.activation(out=ysq, in_=P[pc][:, :], func=AF.Square, bias=bias_, scale=1.0),
                  reads=[t_P[pc], t_pv], writes=[t_tmpf[t2]])

            def cf_stats(c):
                t2 = cf[("t2", c)]
                ysq = tmpf[:, t2, :].bitcast(BF16)[:, 0:T]
                mm(P[6][:, :], onesln[:], my[:, c, :], c == 0, c == KC - 1, [t_const, t_my[c]], [t_P[6]])
                mm(P[7][:, :], onesln[:], ysq, c == 0, c == KC - 1, [t_const, t_tmpf[t2]], [t_P[7]])

            cf_proj(0)
            for c in range(KC):
                if c + 1 < KC:
                    cf_proj(c + 1)
                cf_conv(c)
                if c >= 1:
                    cf_stats(c - 1)
            cf_stats(KC - 1)
            pass
            E("act", lambda e: e.activation(out=stat[:, 0, :], in_=P[6][:, :], func=AF.Copy),
              reads=[t_P[6]], writes=[t_stat])
            ti = next_tmp()
            E("act", lambda e: e.activation(out=tmpf[:, ti, :], in_=P[6][:, :], func=AF.Square),
              reads=[t_P[6]], writes=[t_tmpf[ti]])
            E("dve", lambda e: e.tensor_tensor(out=tmpf[:, ti, :], in0=P[7][:, :], in1=tmpf[:, ti, :], op=ALU.subtract),
              reads=[t_P[7], t_tmpf[ti]], writes=[t_tmpf[ti]])
            E("act", lambda e: e.activation(out=tmpf[:, ti, :], in_=tmpf[:, ti, :], func=AF.Sqrt, bias=EPS, scale=1.0),
              reads=[t_tmpf[ti]], writes=[t_tmpf[ti]])
            E("dve", lambda e: e.reciprocal(out=stat[:, 1, :], in_=tmpf[:, ti, :]),
              reads=[t_tmpf[ti]], writes=[t_stat])
            for c in range(KC):
                ti = next_tmp()
                E("dve", lambda e: e.tensor_tensor(out=tmpf[:, ti, :], in0=my[:, c, :], in1=stat[:, 0, :], op=ALU.subtract),
                  reads=[t_my[c], t_stat], writes=[t_tmpf[ti]])
                E("dve", lambda e: e.tensor_tensor(out=tmpf[:, ti, :], in0=tmpf[:, ti, :], in1=stat[:, 1, :], op=ALU.mult),
                  reads=[t_tmpf[ti], t_stat], writes=[t_tmpf[ti]])
                E("act", lambda e: e.activation(out=my[:, c, :], in_=tmpf[:, ti, :], func=AF.Silu,
                                                scale=pv[:, l, _off["lng"] + c:_off["lng"] + c + 1],
                                                bias=pv[:, l, _off["lnb"] + c:_off["lnb"] + c + 1]),
                  reads=[t_tmpf[ti], t_pv], writes=[t_my[c]])

            chk(5)
            for sj in range(4):
                si = stream.take()
                for tt in range(TT):
                    pa = (0, 1, 4, 5)[(sj * TT + tt) % 4]
                    for kc in range(KC):
                        mm(P[pa][:, :], u[:, kc, tt * 128:(tt + 1) * 128], wsl[:, si, kc, :], kc == 0, kc == KC - 1,
                           [t_u[kc], t_wsl[si]], [t_P[pa]])
                    if sj < 2:
                        for hh in range(2):
                            h = sj * 2 + hh
                            E("dve", (lambda e: e.activation(
                                out=vp[:, tt, h, 0:DH], in_=P[pa][:, hh * DH:(hh + 1) * DH], func=AF.Identity,
                                scale=sm[:, 80 + tt * 4 + h:81 + tt * 4 + h])) if False else (lambda e: e.tensor_scalar(
                                    out=vp[:, tt, h, 0:DH], in0=P[pa][:, hh * DH:(hh + 1) * DH],
                                    scalar1=sm[:, 80 + tt * 4 + h:81 + tt * 4 + h], scalar2=None, op0=ALU.mult)),
                              reads=[t_P[pa], t_sm], writes=[t_vp[tt]])
                    else:
                        E("act", lambda e: e.activation(out=go[:, tt, (sj - 2) * 512:(sj - 1) * 512], in_=P[pa][:, :],
                                                        func=AF.Sigmoid), reads=[t_P[pa]], writes=[t_go[tt]])
                stream.release()

            chk(6)
            ml = {}

            def ktf(tt, h, dc):
                return R1[:, 16 + 2 * tt + h // 2, (h % 2) * 256 + dc * 128:(h % 2) * 256 + (dc + 1) * 128]

            def st_T(tt):
                ts = slice(tt * 128, (tt + 1) * 128)
                for c in range(KC):
                    mm(P[c // 4][:, (c % 4) * 128:(c % 4 + 1) * 128], R1[:, 8 + c, ts], ident[:], True, True,
                       [t_R1[8 + c], t_const], [t_P[c // 4]])
                for a_ in range(2):
                    E("act" if a_ else "dve", (lambda e: e.activation(out=R1[:, 16 + 2 * tt + a_, :], in_=P[a_][:, :], func=AF.Copy))
                      if a_ else (lambda e: e.tensor_copy(out=R1[:, 16 + 2 * tt + a_, :], in_=P[a_][:, :])),
                      reads=[t_P[a_]], writes=[t_R1[16 + 2 * tt + a_]])
                for h in range(H):
                    for dc in range(2):
                        mm(P[2][:, h * 128:(h + 1) * 128], R1[:, 8 + 2 * h + dc, ts], R1[:, 2 * h + dc, ts],
                           dc == 0, dc == 1, [t_R1[8 + 2 * h + dc], t_R1[2 * h + dc]], [t_P[2]])
                Sts = []
                for h in range(H):
                    ti = next_tmp()
                    St = tmpf[:, ti, :].bitcast(BF16)[:, 0:128]
                    Sts.append((St, ti))
                    E("dve", lambda e: e.tensor_tensor(out=St, in0=P[2][:, h * 128:(h + 1) * 128], in1=mask[:], op=ALU.mult),
                      reads=[t_P[2], t_const], writes=[t_tmpf[ti]])
                ml[("Sts", tt)] = Sts

            def st_N(tt):
                ts = slice(tt * 128, (tt + 1) * 128)
                Sts = ml[("Sts", tt)]
                for h in range(H):
                    St, ti = Sts[h]
                    pn = P[4 + h // 2]
                    tpn = t_P[4 + h // 2]
                    no = (h % 2) * DH
                    for dc in range(2):
                        mm(pn[:, no:no + DH], R1[:, 2 * h + dc, ts], Cb[:, l, h, dc * DH:(dc + 1) * DH], dc == 0, False,
                           [t_R1[2 * h + dc], t_Cb[l][h]], [tpn])
                    mm(pn[:, no:no + DH], St, vp[:, tt, h, 0:DH], False, True, [t_tmpf[ti], t_vp[tt]], [tpn])
                for h in range(H):
                    St, ti = Sts[h]
                    for dc in range(2):
                        mm(P[3][:, h * 2:h * 2 + 2], R1[:, 2 * h + dc, ts], nbp(l, h, dc), dc == 0, False, [t_R1[2 * h + dc], t_nb[l]], [t_P[3]])
                    mm(P[3][:, h * 2:h * 2 + 2], St, vp[:, tt, h, DH:DH + 2], False, True, [t_tmpf[ti], t_vp[tt]], [t_P[3]])
                E("act", lambda e: e.activation(out=numsb[:, 0:512], in_=P[4][:, :], func=AF.Copy), reads=[t_P[4]], writes=[t_numsb])
                E("dve", lambda e: e.tensor_copy(out=numsb[:, 512:1024], in_=P[5][:, :]), reads=[t_P[5]], writes=[t_numsb])
                E("dve", lambda e: e.tensor_copy(out=sm[:, 184:192], in_=P[3][:, 0:8]), reads=[t_P[3]], writes=[t_sm])

            def st_U(tt):
                for h in range(H):
                    for dc in range(2):
                        mm(P[3][:, 16 + h * 4 + dc * 2:18 + h * 4 + dc * 2], ktf(tt, h, dc), vp[:, tt, h, DH:DH + 2], True, True,
                           [t_R1[16 + 2 * tt + h // 2], t_vp[tt]], [t_P[3]])
                elp4 = elp[:, l, :] if tt == 0 else sm[:, 112 + (tt - 1) * 4:112 + tt * 4]
                t_elprev = t_elp[l] if tt == 0 else t_sm
                el4 = sm[:, 112 + tt * 4:116 + tt * 4]
                for h in range(H):
                    idx = tt * 4 + h
                    pu = P[6 + h % 2]
                    tpu = t_P[6 + h % 2]
                    for dc in range(2):
                        mm(pu[:, dc * DH:(dc + 1) * DH], ktf(tt, h, dc), vp[:, tt, h, 0:DH], True, True,
                           [t_R1[16 + 2 * tt + h // 2], t_vp[tt]], [tpu])
                    E("dve", lambda e: e.scalar_tensor_tensor(out=Tm[:, l, h, :], in0=Tm[:, l, h, :], scalar=elp4[:, h:h + 1],
                                                              in1=pu[:, :], op0=ALU.mult, op1=ALU.add),
                      reads=[t_Tm[l][h], t_elprev, tpu], writes=[t_Tm[l][h]])
                    E("act", lambda e: e.activation(out=Cb[:, l, h, :], in_=Tm[:, l, h, :], func=AF.Identity,
                                                    scale=sm[:, 112 + idx:113 + idx]),
                      reads=[t_Tm[l][h], t_sm], writes=[t_Cb[l][h]])
                nsl = P[3][:, 16:32].rearrange("p (h d two) -> p h d two", h=4, two=2)[:, :, :, 0]
                E("dve", lambda e: e.tensor_tensor(out=nm[:, l], in0=nm[:, l], in1=elp4.unsqueeze(2).broadcast_to([128, H, 2]),
                                                   op=ALU.mult), reads=[t_nm[l], t_elprev], writes=[t_nm[l]])
                E("dve", lambda e: e.tensor_tensor(out=nm[:, l], in0=nm[:, l], in1=nsl, op=ALU.add),
                  reads=[t_nm[l], t_P[3]], writes=[t_nm[l]])
                E("dve", lambda e: e.tensor_tensor(out=nbw[:, l], in0=nm[:, l].unsqueeze(3).broadcast_to([128, H, 2, 2]),
                                                   in1=el4.unsqueeze(2).unsqueeze(3).broadcast_to([128, H, 2, 2]), op=ALU.mult),
                  reads=[t_nm[l], t_sm], writes=[t_nb[l]])

            def st_F1(tt):
                e4 = sm[:, 96 + tt * 4:100 + tt * 4]
                den = sm[:, 184:192].rearrange("p (h two) -> p h two", two=2)[:, :, 0]
                W0 = sm[:, 136:140]
                W1 = sm[:, 140:144]
                W2 = sm[:, 144:148]
                E("dve", lambda e: e.tensor_tensor(out=W0, in0=den, in1=e4, op=ALU.mult), reads=[t_sm], writes=[t_sm])
                E("act", lambda e: e.activation(out=W0, in_=W0, func=AF.Abs), reads=[t_sm], writes=[t_sm])
                E("dve", lambda e: e.tensor_scalar_max(out=W0, in0=W0, scalar1=1.0), reads=[t_sm], writes=[t_sm])
                E("dve", lambda e: e.reciprocal(out=W0, in_=W0), reads=[t_sm], writes=[t_sm])
                E("dve", lambda e: e.tensor_tensor(out=W0, in0=W0, in1=e4, op=ALU.mult), reads=[t_sm], writes=[t_sm])
                for h in range(H):
                    E("act", lambda e: e.activation(out=junk[:, 0:DH], in_=numsb[:, h * DH:(h + 1) * DH], func=AF.Square,
                                                    accum_out=sm[:, 148 + h:149 + h]),
                      reads=[t_numsb], writes=[t_junk, t_sm])
                SS = sm[:, 148:152]
                E("dve", lambda e: e.tensor_tensor(out=W1, in0=W0, in1=W0, op=ALU.mult), reads=[t_sm], writes=[t_sm])
                E("dve", lambda e: e.tensor_tensor(out=W1, in0=W1, in1=SS, op=ALU.mult), reads=[t_sm], writes=[t_sm])
                E("act", lambda e: e.activation(out=W1, in_=W1, func=AF.Sqrt, scale=1.0 / DH, bias=EPS), reads=[t_sm], writes=[t_sm])
                E("dve", lambda e: e.reciprocal(out=W1, in_=W1), reads=[t_sm], writes=[t_sm])
                E("dve", lambda e: e.tensor_tensor(out=W2, in0=W1, in1=W0, op=ALU.mult), reads=[t_sm], writes=[t_sm])
                hb = tt % 2
                for h in range(H):
                    E("dve", lambda e: e.scalar_tensor_tensor(out=hfin[:, hb, h * DH:(h + 1) * DH], in0=numsb[:, h * DH:(h + 1) * DH],
                                                              scalar=sm[:, 144 + h:145 + h], in1=go[:, tt, h * DH:(h + 1) * DH],
                                                              op0=ALU.mult, op1=ALU.mult),
                      reads=[t_numsb, t_sm, t_go[tt]], writes=[t_hfin[hb]])

            def st_F2(tt):
                ts = slice(tt * 128, (tt + 1) * 128)
                hb = tt % 2
                for c in range(KC):
                    mm(P[c // 4][:, (c % 4) * 128:(c % 4 + 1) * 128], hfin[:, hb, c * 128:(c + 1) * 128], ident[:], True, True,
                       [t_hfin[hb], t_const], [t_P[c // 4]])
                for c in range(KC):
                    src = P[c // 4][:, (c % 4) * 128:(c % 4 + 1) * 128]
                    E("dve" if c >= 4 else "act", (lambda e: e.tensor_scalar(
                        out=u[:, c, ts], in0=src,
                        scalar1=pv[:, l, _off["mlg"] + c:_off["mlg"] + c + 1], scalar2=None, op0=ALU.mult)) if c >= 4 else (
                        lambda e: e.activation(out=u[:, c, ts], in_=src, func=AF.Identity,
                                               scale=pv[:, l, _off["mlg"] + c:_off["mlg"] + c + 1])),
                      reads=[t_P[c // 4], t_pv], writes=[t_u[c]])

            st_T(0)
            for tt in range(TT):
                st_N(tt)
                st_U(tt)
                if tt + 1 < TT:
                    st_T(tt + 1)
                st_F1(tt)
                if tt >= 1:
                    st_F2(tt - 1)
            st_F2(TT - 1)
            E("dve", lambda e: e.tensor_copy(out=elp[:, l, :], in_=sm[:, 124:128]), reads=[t_sm], writes=[t_elp[l]])

            chk(7)
            for nh in range(2):
                sis = [stream.take(), stream.take()]
                for tt in range(TT):
                    pi = nh * 4 + tt
                    for kg in range(2):
                        si = sis[kg]
                        for kc in range(KC):
                            src = (u[:, kc, tt * 128:(tt + 1) * 128], t_u[kc]) if kg == 0 else \
                                (my[:, kc, tt * 128:(tt + 1) * 128], t_my[kc])
                            mm(P[pi][:, :], src[0], wsl[:, si, kc, :], kg == 0 and kc == 0, kg == 1 and kc == KC - 1,
                               [src[1], t_wsl[si]], [t_P[pi]])
                    post_sq(pi)
                stream.release(2)
            post_norm_all(l, 0)

            chk(8)
            prenorm(l, 1)
            pend = []
            for sj in range(6):
                sa = stream.take()
                sg = stream.take()
                nch = 4 if sj < 5 else 2
                ca = cin_ctr[0] % NCIN
                cin_ctr[0] += 1
                cg = cin_ctr[0] % NCIN
                cin_ctr[0] += 1
                c0 = sj * 4
                E("dve", lambda e: e.tensor_copy(out=cin[:, ca, 0:nch, 0:2], in_=hf[:, l, c0:c0 + nch, :]),
                  reads=[t_hf[l]], writes=[t_cin[ca]])
                E("dve", lambda e: e.tensor_copy(out=cin[:, cg, 0:nch, 0:2], in_=hf[:, l, NFC + c0:NFC + c0 + nch, :]),
                  reads=[t_hf[l]], writes=[t_cin[cg]])
                for j in range(nch):
                    for (sw, cb, pp) in ((sa, ca, 0 if j % 2 == 0 else 6), (sg, cg, 1 if j % 2 == 0 else 7)):
                        for kc in range(KC):
                            mm(P[pp][:, :], wsl[:, sw, kc, j * 128:(j + 1) * 128], u[:, kc, :], kc == 0, kc == KC - 1,
                               [t_wsl[sw], t_u[kc]], [t_P[pp]])
                        E("act" if pp % 2 else "dve", (lambda e: e.activation(out=cin[:, cb, j, 2:2 + T], in_=P[pp][:, :], func=AF.Copy))
                          if pp % 2 else (lambda e: e.tensor_copy(out=cin[:, cb, j, 2:2 + T], in_=P[pp][:, :])),
                          reads=[t_P[pp]], writes=[t_cin[cb]])
                stream.release(2)
                E("dve", lambda e: e.tensor_copy(out=hf[:, l, c0:c0 + nch, :], in_=cin[:, ca, 0:nch, T:T + 2]),
                  reads=[t_cin[ca]], writes=[t_hf[l]])
                E("dve", lambda e: e.tensor_copy(out=hf[:, l, NFC + c0:NFC + c0 + nch, :], in_=cin[:, cg, 0:nch, T:T + 2]),
                  reads=[t_cin[cg]], writes=[t_hf[l]])
                for j in range(nch):
                    c = c0 + j
                    qa, qg = (2, 3) if c % 2 == 0 else (4, 5)
                    conv_pe(l, P[qa], t_P[qa], ca, j, _off["fw"], 3, c)
                    conv_pe(l, P[qg], t_P[qg], cg, j, _off["fw"], 3, NFC + c)
                    ti = next_tmp()
                    E("act", lambda e: e.activation(out=tmpf[:, ti, :], in_=P[qg][:, :], func=AF.Silu,
                                                    bias=pv[:, l, _off["fb"] + NFC + c:_off["fb"] + NFC + c + 1], scale=1.0),
                      reads=[t_P[qg], t_pv], writes=[t_tmpf[ti]])
                    E("dve", lambda e: e.scalar_tensor_tensor(out=R1[:, c, :], in0=P[qa][:, :],
                                                              scalar=pv[:, l, _off["fb"] + c:_off["fb"] + c + 1],
                                                              in1=tmpf[:, ti, :], op0=ALU.add, op1=ALU.mult),
                      reads=[t_P[qa], t_pv, t_tmpf[ti]], writes=[t_R1[c]])
            for nh in range(2):
                sis = [stream.take(), stream.take(), stream.take()]
                for tt in range(TT):
                    pi = nh * 4 + tt
                    for kg in range(3):
                        nk = 8 if kg < 2 else 6
                        si = sis[kg]
                        for kc in range(nk):
                            cidx = kg * 8 + kc
                            mm(P[pi][:, :], R1[:, cidx, tt * 128:(tt + 1) * 128], wsl[:, si, kc, :],
                               cidx == 0, cidx == NFC - 1, [t_R1[cidx], t_wsl[si]], [t_P[pi]])
                    post_sq(pi)
                stream.release(3)
            post_norm_all(l, 1)

        def nbp(l, h, dc):
            return nbw[:, l, h, dc, :]

        def nb2(l, h):
            return nbw[:, l, h, :, :]

        E("dve", lambda e: e.memset(sm[:, 128:129], float(-np.log(16.0))), writes=[t_sm])

        for s in range(nseq):
            queue_setup_weights(s)
            for b in range(nblk):
                for l in range(nlayers):
                    queue_layer_weights(l)
        for s in range(nseq):
            setup_seq(s)
            for b in range(nblk):
                for tt in range(TT):
                    fw.dma("sp", xb[:, tt, :], x_d[s, b * T + tt * 128: b * T + (tt + 1) * 128, :], writes=[t_xb[tt]])
                try:
                    for l in range(nlayers):
                        block_layer(l)
                except _Stop:
                    pass
                for tt in range(TT):
                    out_toks.append(fw.dma("sp", y_d[s, b * T + tt * 128: b * T + (tt + 1) * 128, :], xb[:, tt, :],
                                           reads=[t_xb[tt]]))
        fw.finish(out_toks)
        print("instructions:", fw.ninst, fw.per, "sems:", fw.nsems, "slabs:", len(stream.items))
    return nc


def _fm(v):
    return np.ascontiguousarray(v.reshape(-1, 128).T)


def _fmw(w):
    K, C = w.shape
    return np.ascontiguousarray(w.T.reshape(C // 128, 128, K).transpose(1, 0, 2).reshape(128, -1))


def pack_params(inp, layers):
    pvs, bcs = [], []
    for l in layers:
        ab = inp["ada_b"][l]
        cols = [
            _fm(inp["mix_pre_g"][l]), _fmw(inp["qk_conv_w"][l]), _fm(inp["qk_conv_b"][l]),
            _fmw(inp["cv_dw_w"][l]), _fm(inp["cv_dw_b"][l]), _fm(inp["cv_ln_g"][l]), _fm(inp["cv_ln_b"][l]),
            _fm(inp["ffn_pre_g"][l]), _fmw(inp["ffn_conv_w"][l]), _fm(inp["ffn_conv_b"][l]),
            _fm(inp["ml_norm_g"][l]),
            _fm(ab[0:1024]), _fm(ab[1024:2048]), _fm(ab[3072:4096]), _fm(ab[4096:5120]),
        ]
        pvs.append(np.concatenate(cols, axis=1))
        gbias = np.concatenate([inp["igate_b"][l], inp["fgate_b"][l]])
        bcs.append(np.concatenate([inp["mix_post_g"][l], inp["ffn_post_g"][l], ab[2048:3072], ab[5120:6144],
                                   np.tile(gbias, 4)]))
    pvec = np.ascontiguousarray(np.stack(pvs)).astype(np.float32)
    bcv = np.ascontiguousarray(np.stack(bcs)).astype(np.float32)
    assert pvec.shape[2] == NPV and bcv.shape[1] == NBC
    return pvec, bcv


def make_in_maps(inp, ncores, nseq, nblk, layers):
    pvec, bcv = pack_params(inp, layers)
    S = nblk * T
    L = list(layers)
    shared = {
        "ada_w": np.ascontiguousarray(inp["ada_w"][L]), "w_in": np.ascontiguousarray(inp["w_in"][L]),
        "w_out": np.ascontiguousarray(inp["w_out"][L]), "ffn_up": np.ascontiguousarray(inp["ffn_up"][L]),
        "ffn_down": np.ascontiguousarray(inp["ffn_down"][L]), "pvec": pvec, "bcv": bcv,
    }
    maps = []
    for c in range(ncores):
        xs = np.ascontiguousarray(inp["x"][c * nseq:(c + 1) * nseq, :S, :])
        cc = inp["c"][c * nseq:(c + 1) * nseq]
        cT = np.ascontiguousarray(cc.reshape(nseq, KC, 128).transpose(2, 0, 1).reshape(128, nseq * KC))
        m = dict(shared)
        m["x"] = xs
        m["cT"] = cT
        maps.append(m)
    return maps


_NC_CACHE = {}


def kernel(**inputs):
    inp = {k: np.asarray(v, dtype=np.float32) for k, v in inputs.items()}
    ncores, nseq, nblk = 8, 2, 4
    key = (nseq, nblk, 2)
    if key not in _NC_CACHE:
        _NC_CACHE[key] = build(nseq, nblk, 2)
    nc = _NC_CACHE[key]
    maps = make_in_maps(inp, ncores, nseq, nblk, (0, 1))
    res = run_bass_kernel_spmd(nc, maps, core_ids=list(range(ncores)))
    out = np.concatenate([np.asarray(r["y"]).reshape(nseq, SEQ, D) for r in res.results], axis=0)
    return out.astype(np.float32)
```
